# Optimizing a Trainium2 kernel written in Bass

```python
import jax, jax.numpy as jnp
from jax import lax
import numpy as np

D_MODEL = 1024
BATCH = 8
SEQ = 2048
DEPTH = 1
DEC_BATCH = 128
DEC_SEQ = 4
PAST_LEN = 16384
PAGE_SIZE = 128

HGRN_HEADS = 8
HGRN_DK = 128
HGRN_DV = 128
HGRN_FDIM = HGRN_HEADS * HGRN_DK
HGRN_WIDTH = HGRN_HEADS * HGRN_DV
MLSTM_HEADS = 4
MLSTM_DH = 256
MLSTM_WIDTH = MLSTM_HEADS * MLSTM_DH
D_MIX = HGRN_WIDTH + MLSTM_WIDTH
CONV_WIDTH = 4
QKV_BLOCK = 4
CHUNK = 64
EPS = 1e-6
IN_SIZES = (HGRN_FDIM, HGRN_FDIM, HGRN_WIDTH, HGRN_WIDTH, MLSTM_WIDTH, MLSTM_WIDTH, MLSTM_WIDTH, MLSTM_HEADS, MLSTM_HEADS)
W_IN_COLS = sum(IN_SIZES)
SPLIT_IDX = tuple(int(s) for s in np.cumsum(IN_SIZES)[:-1])
FGATE_OFFSET = W_IN_COLS - MLSTM_HEADS

kernel_name = "hymba_hgrn2_mlstm_adaln_step"

F32 = jnp.float32


def _rms(x, g):
    xf = x.astype(F32)
    y = xf * lax.rsqrt(jnp.mean(xf * xf, -1, keepdims=True) + EPS)
    return (y * g.astype(F32)).astype(x.dtype)


def _head_rms(h, g):
    B, T, H, D = h.shape
    y = h * lax.rsqrt(jnp.mean(h * h, -1, keepdims=True) + EPS)
    return y.reshape(B, T, H * D) * g.astype(F32)


def _to_chunks(a, chunk):
    B, T = a.shape[:2]
    return a.reshape((B, T // chunk, chunk) + a.shape[2:]).swapaxes(0, 1)


def _from_chunks(a):
    n, B, L = a.shape[:3]
    return a.swapaxes(0, 1).reshape((B, n * L) + a.shape[3:])


def _hgrn2(q, logf, v, S0, chunk):
    k = -jnp.expm1(logf)
    L = chunk
    causal = jnp.tril(jnp.ones((L, L), bool))

    def step(S, inp):
        qc, lfc, kc, vc = inp
        G = jnp.cumsum(lfc, axis=1)
        diff = G[:, :, None] - G[:, None, :]
        decay = jnp.where(causal[None, :, :, None, None], jnp.exp(jnp.minimum(diff, 0.0)), 0.0)
        A = jnp.einsum('bthk,btjhk,bjhk->bhtj', qc, decay, kc)
        o_intra = jnp.einsum('bhtj,bjhv->bthv', A, vc)
        o_inter = jnp.einsum('bthk,bhkv->bthv', qc * jnp.exp(G), S)
        GL = G[:, -1]
        kdec = kc * jnp.exp(GL[:, None] - G)
        S_new = jnp.exp(GL)[..., None] * S + jnp.einsum('bjhk,bjhv->bhkv', kdec, vc)
        return S_new, o_intra + o_inter

    S_fin, o = lax.scan(step, S0, (_to_chunks(q, L), _to_chunks(logf, L), _to_chunks(k, L), _to_chunks(v, L)))
    return _from_chunks(o), S_fin


def _mlstm(q, k, v, ig, logf, C0, n0, m0, chunk):
    L = chunk
    causal = jnp.tril(jnp.ones((L, L), bool))

    def step(carry, inp):
        C, nv, m = carry
        qc, kc, vc, ic, fc = inp
        b = jnp.cumsum(fc, axis=1)
        logD = b[:, :, None, :] - b[:, None, :, :] + ic[:, None, :, :]
        logD = jnp.where(causal[None, :, :, None], logD, -jnp.inf)
        m_inter = b + m[:, None, :]
        m_t = jnp.maximum(m_inter, jnp.max(logD, axis=2))
        Dm = jnp.exp(logD - m_t[:, :, None, :])
        inter = jnp.exp(m_inter - m_t)
        s = jnp.einsum('bthd,bjhd->btjh', qc, kc) * Dm
        num = inter[..., None] * jnp.einsum('bthk,bhkv->bthv', qc, C) + jnp.einsum('btjh,bjhv->bthv', s, vc)
        den = inter * jnp.einsum('bthk,bhk->bth', qc, nv) + jnp.sum(s, axis=2)
        h = num / jnp.maximum(jnp.abs(den), jnp.exp(-m_t))[..., None]
        m_new = m_t[:, -1]
        wC = jnp.exp(b[:, -1:] - b + ic - m_new[:, None])
        dec = jnp.exp(b[:, -1] + m - m_new)
        C_new = dec[..., None, None] * C + jnp.einsum('bjh,bjhk,bjhv->bhkv', wC, kc, vc)
        n_new = dec[..., None] * nv + jnp.einsum('bjh,bjhk->bhk', wC, kc)
        return (C_new, n_new, m_new), h

    (C, nv, m), h = lax.scan(step, (C0, n0, m0), (_to_chunks(q, L), _to_chunks(k, L), _to_chunks(v, L), _to_chunks(ig, L), _to_chunks(logf, L)))
    return _from_chunks(h), C, nv, m


def _blockdiag(x, w):
    B, T, W = x.shape
    xb = x.reshape(B, T, W // QKV_BLOCK, QKV_BLOCK)
    return jnp.einsum('btgi,gij->btgj', xb, w).reshape(B, T, W)


def _layer(x, c, S_h, C_m, n_m, m_m, buf, chunk, lb, w_ada, b_ada, norm_g, w_in, b_in, hgrn_norm_g,
           conv_w, conv_b, wq, wk, wv, mnorm_g, skip, w_out):
    B, T, _ = x.shape
    mod = jnp.dot(jax.nn.silu(c), w_ada) + b_ada
    shift, scale, gate = jnp.split(mod, 3, axis=-1)
    h = _rms(x, norm_g) * (1 + scale[:, None]) + shift[:, None]
    proj = jnp.dot(h, w_in) + b_in
    hq, hf, hi, hz, mu, mz, mo, mi, mf = jnp.split(proj, SPLIT_IDX, axis=-1)

    f = lb + (1.0 - lb) * jax.nn.sigmoid(hf.astype(F32))
    logf = jnp.log(f).reshape(B, T, HGRN_HEADS, HGRN_DK)
    qh = hq.astype(F32).reshape(B, T, HGRN_HEADS, HGRN_DK)
    vh = hi.astype(F32).reshape(B, T, HGRN_HEADS, HGRN_DV)
    o_h, S_new = _hgrn2(qh, logf, vh, S_h.astype(F32), chunk)
    y_h = _head_rms(o_h, hgrn_norm_g) * jax.nn.silu(hz.astype(F32))

    ext = jnp.concatenate([buf.astype(mu.dtype), mu], axis=1)
    new_buf = ext[:, -(CONV_WIDTH - 1):]
    conv = lax.conv_general_dilated(ext, conv_w[:, None, :].astype(mu.dtype), (1,), 'VALID',
                                    dimension_numbers=('NWC', 'WIO', 'NWC'),
                                    feature_group_count=MLSTM_WIDTH) + conv_b
    xc = jax.nn.silu(conv)
    q = _blockdiag(xc, wq).astype(F32).reshape(B, T, MLSTM_HEADS, MLSTM_DH)
    k = (_blockdiag(xc, wk).astype(F32) * (MLSTM_DH ** -0.5)).reshape(B, T, MLSTM_HEADS, MLSTM_DH)
    v = _blockdiag(mu, wv).astype(F32).reshape(B, T, MLSTM_HEADS, MLSTM_DH)
    ig = mi.astype(F32)
    lfm = jax.nn.log_sigmoid(mf.astype(F32))
    h_m, C_new, n_new, m_new = _mlstm(q, k, v, ig, lfm, C_m.astype(F32), n_m.astype(F32), m_m.astype(F32), chunk)
    h_m = jax.nn.sigmoid(mo.astype(F32)).reshape(B, T, MLSTM_HEADS, MLSTM_DH) * h_m
    y_m = (_head_rms(h_m, mnorm_g) + skip.astype(F32) * xc.astype(F32)) * jax.nn.silu(mz.astype(F32))

    mix = jnp.concatenate([y_h, y_m], axis=-1).astype(x.dtype)
    out = x + gate[:, None] * jnp.dot(mix, w_out)
    return out, (S_new, C_new, n_new, m_new, new_buf)


def setup_inputs(seed: int = 0) -> dict:
    key = jax.random.key(seed)
    ks = jax.random.split(key, 32)
    nrm = jax.random.normal
    d = {}
    d['x_prompt'] = nrm(ks[0], (BATCH, SEQ, D_MODEL), F32)
    d['x_sample'] = nrm(ks[1], (DEC_BATCH, DEC_SEQ, D_MODEL), F32)
    d['c_prompt'] = nrm(ks[2], (BATCH, D_MODEL), F32)
    d['c_sample'] = nrm(ks[3], (DEC_BATCH, D_MODEL), F32)
    d['state_hgrn'] = 0.5 * nrm(ks[4], (DEPTH, DEC_BATCH, HGRN_HEADS, HGRN_DK, HGRN_DV), F32)
    d['state_mlstm_C'] = 0.1 * nrm(ks[5], (DEPTH, DEC_BATCH, MLSTM_HEADS, MLSTM_DH, MLSTM_DH), F32)
    d['state_mlstm_n'] = 0.5 * nrm(ks[6], (DEPTH, DEC_BATCH, MLSTM_HEADS, MLSTM_DH), F32)
    d['state_mlstm_m'] = 0.5 * nrm(ks[7], (DEPTH, DEC_BATCH, MLSTM_HEADS), F32)
    d['state_mlstm_conv'] = nrm(ks[8], (DEPTH, DEC_BATCH, CONV_WIDTH - 1, MLSTM_WIDTH), F32)
    d['w_ada'] = 0.5 * (D_MODEL ** -0.5) * nrm(ks[9], (DEPTH, D_MODEL, 3 * D_MODEL), F32)
    d['b_ada'] = 0.01 * nrm(ks[10], (DEPTH, 3 * D_MODEL), F32)
    d['norm_g'] = 1.0 + 0.01 * nrm(ks[11], (DEPTH, D_MODEL), F32)
    d['w_in'] = (D_MODEL ** -0.5) * nrm(ks[12], (DEPTH, D_MODEL, W_IN_COLS), F32)
    b_in = 0.01 * nrm(ks[13], (DEPTH, W_IN_COLS), F32)
    d['b_in'] = b_in.at[:, FGATE_OFFSET:].add(jnp.linspace(3.0, 6.0, MLSTM_HEADS, dtype=F32))
    d['hgrn_lb_logits'] = 0.1 * nrm(ks[14], (DEPTH + 1, HGRN_FDIM), F32)
    d['hgrn_norm_g'] = 1.0 + 0.01 * nrm(ks[15], (DEPTH, HGRN_WIDTH), F32)
    d['mlstm_conv_w'] = (CONV_WIDTH ** -0.5) * nrm(ks[16], (DEPTH, CONV_WIDTH, MLSTM_WIDTH), F32)
    d['mlstm_conv_b'] = 0.01 * nrm(ks[17], (DEPTH, MLSTM_WIDTH), F32)
    nb = MLSTM_WIDTH // QKV_BLOCK
    d['mlstm_wq'] = (QKV_BLOCK ** -0.5) * nrm(ks[18], (DEPTH, nb, QKV_BLOCK, QKV_BLOCK), F32)
    d['mlstm_wk'] = (QKV_BLOCK ** -0.5) * nrm(ks[19], (DEPTH, nb, QKV_BLOCK, QKV_BLOCK), F32)
    d['mlstm_wv'] = (QKV_BLOCK ** -0.5) * nrm(ks[20], (DEPTH, nb, QKV_BLOCK, QKV_BLOCK), F32)
    d['mlstm_norm_g'] = 1.0 + 0.01 * nrm(ks[21], (DEPTH, MLSTM_WIDTH), F32)
    d['mlstm_skip'] = 1.0 + 0.01 * nrm(ks[22], (DEPTH, MLSTM_WIDTH), F32)
    d['w_out'] = (D_MIX ** -0.5) * nrm(ks[23], (DEPTH, D_MIX, D_MODEL), F32)
    d['final_g'] = 1.0 + 0.01 * nrm(ks[24], (D_MODEL,), F32)
    return d


def reference(x_prompt, x_sample, c_prompt, c_sample, state_hgrn, state_mlstm_C, state_mlstm_n,
              state_mlstm_m, state_mlstm_conv, w_ada, b_ada, norm_g, w_in, b_in, hgrn_lb_logits,
              hgrn_norm_g, mlstm_conv_w, mlstm_conv_b, mlstm_wq, mlstm_wk, mlstm_wv, mlstm_norm_g,
              mlstm_skip, w_out, final_g):
    lb_all = jnp.cumsum(jax.nn.softmax(hgrn_lb_logits.astype(F32), axis=0), axis=0)
    xp, xs = x_prompt, x_sample
    Bp, Tp = xp.shape[0], xp.shape[1]
    Ts = xs.shape[1]
    chunk_p = CHUNK if Tp % CHUNK == 0 else Tp
    sp_all = [[], [], [], [], []]
    ss_all = [[], [], [], [], []]
    for l in range(DEPTH):
        wts = (lb_all[l], w_ada[l], b_ada[l], norm_g[l], w_in[l], b_in[l], hgrn_norm_g[l],
               mlstm_conv_w[l], mlstm_conv_b[l], mlstm_wq[l], mlstm_wk[l], mlstm_wv[l],
               mlstm_norm_g[l], mlstm_skip[l], w_out[l])
        zS = jnp.zeros((Bp, HGRN_HEADS, HGRN_DK, HGRN_DV), F32)
        zC = jnp.zeros((Bp, MLSTM_HEADS, MLSTM_DH, MLSTM_DH), F32)
        zn = jnp.zeros((Bp, MLSTM_HEADS, MLSTM_DH), F32)
        zm = jnp.zeros((Bp, MLSTM_HEADS), F32)
        zb = jnp.zeros((Bp, CONV_WIDTH - 1, MLSTM_WIDTH), xp.dtype)
        xp, sp = _layer(xp, c_prompt, zS, zC, zn, zm, zb, chunk_p, *wts)
        xs, ss = _layer(xs, c_sample, state_hgrn[l], state_mlstm_C[l], state_mlstm_n[l],
                        state_mlstm_m[l], state_mlstm_conv[l], Ts, *wts)
        for i in range(5):
            sp_all[i].append(sp[i])
            ss_all[i].append(ss[i])
    y_prompt = _rms(xp, final_g)
    y_sample = _rms(xs, final_g)
    hgrn_p = jnp.stack(sp_all[0]); C_p = jnp.stack(sp_all[1]); n_p = jnp.stack(sp_all[2])
    m_p = jnp.stack(sp_all[3]); conv_p = jnp.stack(sp_all[4])
    hgrn_s = jnp.stack(ss_all[0]); C_s = jnp.stack(ss_all[1]); n_s = jnp.stack(ss_all[2])
    m_s = jnp.stack(ss_all[3]); conv_s = jnp.stack(ss_all[4])
    return (y_prompt, y_sample, hgrn_p, C_p, n_p, m_p, conv_p, hgrn_s, C_s, n_s, m_s, conv_s)
```

```python
import numpy as np
from contextlib import ExitStack
import concourse.bass as bass
import concourse.mybir as mybir
from concourse.bass_utils import run_bass_kernel_spmd

F32 = mybir.dt.float32
BF16 = mybir.dt.bfloat16
AF = mybir.ActivationFunctionType
ALU = mybir.AluOpType
AX = mybir.AxisListType
NCORES = 8
EPS = 1e-6
LN16 = float(np.log(16.0))


class Res:
    def __init__(self):
        self.last_write = None
        self.reads = []
        self.psum = False


class Tl:
    def __init__(self, ap):
        self.ap = ap
        self.r = Res()

    def __getitem__(self, k):
        return self.ap[k]


class Sched:
    def __init__(self, nc, ctx):
        self.nc = nc
        self.engs = {}
        for name in ["pe", "act", "dve", "pool", "sp"]:
            sem = ctx.enter_context(nc.semaphore("sem_" + name))
            self.engs[name] = dict(name=name, sem=sem, count=0, ops=[], waited={})
        self.dma_ring = {}
        for q in ["sp", "pool"]:
            n = 24
            sems = [ctx.enter_context(nc.semaphore(f"dq_{q}_{i}")) for i in range(n)]
            self.dma_ring[q] = dict(sems=sems, idx=0, vals=[0] * n)
        self.out_events = []
        self.dma_events = []
        self.defer = None

    def _wait(self, e, ev):
        sem, val = ev
        w = e["waited"]
        key = sem.num
        if w.get(key, 0) >= val:
            return
        w[key] = val
        e["ops"].append(("wait", sem, val))

    def _deps(self, e, reads, writes):
        for t in reads:
            r = t.r
            if r.last_write is not None:
                self._wait(e, r.last_write)
            if r.psum:
                for ev in r.reads:
                    if ev[0] is not e["sem"]:
                        self._wait(e, ev)
        for t in writes:
            r = t.r
            if r.last_write is not None:
                self._wait(e, r.last_write)
            for ev in r.reads:
                self._wait(e, ev)

    def _record(self, ev, reads, writes):
        for t in writes:
            t.r.last_write = ev
            t.r.reads = []
        for t in reads:
            if t in writes:
                continue
            t.r.reads = [x for x in t.r.reads if x[0] is not ev[0]] + [ev]

    def op(self, eng, fn, reads=(), writes=()):
        if self.defer is not None:
            self.defer.append(("op", eng, fn, tuple(reads), tuple(writes), False))
            return None
        e = self.engs[eng]
        self._deps(e, reads, writes)
        e["count"] += 1
        ev = (e["sem"], e["count"])
        e["ops"].append(("op", fn, e["sem"], 1))
        self._record(ev, reads, writes)
        return ev

    def mm(self, fns, reads=(), writes=()):
        if self.defer is not None:
            self.defer.append(("mm", "pe", fns, tuple(reads), tuple(writes), False))
            return None
        e = self.engs["pe"]
        self._deps(e, reads, writes)
        for fn in fns[:-1]:
            e["ops"].append(("op", fn, None, 0))
        e["count"] += 1
        ev = (e["sem"], e["count"])
        e["ops"].append(("op", fns[-1], e["sem"], 1))
        self._record(ev, reads, writes)
        return ev

    def dma(self, q, fn, reads=(), writes=(), is_output=False):
        if self.defer is not None:
            self.defer.append(("dma", q, fn, tuple(reads), tuple(writes), is_output))
            return None
        e = self.engs[q]
        ring = self.dma_ring[q]
        i = ring["idx"] % len(ring["sems"])
        ring["idx"] += 1
        sem = ring["sems"][i]
        if ring["vals"][i] > 0:
            self._wait(e, (sem, ring["vals"][i]))
        self._deps(e, reads, writes)
        ring["vals"][i] += 16
        ev = (sem, ring["vals"][i])
        e["ops"].append(("op", fn, sem, 16))
        for t in writes:
            t.r.last_write = ev
            t.r.reads = []
        for t in reads:
            t.r.reads = t.r.reads + [ev]
        self.dma_events.append(ev)
        if is_output:
            self.out_events.append(ev)
        return ev

    def alias(self, src, dst):
        if self.defer is not None:
            self.defer.append(("alias", "none", None, tuple(src), tuple(dst), False))
            return
        evs = []
        for t in src:
            if t.r.last_write is not None:
                evs.append(t.r.last_write)
            evs.extend(t.r.reads)
        for t in dst:
            t.r.reads = list(t.r.reads) + evs

    def begin_window(self):
        assert self.defer is None
        self.defer = []

    @staticmethod
    def _probe_cost(kind, eng, fn):
        class _P:
            def __init__(self):
                self.calls = []
            def __getattr__(self, name):
                def f(*a, **k):
                    self.calls.append((name, a, k))
                    return self
                return f
        fns = fn if kind == "mm" else [fn]
        tot = 0.0
        lat = 0.0
        for f in fns:
            p = _P()
            try:
                f(p)
            except Exception:
                tot += 0.3
                continue
            for (name, a, k) in p.calls:
                out = k.get("out", a[0] if a else None)
                try:
                    shp = tuple(out.shape)
                    n = 1
                    for d in shp[1:]:
                        n *= int(d)
                    nb = n * int(shp[0]) * mybir.dt.size(out.dtype)
                except Exception:
                    n, nb = 256, 65536
                if kind == "mm":
                    slow = 1.0
                    try:
                        if k.get("lhsT", None) is not None and k["lhsT"].dtype == F32:
                            slow = 4.0
                    except Exception:
                        pass
                    tot += max(0.07, slow * n / 2000.0 + 0.02)
                elif kind == "dma":
                    tot += 0.1
                    lat = max(lat, 2.5 + nb / 120e3)
                elif eng == "act":
                    tot += 0.22 + n / 1200.0
                elif eng == "dve":
                    tot += 0.08 + n / 960.0
                else:
                    tot += 0.15 + n / 150.0
        return tot, lat

    def end_window(self):
        ops = self.defer
        self.defer = None
        n = len(ops)
        if n == 0:
            return
        rid = lambda t: id(t.r)
        preds = [set() for _ in range(n)]
        succs = [[] for _ in range(n)]
        last_w = {}
        readers = {}
        psum_ids = set()
        for o in ops:
            for t in o[3] + o[4]:
                if t.r.psum:
                    psum_ids.add(rid(t))
        for j, o in enumerate(ops):
            Rj = set(rid(t) for t in o[3])
            Wj = set(rid(t) for t in o[4])
            for r in Rj | Wj:
                if r in last_w:
                    preds[j].add(last_w[r])
            for r in Rj & psum_ids:
                for i in readers.get(r, ()):
                    if ops[i][1] != o[1]:
                        preds[j].add(i)
            for r in Wj:
                for i in readers.get(r, ()):
                    preds[j].add(i)
            for r in Wj:
                last_w[r] = j
                readers[r] = []
            for r in Rj - Wj:
                readers.setdefault(r, []).append(j)
            preds[j].discard(j)
        preds = [sorted(p) for p in preds]
        for j in range(n):
            for i in preds[j]:
                succs[i].append(j)
        cost = [(0.0, 0.0) if o[0] == "alias" else self._probe_cost(o[0], o[1], o[2]) for o in ops]
        cp = [0.0] * n
        for i in range(n - 1, -1, -1):
            m = 0.0
            for j in succs[i]:
                m = max(m, cp[j])
            cp[i] = cost[i][0] + cost[i][1] + m
        eng_free = {}
        fin = [0.0] * n
        npred = [len(p) for p in preds]
        ready = [i for i in range(n) if npred[i] == 0]
        done = [False] * n
        order = []
        LAT = 0.7
        while ready:
            best = None
            for i in ready:
                eng = ops[i][1]
                st = eng_free.get(eng, 0.0)
                for p in preds[i]:
                    l = 0.0 if ops[p][1] == eng else LAT
                    st = max(st, fin[p] + l)
                key = (st, -cp[i], i)
                if best is None or key < best[0]:
                    best = (key, i, st)
            _, i, st = best
            ready.remove(i)
            eng = ops[i][1]
            if eng != "none":
                eng_free[eng] = st + cost[i][0]
            fin[i] = st + cost[i][0] + cost[i][1]
            order.append(i)
            for j in succs[i]:
                npred[j] -= 1
                if npred[j] == 0:
                    ready.append(j)
        assert len(order) == n
        for i in order:
            kind, eng, fn, r, w, is_out = ops[i]
            if kind == "alias":
                self.alias(r, w)
            elif kind == "op":
                self.op(eng, fn, r, w)
            elif kind == "mm":
                self.mm(fn, r, w)
            else:
                self.dma(eng, fn, r, w, is_output=is_out)

    def barrier(self):
        evs = [(e["sem"], e["count"]) for e in self.engs.values() if e["count"] > 0] + self.dma_events
        for e in self.engs.values():
            for ev in evs:
                if ev[0] is e["sem"]:
                    continue
                self._wait(e, ev)
        self.dma_events = []

    def finish(self):
        self.barrier()
        e = self.engs["sp"]
        for ev in self.out_events:
            self._wait(e, ev)

    def emit(self, block):
        def replay(engname):
            def f(engine):
                for item in self.engs[engname]["ops"]:
                    if item[0] == "wait":
                        engine.wait_ge(item[1], item[2])
                    else:
                        _, fn, sem, inc = item
                        ins = fn(engine)
                        if sem is not None:
                            ins.then_inc(sem, inc)
            return f
        block.sync(replay("sp"))
        block.scalar(replay("act"))
        block.vector(replay("dve"))
        block.gpsimd(replay("pool"))
        block.tensor(replay("pe"))


class Arena:
    def __init__(self, ap, nwords):
        self.ap = ap
        self.n = nwords
        self.off = 0

    def reset(self):
        self.off = 0

    def alloc(self, rows, free, dt):
        nel = int(np.prod(free))
        nw = (nel + 1) // 2 if dt == BF16 else nel
        nw = (nw + 7) // 8 * 8
        assert self.off + nw <= self.n, f"arena overflow {self.off}+{nw}>{self.n}"
        v = self.ap[0:rows, self.off:self.off + nw]
        self.off += nw
        if dt == BF16:
            v = v.bitcast(BF16)
        v = v[:, 0:nel]
        if len(free) == 2:
            v = v.rearrange("p (a b) -> p a b", a=free[0])
        elif len(free) == 3:
            v = v.rearrange("p (a b c) -> p a b c", a=free[0], b=free[1])
        return Tl(v)


def build_nc():
    nc = bass.Bass("TRN2", target_bir_lowering=False)
    di = lambda name, shape: nc.dram_tensor(name, shape, F32, kind="ExternalInput").ap()
    do = lambda name, shape: nc.dram_tensor(name, shape, F32, kind="ExternalOutput").ap()
    xp = di("xp", [2048, 1024]); xs = di("xs", [64, 1024]); ccd = di("cc", [17, 1024])
    sh = di("sh", [16, 8, 128, 128]); sC = di("sC", [16, 4, 256, 256]); sn = di("sn", [16, 4, 256])
    sm = di("sm", [16, 4]); scv = di("scv", [48, 1024])
    w_ada = di("w_ada", [1024, 3072]); b_ada = di("b_ada", [24, 128]);
    w_out = di("w_out", [2048, 1024])
    w_hd = di("w_h", [8, 128, 4096]); w_md = di("w_m", [4, 128, 6144]); w_gd = di("w_g", [128, 64])
    prm = di("prm", [32, 1024]); bgd = di("bg", [4, 2]); wbdd = di("wbd", [128, 3, 8, 128])
    cmask_d = di("cmask", [128, 128]); smask_d = di("smask", [64, 64]); rowmask_d = di("rowmask", [64, 16])
    colmask_d = di("colmask", [1, 1024]); sel_d = di("sel", [32, 3, 128]); selg_d = di("selg", [17, 192])
    yp = do("yp", [2048, 1024]); ys = do("ys", [64, 1024])
    ohp = do("ohp", [8, 128, 128]); oCp = do("oCp", [4, 256, 256]); onp = do("onp", [4, 256]); omp = do("omp", [4, 1])
    ocp = do("ocp", [3, 1024])
    ohs = do("ohs", [16, 8, 128, 128]); oCs = do("oCs", [16, 4, 256, 256]); ons = do("ons", [16, 4, 256])
    oms = do("oms", [16, 4]); ocs = do("ocs", [16, 3, 1024])

    w_ada_v = w_ada.rearrange("(k p) c -> p k c", p=128)
    w_out_v = w_out.rearrange("(k p) c -> p k c", p=128)

    with ExitStack() as ctx:
        S = Sched(nc, ctx)
        sbt = lambda name, shape, dt: Tl(ctx.enter_context(nc.sbuf_tensor(name, shape, dt))[:])
        def pst(name, shape, dt):
            t = Tl(ctx.enter_context(nc.psum_tensor(name, shape, dt))[:])
            t.r.psum = True
            return t
        ACT = lambda fn, r=(), w=(): S.op("act", fn, r, w)
        DVE = lambda fn, r=(), w=(): S.op("dve", fn, r, w)
        POOL = lambda fn, r=(), w=(): S.op("pool", fn, r, w)
        MM = lambda fns, r=(), w=(): S.mm(fns, r, w)
        DMA = lambda fn, r=(), w=(), out=False: S.dma("sp", fn, r, w, is_output=out)
        DMAC = lambda fn, r=(), w=(): S.dma("pool", fn, r, w)

        hT = sbt("hT", [128, 8, 2112], BF16)
        mixT = sbt("mixT", [128, 16, 2112], BF16)
        wbuf = sbt("wbuf", [128, 8 * 768], BF16)
        arena_t = sbt("arena", [128, 15872], F32)
        AR = Arena(arena_t.ap, 15872)
        ident32 = sbt("ident32", [128, 128], F32)
        identb = sbt("identb", [128, 128], BF16)
        ones32 = sbt("ones32", [128, 128], F32)
        P = sbt("P", [128, 8, 32], F32)
        Pn = sbt("Pn", [128, 8, 32], F32)
        lbv = sbt("lbv", [128, 8, 4], F32)
        convd = sbt("convd", [128, 8, 4, 128], BF16)
        wbd = sbt("wbd_s", [128, 3, 8, 128], BF16)
        bc_hi = sbt("bc_hi", [128, 1024], BF16)
        bc_mo = sbt("bc_mo", [128, 1024], BF16)
        msk = sbt("msk", [128, 576], F32)
        cmask = sbt("cmask_s", [128, 128], F32)
        smask = sbt("smask_s", [64, 64], F32)
        rowmask = sbt("rowmask_s", [64, 16], BF16)
        colmask = sbt("colmask_s", [128, 16, 64], BF16)
        modT = sbt("modT", [128, 24, 17], F32)
        Amod = sbt("Amod", [128, 8, 17], F32)
        Sbf = sbt("Sbf", [128, 128], BF16)
        sm8 = sbt("sm8", [128, 16], F32)
        gbc = sbt("gbc", [128, 32, 4], F32)
        ectok = sbt("ectok", [128, 17, 4], F32)
        thrtok = sbt("thrtok", [128, 17, 4], F32)
        bA = pst("bA", [128, 512], F32); bB = pst("bB", [128, 512], F32)
        bC = pst("bC", [128, 512], F32); bD = pst("bD", [128, 512], F32)
        bS = pst("bS", [128, 512], F32); bO = pst("bO", [128, 512], F32); bU = pst("bU", [128, 512], F32)
        bT = pst("bT", [128, 1024], BF16)

        POOL(lambda e: e.memset(ident32.ap, 0.0), w=[ident32])
        POOL(lambda e: e.affine_select(out=ident32.ap, in_=ident32.ap, pattern=[[-1, 128]], compare_op=ALU.not_equal,
                                       fill=1.0, base=0, channel_multiplier=1), r=[ident32], w=[ident32])
        DVE(lambda e: e.tensor_copy(out=identb.ap, in_=ident32.ap), r=[ident32], w=[identb])
        POOL(lambda e: e.memset(ones32.ap, 1.0), w=[ones32])
        POOL(lambda e: e.memset(msk.ap, 1.0), w=[msk])
        POOL(lambda e: e.memset(msk.ap[:, 0:512].rearrange("p (c l) -> p c l", l=128)[:, :, 0:1], 0.0), r=[msk], w=[msk])
        POOL(lambda e: e.memset(msk.ap[:, 512:576].rearrange("p (c l) -> p c l", l=4)[:, :, 0:1], 0.0), r=[msk], w=[msk])
        DMA(lambda e: e.dma_start(out=cmask.ap, in_=cmask_d), w=[cmask])
        DMA(lambda e: e.dma_start(out=smask.ap, in_=smask_d), w=[smask])
        DMAC(lambda e: e.dma_start(out=rowmask.ap, in_=rowmask_d), w=[rowmask])
        DMAC(lambda e: e.dma_start(out=colmask.ap.rearrange("p a b -> p (a b)"), in_=colmask_d.partition_broadcast(128)),
             w=[colmask])
        DMAC(lambda e: e.dma_start(out=wbd.ap, in_=wbdd), w=[wbd])

        AR.reset()
        prm_s = AR.alloc(32, (1024,), F32)
        sel_s = AR.alloc(32, (3, 128), F32)
        bada_s = AR.alloc(24, (128,), F32)
        badaT = AR.alloc(128, (24,), F32)
        DMA(lambda e: e.dma_start(out=prm_s.ap, in_=prm), w=[prm_s])
        DMA(lambda e: e.dma_start(out=sel_s.ap, in_=sel_d), w=[sel_s])
        DMA(lambda e: e.dma_start(out=bada_s.ap, in_=b_ada), w=[bada_s])
        MM([lambda e, k=k: e.transpose(out=bS.ap[:, k * 32:(k + 1) * 32], in_=prm_s.ap[:, k * 128:(k + 1) * 128],
                                       identity=ident32.ap[0:32, 0:32]) for k in range(8)],
           r=[prm_s, ident32], w=[bS])
        DVE(lambda e: e.tensor_copy(out=P.ap.rearrange("p a b -> p (a b)"), in_=bS.ap[:, 0:256]), r=[bS], w=[P])
        DVE(lambda e: e.tensor_scalar(out=Pn.ap.rearrange("p a b -> p (a b)"), in0=P.ap.rearrange("p a b -> p (a b)"),
                                      scalar1=-1.0, scalar2=None, op0=ALU.mult), r=[P], w=[Pn])
        MM([lambda e: e.transpose(out=bO.ap[:, 0:24], in_=bada_s.ap, identity=ident32.ap[0:24, 0:24])],
           r=[bada_s, ident32], w=[bO])
        DVE(lambda e: e.tensor_copy(out=badaT.ap, in_=bO.ap[:, 0:24]), r=[bO], w=[badaT])
        DVE(lambda e: e.tensor_tensor(out=lbv.ap[:, :, 2], in0=P.ap[:, :, 9], in1=P.ap[:, :, 8], op=ALU.subtract), r=[P], w=[lbv])
        ACT(lambda e: e.activation(out=lbv.ap[:, :, 2], in_=lbv.ap[:, :, 2], func=AF.Exp), r=[lbv], w=[lbv])
        DVE(lambda e: e.tensor_scalar(out=lbv.ap[:, :, 2], in0=lbv.ap[:, :, 2], scalar1=1.0, scalar2=None, op0=ALU.add), r=[lbv], w=[lbv])
        DVE(lambda e: e.reciprocal(out=lbv.ap[:, :, 0], in_=lbv.ap[:, :, 2]), r=[lbv], w=[lbv])
        ACT(lambda e: e.activation(out=lbv.ap[:, :, 3], in_=lbv.ap[:, :, 0], func=AF.Ln, scale=-1.0, bias=1.0), r=[lbv], w=[lbv])
        DVE(lambda e: e.tensor_tensor(out=lbv.ap[:, :, 1], in0=lbv.ap[:, :, 3], in1=P.ap[:, :, 1], op=ALU.subtract), r=[lbv, P], w=[lbv])
        for k in range(8):
            for i in range(4):
                DVE(lambda e, k=k, i=i: e.tensor_scalar(out=convd.ap[:, k, i, :], in0=ident32.ap, scalar1=P.ap[:, k, 11 + i:12 + i],
                                                        scalar2=None, op0=ALU.mult), r=[ident32, P], w=[convd])
        for (si, dst) in ((0, bc_hi), (1, bc_mo)):
            for half, bank in ((0, bA), (1, bB)):
                MM([lambda e, si=si, half=half, bank=bank: e.matmul(bank.ap, lhsT=sel_s.ap[:, si, :], rhs=prm_s.ap[:, half * 512:(half + 1) * 512],
                                                                    start=True, stop=True)], r=[sel_s, prm_s], w=[bank])
                DVE(lambda e, dst=dst, half=half, bank=bank: e.tensor_copy(out=dst.ap[:, half * 512:(half + 1) * 512], in_=bank.ap), r=[bank], w=[dst])

        S.begin_window()
        cc_s = AR.alloc(17, (1024,), F32)
        ce = AR.alloc(17, (1024,), F32)
        csil = AR.alloc(17, (1024,), BF16)
        siluT = AR.alloc(128, (8, 17), BF16)
        modtok = AR.alloc(17, (3072,), F32)
        wa = [AR.alloc(128, (3072,), BF16) for _ in range(2)]
        DMA(lambda e: e.dma_start(out=cc_s.ap, in_=ccd), w=[cc_s])
        ACT(lambda e: e.activation(out=ce.ap, in_=cc_s.ap, func=AF.Exp, scale=-1.0), r=[cc_s], w=[ce])
        DVE(lambda e: e.tensor_scalar(out=ce.ap, in0=ce.ap, scalar1=1.0, scalar2=None, op0=ALU.add), r=[ce], w=[ce])
        DVE(lambda e: e.reciprocal(out=ce.ap, in_=ce.ap), r=[ce], w=[ce])
        DVE(lambda e: e.tensor_tensor(out=csil.ap, in0=cc_s.ap, in1=ce.ap, op=ALU.mult), r=[cc_s, ce], w=[csil])
        MM([lambda e, k=k: e.transpose(out=bT.ap[:, k * 32:k * 32 + 17], in_=csil.ap[:, k * 128:(k + 1) * 128],
                                       identity=identb.ap[0:17, 0:17]) for k in range(8)], r=[csil, identb], w=[bT])
        DVE(lambda e: e.tensor_copy(out=siluT.ap, in_=bT.ap[:, 0:256].rearrange("p (a b) -> p a b", a=8)[:, :, 0:17]), r=[bT], w=[siluT])
        banks6 = [bA, bB, bC, bD, bS, bO]
        for k in range(8):
            w_t = wa[k % 2]
            DMAC(lambda e, k=k, w_t=w_t: e.dma_start(out=w_t.ap, in_=w_ada_v[:, k, :]), w=[w_t])
            for n in range(6):
                MM([lambda e, k=k, n=n, w_t=w_t: e.matmul(banks6[n].ap[0:17, :], lhsT=siluT.ap[:, k, :], rhs=w_t.ap[:, n * 512:(n + 1) * 512],
                                                          start=(k == 0), stop=(k == 7))], r=[siluT, w_t], w=[banks6[n]])
        for n in range(6):
            DVE(lambda e, n=n: e.tensor_copy(out=modtok.ap[:, n * 512:(n + 1) * 512], in_=banks6[n].ap[0:17, :]), r=[banks6[n]], w=[modtok])
        MM([lambda e, j=j: e.transpose(out=bU.ap[:, j * 17:(j + 1) * 17], in_=modtok.ap[:, j * 128:(j + 1) * 128],
                                       identity=ident32.ap[0:17, 0:17]) for j in range(24)], r=[modtok, ident32], w=[bU])
        DVE(lambda e: e.tensor_tensor(out=modT.ap, in0=bU.ap[:, 0:408].rearrange("p (a b) -> p a b", a=24),
                                      in1=badaT.ap.unsqueeze(2).to_broadcast([128, 24, 17]), op=ALU.add), r=[bU, badaT], w=[modT])
        DVE(lambda e: e.scalar_tensor_tensor(out=Amod.ap, in0=modT.ap[:, 8:16, :], scalar=1.0,
                                             in1=P.ap[:, :, 7:8].to_broadcast([128, 8, 17]), op0=ALU.add, op1=ALU.mult),
            r=[modT, P], w=[Amod])

        xt = [AR.alloc(128, (1024,), F32) for _ in range(2)]
        junk = AR.alloc(128, (1024,), BF16)
        xn = [AR.alloc(128, (1024,), BF16) for _ in range(2)]
        tmpf = AR.alloc(128, (8, 128), F32)
        st2 = AR.alloc(128, (8,), F32)
        def tile2(i):
            rows = 128 if i < 16 else 64
            x_t = xt[i % 2]; xn_t = xn[i % 2]
            src = xp[i * 128:(i + 1) * 128, :] if i < 16 else xs
            DMA(lambda e, x_t=x_t, src=src, rows=rows: e.dma_start(out=x_t.ap[0:rows, :], in_=src), w=[x_t])
            ACT(lambda e, x_t=x_t, rows=rows: e.activation(out=junk.ap[0:rows, :], in_=x_t.ap[0:rows, :], func=AF.Square, scale=1.0 / 32.0,
                                                          accum_out=st2.ap[0:rows, 0:1]), r=[x_t], w=[junk, st2])
            ACT(lambda e, rows=rows: e.activation(out=st2.ap[0:rows, 1:2], in_=st2.ap[0:rows, 0:1], func=AF.Ln, bias=EPS), r=[st2], w=[st2])
            ACT(lambda e, rows=rows: e.activation(out=st2.ap[0:rows, 2:3], in_=st2.ap[0:rows, 1:2], func=AF.Exp, scale=-0.5), r=[st2], w=[st2])
            ACT(lambda e, x_t=x_t, xn_t=xn_t, rows=rows: e.activation(out=xn_t.ap[0:rows, :], in_=x_t.ap[0:rows, :], func=AF.Copy,
                                                                     scale=st2.ap[0:rows, 2:3]), r=[x_t, st2], w=[xn_t])
            MM([lambda e, k=k, xn_t=xn_t, rows=rows: e.transpose(out=bT.ap[:, k * 128:k * 128 + rows], in_=xn_t.ap[0:rows, k * 128:(k + 1) * 128],
                                                                identity=identb.ap[0:rows, 0:rows]) for k in range(8)],
               r=[xn_t, identb], w=[bT])
            pv = bT.ap.rearrange("p (a b) -> p a b", a=8)[:, :, 0:rows]
            if i < 16:
                DVE(lambda e, pv=pv: e.tensor_tensor(out=tmpf.ap, in0=pv, in1=Amod.ap[:, :, 0:1].to_broadcast([128, 8, 128]), op=ALU.mult),
                    r=[bT, Amod], w=[tmpf])
                DVE(lambda e, i=i: e.tensor_tensor(out=hT.ap[:, :, i * 128:(i + 1) * 128], in0=tmpf.ap,
                                                    in1=modT.ap[:, 0:8, 0:1].to_broadcast([128, 8, 128]), op=ALU.add),
                     r=[tmpf, modT], w=[hT])
            else:
                for k in range(8):
                    DVE(lambda e, k=k: e.tensor_tensor(out=tmpf.ap[:, k, 0:64].rearrange("p (b t) -> p b t", t=4),
                                                       in0=bT.ap[:, k * 128:k * 128 + 64].rearrange("p (b t) -> p b t", t=4),
                                                       in1=Amod.ap[:, k, 1:17].unsqueeze(2).to_broadcast([128, 16, 4]), op=ALU.mult),
                        r=[bT, Amod], w=[tmpf])
                    DVE(lambda e, k=k: e.tensor_tensor(out=hT.ap[:, k, 2048:2112].rearrange("p (b t) -> p b t", t=4),
                                                        in0=tmpf.ap[:, k, 0:64].rearrange("p (b t) -> p b t", t=4),
                                                        in1=modT.ap[:, k, 1:17].unsqueeze(2).to_broadcast([128, 16, 4]), op=ALU.add),
                         r=[tmpf, modT], w=[hT])

        for i in range(17):
            tile2(i)
        S.end_window()

        GROUPS = [(0, 512), (512, 512), (1024, 512), (1536, 512), (2048, 64)]

        def silu_from_psum(ps, nbias_ap, bias_ap, T, ez, sg, out_ap, out_t, extra_r=()):
            ACT(lambda e: e.activation(out=ez.ap[:, 0:T], in_=ps.ap[:, 0:T], func=AF.Exp, scale=-1.0, bias=nbias_ap), r=[ps, Pn], w=[ez])
            ACT(lambda e: e.activation(out=ez.ap[:, 0:T], in_=ez.ap[:, 0:T], func=AF.Ln, bias=1.0), r=[ez], w=[ez])
            ACT(lambda e: e.activation(out=sg.ap[:, 0:T], in_=ez.ap[:, 0:T], func=AF.Exp, scale=-1.0), r=[ez], w=[sg])
            DVE(lambda e: e.scalar_tensor_tensor(out=out_ap, in0=ps.ap[:, 0:T], scalar=bias_ap, in1=sg.ap[:, 0:T], op0=ALU.add, op1=ALU.mult),
                r=[ps, sg, P], w=[out_t])

        def interleave(*gens):
            gens = [g for g in gens if g is not None]
            while gens:
                for g in list(gens):
                    try:
                        next(g)
                    except StopIteration:
                        gens.remove(g)

        def silu_gen(ps, nbias_ap, bias_ap, T, ez, sg, out_ap, out_t):
            ACT(lambda e: e.activation(out=ez.ap[:, 0:T], in_=ps.ap[:, 0:T], func=AF.Exp, scale=-1.0, bias=nbias_ap), r=[ps, Pn], w=[ez]); yield
            ACT(lambda e: e.activation(out=ez.ap[:, 0:T], in_=ez.ap[:, 0:T], func=AF.Ln, bias=1.0), r=[ez], w=[ez]); yield
            ACT(lambda e: e.activation(out=sg.ap[:, 0:T], in_=ez.ap[:, 0:T], func=AF.Exp, scale=-1.0), r=[ez], w=[sg]); yield
            DVE(lambda e: e.scalar_tensor_tensor(out=out_ap, in0=ps.ap[:, 0:T], scalar=bias_ap, in1=sg.ap[:, 0:T], op0=ALU.add, op1=ALU.mult),
                r=[ps, sg, P], w=[out_t]); yield

        S.barrier()
        AR.reset()
        f_e = AR.alloc(128, (512,), F32); f_L1 = AR.alloc(128, (512,), F32); f_L2 = AR.alloc(128, (512,), F32)
        f_G = AR.alloc(128, (512,), F32); f_eG = AR.alloc(128, (512,), F32)
        f_ez = AR.alloc(128, (512,), F32); f_sg = AR.alloc(128, (512,), F32)
        wh2_t = AR.alloc(128, (4096,), BF16)
        GO = []
        for _ in range(2):
            GO.append(dict(qt=AR.alloc(128, (512,), BF16), kt=AR.alloc(128, (512,), BF16), kd=AR.alloc(128, (512,), BF16),
                           gzg=AR.alloc(128, (512,), BF16), vtok=AR.alloc(128, (4, 128), BF16), kdtok=AR.alloc(128, (4, 128), BF16),
                           eGL=AR.alloc(128, (16,), F32)))
        ATm = AR.alloc(128, (128,), BF16); on_t = AR.alloc(128, (128,), BF16); junk2 = AR.alloc(128, (128,), BF16)
        ATm4 = AR.alloc(128, (4, 128), BF16); on4 = AR.alloc(128, (4, 128), BF16)
        S4 = AR.alloc(128, (4, 128), F32); Sb3 = AR.alloc(128, (3, 128), BF16); sq4 = AR.alloc(128, (512,), F32)
        Sin = AR.alloc(128, (16, 128), F32); Sbb = AR.alloc(128, (16, 128), BF16)
        qexp = AR.alloc(128, (16, 64), BF16); kexp = AR.alloc(64, (16, 128), BF16)
        wh_a = Tl(wbuf.ap[:, 0:4096].rearrange("p (k s c) -> p k s c", k=8, s=4))
        wh_a.r = wbuf.r
        wh_b = Tl(wh2_t.ap.rearrange("p (k s c) -> p k s c", k=8, s=4))
        wh_b.r = wh2_t.r
        whs = [wh_a, wh_b]
        bDb = Tl(bD.ap.bitcast(BF16))
        bDb.r = bD.r
        bTf = Tl(bT.ap.bitcast(F32))
        bTf.r = bT.r

        def h_stage1(h, tok0, T, go):
            wh = whs[h % 2]
            qt, kt, kd, gzg, vtok, kdtok, eGL = go["qt"], go["kt"], go["kd"], go["gzg"], go["vtok"], go["kdtok"], go["eGL"]
            sample = (T == 64)
            rows = 64 if sample else 128
            nch = 1 if sample else 4
            mk = msk.ap[:, 512:576] if sample else msk.ap[:, 0:512]
            if tok0 == 0:
                DMAC(lambda e: e.dma_start(out=wh.ap.rearrange("p k s c -> p (k s c)"), in_=w_hd[h]), w=[wh])
                yield
            for (bank, s_i) in ((bA, 1), (bB, 0), (bC, 3)):
                MM([lambda e, k=k, bank=bank, s_i=s_i: e.matmul(bank.ap[:, 0:T], lhsT=wh.ap[:, k, s_i, :], rhs=hT.ap[:, k, tok0:tok0 + T],
                                                                start=(k == 0), stop=(k == 7)) for k in range(8)], r=[wh, hT], w=[bank])
                yield
            fns = []
            for i in range(nch):
                for k in range(8):
                    fns.append(lambda e, i=i, k=k: e.matmul(bD.ap[0:rows, i * 128:(i + 1) * 128], lhsT=hT.ap[:, k, tok0 + i * 128: tok0 + i * 128 + rows],
                                                            rhs=wh.ap[:, k, 2, :], start=(k == 0), stop=(k == 7)))
            MM(fns, r=[wh, hT], w=[bD]); yield
            ACT(lambda e: e.activation(out=f_e.ap[:, 0:T], in_=bA.ap[:, 0:T], func=AF.Exp, scale=-1.0, bias=Pn.ap[:, h, 1:2]), r=[bA, Pn], w=[f_e]); yield
            ACT(lambda e: e.activation(out=f_L1.ap[:, 0:T], in_=f_e.ap[:, 0:T], func=AF.Ln, bias=1.0), r=[f_e], w=[f_L1]); yield
            ACT(lambda e: e.activation(out=f_L2.ap[:, 0:T], in_=f_e.ap[:, 0:T], func=AF.Ln, bias=1.0, scale=lbv.ap[:, h, 0:1]), r=[f_e, lbv], w=[f_L2]); yield
            DVE(lambda e: e.tensor_tensor(out=f_L2.ap[:, 0:T], in0=f_L2.ap[:, 0:T], in1=f_L1.ap[:, 0:T], op=ALU.subtract), r=[f_L2, f_L1], w=[f_L2]); yield
            DVE(lambda e: e.tensor_tensor_scan(out=f_G.ap[:, 0:T], data0=mk, data1=f_L2.ap[:, 0:T], initial=0.0, op0=ALU.mult, op1=ALU.add),
                r=[msk, f_L2], w=[f_G]); yield
            ACT(lambda e: e.activation(out=f_eG.ap[:, 0:T], in_=f_G.ap[:, 0:T], func=AF.Exp), r=[f_G], w=[f_eG]); yield
            DVE(lambda e: e.scalar_tensor_tensor(out=qt.ap[:, 0:T], in0=bB.ap[:, 0:T], scalar=P.ap[:, h, 0:1], in1=f_eG.ap[:, 0:T],
                                                 op0=ALU.add, op1=ALU.mult), r=[bB, P, f_eG], w=[qt]); yield
            DVE(lambda e: e.scalar_tensor_tensor(out=f_e.ap[:, 0:T], in0=bA.ap[:, 0:T], scalar=-1.0, in1=f_L1.ap[:, 0:T],
                                                 op0=ALU.mult, op1=ALU.subtract), r=[bA, f_L1], w=[f_e]); yield
            DVE(lambda e: e.tensor_tensor(out=f_e.ap[:, 0:T], in0=f_e.ap[:, 0:T], in1=f_G.ap[:, 0:T], op=ALU.subtract), r=[f_e, f_G], w=[f_e]); yield
            ACT(lambda e: e.activation(out=kt.ap[:, 0:T], in_=f_e.ap[:, 0:T], func=AF.Exp, bias=lbv.ap[:, h, 1:2]), r=[f_e, lbv], w=[kt]); yield
            L = 4 if sample else 128
            ncg = T // L
            Gv = f_G.ap[:, 0:T].rearrange("p (c l) -> p c l", l=L)
            ACT(lambda e: e.activation(out=eGL.ap[:, 0:ncg], in_=Gv[:, :, L - 1], func=AF.Exp), r=[f_G], w=[eGL]); yield
            DVE(lambda e: e.tensor_tensor(out=kd.ap[:, 0:T].rearrange("p (c l) -> p c l", l=L), in0=kt.ap[:, 0:T].rearrange("p (c l) -> p c l", l=L),
                                          in1=eGL.ap[:, 0:ncg].unsqueeze(2).to_broadcast([128, ncg, L]), op=ALU.mult), r=[kt, eGL], w=[kd]); yield
            yield from silu_gen(bC, Pn.ap[:, h, 3:4], P.ap[:, h, 3:4], T, f_ez, f_sg, gzg.ap[:, 0:T], gzg)

            DVE(lambda e: e.tensor_tensor(out=vtok.ap[0:rows, 0:nch, :], in0=bD.ap[0:rows, 0:nch * 128].rearrange("p (a b) -> p a b", a=nch),
                                          in1=bc_hi.ap[0:rows, h * 128:(h + 1) * 128].unsqueeze(1).to_broadcast([rows, nch, 128]), op=ALU.add),
                r=[bD, bc_hi], w=[vtok]); yield
            MM([lambda e, i=i: e.transpose(out=bDb.ap[0:rows, i * 128:(i + 1) * 128], in_=kd.ap[:, i * 128:i * 128 + rows], identity=identb.ap)
                for i in range(nch)], r=[kd, identb], w=[bDb]); yield
            DVE(lambda e: e.tensor_copy(out=kdtok.ap[0:rows, 0:nch, :], in_=bDb.ap[0:rows, 0:nch * 128].rearrange("p (a b) -> p a b", a=nch)),
                r=[bDb], w=[kdtok]); yield

        def o_epilogue(h, rows, tok0, gzg):
            ACT(lambda e: e.activation(out=junk2.ap[0:rows, :], in_=bO.ap[0:rows, 0:128], func=AF.Square, scale=float(128 ** -0.5),
                                       accum_out=sm8.ap[0:rows, 0:1]), r=[bO], w=[junk2, sm8]); yield
            ACT(lambda e: e.activation(out=sm8.ap[0:rows, 1:2], in_=sm8.ap[0:rows, 0:1], func=AF.Ln, bias=EPS), r=[sm8], w=[sm8]); yield
            ACT(lambda e: e.activation(out=sm8.ap[0:rows, 2:3], in_=sm8.ap[0:rows, 1:2], func=AF.Exp, scale=-0.5), r=[sm8], w=[sm8]); yield
            ACT(lambda e: e.activation(out=on_t.ap[0:rows, :], in_=bO.ap[0:rows, 0:128], func=AF.Copy, scale=sm8.ap[0:rows, 2:3]),
                r=[bO, sm8], w=[on_t]); yield
            MM([lambda e: e.transpose(out=bT.ap[:, 0:rows], in_=on_t.ap[0:rows, :], identity=identb.ap[0:rows, 0:rows])],
               r=[on_t, identb], w=[bT]); yield
            DVE(lambda e: e.scalar_tensor_tensor(out=mixT.ap[:, h, tok0:tok0 + rows], in0=bT.ap[:, 0:rows], scalar=P.ap[:, h, 10:11],
                                                 in1=gzg.ap[:, tok0 % 512:tok0 % 512 + rows], op0=ALU.mult, op1=ALU.mult), r=[bT, gzg, P], w=[mixT]); yield

        def h_stage2(h, tok0, T, go):
            qt, kt, kd, gzg, vtok, kdtok, eGL = go["qt"], go["kt"], go["kd"], go["gzg"], go["vtok"], go["kdtok"], go["eGL"]
            sample = (T == 64)
            if tok0 == 0:
                POOL(lambda e: e.memset(S4.ap, 0.0), w=[S4])
                POOL(lambda e: e.memset(Sbf.ap, 0.0), w=[Sbf]); yield
            if not sample:
                MM([lambda e, i=i: e.matmul(bS.ap[:, i * 128:(i + 1) * 128], lhsT=kt.ap[:, i * 128:(i + 1) * 128], rhs=qt.ap[:, i * 128:(i + 1) * 128],
                                            start=True, stop=True) for i in range(4)], r=[kt, qt], w=[bS]); yield
                DVE(lambda e: e.tensor_tensor(out=ATm4.ap, in0=bS.ap.rearrange("p (a b) -> p a b", a=4),
                                              in1=cmask.ap.unsqueeze(1).to_broadcast([128, 4, 128]), op=ALU.mult), r=[bS, cmask], w=[ATm4]); yield
                MM([lambda e, i=i: e.matmul(bU.ap[:, i * 128:(i + 1) * 128], lhsT=kdtok.ap[:, i, :], rhs=vtok.ap[:, i, :], start=True, stop=True)
                    for i in range(4)], r=[kdtok, vtok], w=[bU]); yield
                for i in range(4):
                    DVE(lambda e, i=i: e.scalar_tensor_tensor(out=S4.ap[:, i, :], in0=S4.ap[:, (i + 3) % 4, :], scalar=eGL.ap[:, i:i + 1],
                                                              in1=bU.ap[:, i * 128:(i + 1) * 128], op0=ALU.mult, op1=ALU.add), r=[S4, eGL, bU], w=[S4]); yield
                ACT(lambda e: e.activation(out=Sb3.ap, in_=S4.ap[:, 0:3, :], func=AF.Copy), r=[S4], w=[Sb3]); yield
                fns = []
                for i in range(4):
                    fns.append(lambda e, i=i: e.matmul(bO.ap[:, i * 128:(i + 1) * 128], lhsT=ATm4.ap[:, i, :], rhs=vtok.ap[:, i, :], start=True, stop=False))
                    rhs_s = Sbf.ap if i == 0 else Sb3.ap[:, i - 1, :]
                    fns.append(lambda e, i=i, rhs_s=rhs_s: e.matmul(bO.ap[:, i * 128:(i + 1) * 128], lhsT=qt.ap[:, i * 128:(i + 1) * 128], rhs=rhs_s,
                                                                    start=False, stop=True))
                MM(fns, r=[ATm4, vtok, qt, Sbf, Sb3], w=[bO]); yield
                ACT(lambda e: e.activation(out=Sbf.ap, in_=S4.ap[:, 3, :], func=AF.Copy), r=[S4], w=[Sbf]); yield
                for i in range(4):
                    ACT(lambda e, i=i: e.activation(out=sq4.ap[:, i * 128:(i + 1) * 128], in_=bO.ap[:, i * 128:(i + 1) * 128], func=AF.Square,
                                                    scale=float(128 ** -0.5), accum_out=sm8.ap[:, i:i + 1]), r=[bO], w=[sq4, sm8]); yield
                ACT(lambda e: e.activation(out=sm8.ap[:, 4:8], in_=sm8.ap[:, 0:4], func=AF.Ln, bias=EPS), r=[sm8], w=[sm8]); yield
                ACT(lambda e: e.activation(out=sm8.ap[:, 8:12], in_=sm8.ap[:, 4:8], func=AF.Exp, scale=-0.5), r=[sm8], w=[sm8]); yield
                for i in range(4):
                    ACT(lambda e, i=i: e.activation(out=on4.ap[:, i, :], in_=bO.ap[:, i * 128:(i + 1) * 128], func=AF.Copy, scale=sm8.ap[:, 8 + i:9 + i]),
                        r=[bO, sm8], w=[on4]); yield
                MM([lambda e, i=i: e.transpose(out=bT.ap[:, i * 128:(i + 1) * 128], in_=on4.ap[:, i, :], identity=identb.ap) for i in range(4)],
                   r=[on4, identb], w=[bT]); yield
                DVE(lambda e: e.scalar_tensor_tensor(out=mixT.ap[:, h, tok0:tok0 + 512], in0=bT.ap[:, 0:512], scalar=P.ap[:, h, 10:11], in1=gzg.ap[:, 0:512],
                                                     op0=ALU.mult, op1=ALU.mult), r=[bT, gzg, P], w=[mixT]); yield
                if tok0 == 1536:
                    DMA(lambda e: e.dma_start(out=ohp[h], in_=S4.ap[:, 3, :]), r=[S4], out=True); yield
            else:
                DMA(lambda e: e.dma_start(out=Sin.ap, in_=sh[:, h].rearrange("b k v -> k b v")), w=[Sin]); yield
                ACT(lambda e: e.activation(out=Sbb.ap, in_=Sin.ap, func=AF.Copy), r=[Sin], w=[Sbb]); yield
                DVE(lambda e: e.tensor_tensor(out=qexp.ap, in0=qt.ap[:, 0:64].unsqueeze(1).to_broadcast([128, 16, 64]), in1=colmask.ap, op=ALU.mult),
                    r=[qt, colmask], w=[qexp]); yield
                DVE(lambda e: e.tensor_tensor(out=kexp.ap, in0=kdtok.ap[0:64, 0, :].unsqueeze(1).to_broadcast([64, 16, 128]),
                                              in1=rowmask.ap.unsqueeze(2).to_broadcast([64, 16, 128]), op=ALU.mult), r=[kdtok, rowmask], w=[kexp]); yield
                MM([lambda e: e.matmul(bS.ap[0:64, 0:64], lhsT=kt.ap[:, 0:64], rhs=qt.ap[:, 0:64], start=True, stop=True)], r=[kt, qt], w=[bS]); yield
                DVE(lambda e: e.tensor_tensor(out=ATm.ap[0:64, 0:64], in0=bS.ap[0:64, 0:64], in1=smask.ap, op=ALU.mult), r=[bS, smask], w=[ATm]); yield
                fns = [lambda e: e.matmul(bO.ap[0:64, 0:128], lhsT=ATm.ap[0:64, 0:64], rhs=vtok.ap[0:64, 0, :], start=True, stop=False)]
                for b in range(16):
                    fns.append(lambda e, b=b: e.matmul(bO.ap[0:64, 0:128], lhsT=qexp.ap[:, b, :], rhs=Sbb.ap[:, b, :], start=False, stop=(b == 15)))
                MM(fns, r=[ATm, vtok, qexp, Sbb], w=[bO]); yield
                for rnd in range(4):
                    MM([lambda e, b=b: e.matmul(bU.ap[:, (b % 4) * 128:(b % 4 + 1) * 128], lhsT=kexp.ap[:, b, :], rhs=vtok.ap[0:64, 0, :], start=True, stop=True)
                        for b in range(rnd * 4, rnd * 4 + 4)], r=[kexp, vtok], w=[bU]); yield
                    for b in range(rnd * 4, rnd * 4 + 4):
                        DVE(lambda e, b=b: e.scalar_tensor_tensor(out=Sin.ap[:, b, :], in0=Sin.ap[:, b, :], scalar=eGL.ap[:, b:b + 1],
                                                                  in1=bU.ap[:, (b % 4) * 128:(b % 4 + 1) * 128], op0=ALU.mult, op1=ALU.add),
                            r=[Sin, eGL, bU], w=[Sin]); yield
                DMA(lambda e: e.dma_start(out=ohs[:, h].rearrange("b k v -> k b v"), in_=Sin.ap), r=[Sin], out=True); yield
                yield from o_epilogue(h, 64, 2048, gzg)

        h_units = [(h, tok0, T) for h in range(8) for (tok0, T) in GROUPS]
        S.begin_window()
        for n, u in enumerate(h_units):
            interleave(h_stage1(*u, GO[n % 2]))
            interleave(h_stage2(*u, GO[n % 2]))
        S.end_window()

        S.barrier()
        AR.reset()
        wg = AR.alloc(128, (8, 8), BF16)
        bg_s = AR.alloc(4, (2,), F32); nbg = AR.alloc(4, (2,), F32)
        g_ig = AR.alloc(4, (2112,), F32); g_L = AR.alloc(4, (2112,), F32); g_nb = AR.alloc(4, (2112,), F32)
        cmax = AR.alloc(4, (32,), F32); nbL = AR.alloc(4, (32,), F32); bLn = AR.alloc(4, (32,), F32); d1 = AR.alloc(4, (32,), F32)
        mP = AR.alloc(4, (32,), F32); rall = AR.alloc(4, (32,), F32); mprev = AR.alloc(4, (32,), F32); gall = AR.alloc(4, (32,), F32)
        gdiag = AR.alloc(4, (32, 4), F32)
        DMAC(lambda e: e.dma_start(out=wg.ap.rearrange("p a b -> p (a b)"), in_=w_gd), w=[wg])
        DMA(lambda e: e.dma_start(out=bg_s.ap, in_=bgd), w=[bg_s])
        DVE(lambda e: e.tensor_scalar(out=nbg.ap, in0=bg_s.ap, scalar1=-1.0, scalar2=None, op0=ALU.mult), r=[bg_s], w=[nbg])
        POOL(lambda e: e.memset(mprev.ap, 0.0), w=[mprev])
        DMA(lambda e: e.dma_start(out=mprev.ap[:, 16:32], in_=sm.rearrange("b h -> h b"), allow_slow_non_contiguous=True), r=[mprev], w=[mprev])
        def gate_group(tok0, T):
            MM([lambda e, k=k: e.matmul(bA.ap[0:4, 0:T], lhsT=wg.ap[:, k, 0:4], rhs=hT.ap[:, k, tok0:tok0 + T], start=(k == 0), stop=(k == 7)) for k in range(8)],
               r=[wg, hT], w=[bA])
            MM([lambda e, k=k: e.matmul(bB.ap[0:4, 0:T], lhsT=wg.ap[:, k, 4:8], rhs=hT.ap[:, k, tok0:tok0 + T], start=(k == 0), stop=(k == 7)) for k in range(8)],
               r=[wg, hT], w=[bB])
            ACT(lambda e: e.activation(out=g_ig.ap[:, tok0:tok0 + T], in_=bA.ap[0:4, 0:T], func=AF.Identity, bias=bg_s.ap[:, 0:1]), r=[bA, bg_s], w=[g_ig])
            ACT(lambda e: e.activation(out=g_L.ap[:, tok0:tok0 + T], in_=bB.ap[0:4, 0:T], func=AF.Exp, scale=-1.0, bias=nbg.ap[:, 1:2]), r=[bB, nbg], w=[g_L])
            ACT(lambda e: e.activation(out=g_L.ap[:, tok0:tok0 + T], in_=g_L.ap[:, tok0:tok0 + T], func=AF.Ln, bias=1.0), r=[g_L], w=[g_L])
            mk = msk.ap[0:4, 512:576] if T == 64 else msk.ap[0:4, 0:512]
            DVE(lambda e, mk=mk: e.tensor_tensor_scan(out=g_nb.ap[:, tok0:tok0 + T], data0=mk, data1=g_L.ap[:, tok0:tok0 + T], initial=0.0,
                                                      op0=ALU.mult, op1=ALU.add), r=[msk, g_L], w=[g_nb])
        for (tok0, T) in GROUPS:
            gate_group(tok0, T)
        DVE(lambda e: e.tensor_tensor(out=g_ig.ap, in0=g_ig.ap, in1=g_nb.ap, op=ALU.add), r=[g_ig, g_nb], w=[g_ig])
        DVE(lambda e: e.reduce_max(out=cmax.ap[:, 0:16], in_=g_ig.ap[:, 0:2048].rearrange("p (c l) -> p c l", l=128), axis=AX.X), r=[g_ig], w=[cmax])
        DVE(lambda e: e.reduce_max(out=cmax.ap[:, 16:32], in_=g_ig.ap[:, 2048:2112].rearrange("p (c l) -> p c l", l=4), axis=AX.X), r=[g_ig, cmax], w=[cmax])
        DVE(lambda e: e.tensor_copy(out=nbL.ap[:, 0:16], in_=g_nb.ap[:, 0:2048].rearrange("p (c l) -> p c l", l=128)[:, :, 127]), r=[g_nb], w=[nbL])
        DVE(lambda e: e.tensor_copy(out=nbL.ap[:, 16:32], in_=g_nb.ap[:, 2048:2112].rearrange("p (c l) -> p c l", l=4)[:, :, 3]), r=[g_nb, nbL], w=[nbL])
        DVE(lambda e: e.tensor_scalar(out=bLn.ap, in0=nbL.ap, scalar1=-1.0, scalar2=None, op0=ALU.mult), r=[nbL], w=[bLn])
        DVE(lambda e: e.tensor_tensor(out=d1.ap, in0=cmax.ap, in1=nbL.ap, op=ALU.subtract), r=[cmax, nbL], w=[d1])
        DVE(lambda e: e.tensor_tensor_scan(out=mP.ap[:, 0:16], data0=bLn.ap[:, 0:16], data1=d1.ap[:, 0:16], initial=0.0, op0=ALU.add, op1=ALU.max),
            r=[bLn, d1], w=[mP])
        DVE(lambda e: e.tensor_copy(out=mprev.ap[:, 1:16], in_=mP.ap[:, 0:15]), r=[mP, mprev], w=[mprev])
        DVE(lambda e: e.tensor_tensor(out=rall.ap[:, 0:16], in0=mP.ap[:, 0:16], in1=nbL.ap[:, 0:16], op=ALU.add), r=[mP, nbL], w=[rall])
        DVE(lambda e: e.tensor_tensor(out=rall.ap[:, 16:32], in0=mprev.ap[:, 16:32], in1=cmax.ap[:, 16:32], op=ALU.max), r=[mprev, cmax, rall], w=[rall])
        DVE(lambda e: e.tensor_tensor(out=mP.ap[:, 16:32], in0=rall.ap[:, 16:32], in1=nbL.ap[:, 16:32], op=ALU.subtract), r=[rall, nbL, mP], w=[mP])
        DMA(lambda e: e.dma_start(out=omp, in_=mP.ap[:, 15:16]), r=[mP], out=True)
        DMA(lambda e: e.dma_start(out=oms.rearrange("b h -> h b"), in_=mP.ap[:, 16:32], allow_slow_non_contiguous=True), r=[mP], out=True)
        DVE(lambda e: e.tensor_tensor(out=gall.ap, in0=mprev.ap, in1=rall.ap, op=ALU.subtract), r=[mprev, rall], w=[gall])
        ACT(lambda e: e.activation(out=gall.ap, in_=gall.ap, func=AF.Exp), r=[gall], w=[gall])
        for (dst, lo, hi, L) in ((g_ig, 0, 2048, 128), (g_ig, 2048, 2112, 4), (g_nb, 0, 2048, 128), (g_nb, 2048, 2112, 4)):
            n = (hi - lo) // L
            r0 = 0 if lo == 0 else 16
            DVE(lambda e, dst=dst, lo=lo, hi=hi, L=L, n=n, r0=r0: e.tensor_tensor(
                out=dst.ap[:, lo:hi].rearrange("p (c l) -> p c l", l=L), in0=dst.ap[:, lo:hi].rearrange("p (c l) -> p c l", l=L),
                in1=rall.ap[:, r0:r0 + n].unsqueeze(2).to_broadcast([4, n, L]), op=ALU.subtract), r=[dst, rall], w=[dst])
        DVE(lambda e: e.tensor_scalar(out=g_ig.ap, in0=g_ig.ap, scalar1=-LN16, scalar2=None, op0=ALU.add), r=[g_ig], w=[g_ig])
        ACT(lambda e: e.activation(out=g_ig.ap, in_=g_ig.ap, func=AF.Exp), r=[g_ig], w=[g_ig])
        ACT(lambda e: e.activation(out=g_nb.ap, in_=g_nb.ap, func=AF.Exp), r=[g_nb], w=[g_nb])
        fns = []
        for i in range(17):
            rows = 128 if i < 16 else 64
            fns.append(lambda e, i=i, rows=rows: e.transpose(out=bS.ap[0:rows, i * 4:(i + 1) * 4], in_=g_ig.ap[:, i * 128:i * 128 + rows], identity=ident32.ap[0:4, 0:4]))
            fns.append(lambda e, i=i, rows=rows: e.transpose(out=bS.ap[0:rows, 128 + i * 4:128 + (i + 1) * 4], in_=g_nb.ap[:, i * 128:i * 128 + rows], identity=ident32.ap[0:4, 0:4]))
        MM(fns, r=[g_ig, g_nb, ident32], w=[bS])
        DVE(lambda e: e.tensor_copy(out=ectok.ap.rearrange("p a b -> p (a b)"), in_=bS.ap[:, 0:68]), r=[bS], w=[ectok])
        DVE(lambda e: e.tensor_copy(out=thrtok.ap.rearrange("p a b -> p (a b)"), in_=bS.ap[:, 128:196]), r=[bS], w=[thrtok])
        DVE(lambda e: e.tensor_tensor(out=gdiag.ap, in0=gall.ap.unsqueeze(2).to_broadcast([4, 32, 4]),
                                      in1=ident32.ap[0:4, 0:4].unsqueeze(1).to_broadcast([4, 32, 4]), op=ALU.mult), r=[gall, ident32], w=[gdiag])
        MM([lambda e: e.matmul(bO.ap[:, 0:128], lhsT=ones32.ap[0:4, :], rhs=gdiag.ap.rearrange("p a b -> p (a b)"), start=True, stop=True)],
           r=[ones32, gdiag], w=[bO])
        DVE(lambda e: e.tensor_copy(out=gbc.ap.rearrange("p a b -> p (a b)"), in_=bO.ap[:, 0:128]), r=[bO], w=[gbc])

        S.barrier()
        AR.reset()
        muX = AR.alloc(128, (2, 520), BF16)
        extS = AR.alloc(128, (2, 16, 7), BF16)
        shS = AR.alloc(128, (2, 4, 64), BF16)
        scr = AR.alloc(128, (1024,), F32)
        m_ez = Tl(scr.ap[:, 0:512]); m_ez.r = scr.r
        m_sg = Tl(scr.ap[:, 512:1024]); m_sg.r = scr.r
        xcT = AR.alloc(128, (2, 512), BF16)
        mu32 = AR.alloc(128, (2, 128), F32); mutok = AR.alloc(128, (256,), F32)
        bufT = AR.alloc(128, (8, 48), BF16)
        MO = []
        for _ in range(2):
            MO.append(dict(smz=AR.alloc(128, (2, 512), BF16), sxc=AR.alloc(128, (2, 512), BF16), qT=AR.alloc(128, (2, 512), BF16),
                           kT=AR.alloc(128, (2, 512), BF16), kp4=AR.alloc(128, (4, 256), BF16), vext4=AR.alloc(128, (4, 258), BF16),
                           so4=AR.alloc(128, (4, 256), BF16)))
        STm4 = AR.alloc(128, (4, 128), BF16); Cgb = AR.alloc(128, (2, 257), BF16); Cst = AR.alloc(128, (2, 257), F32)
        big = AR.alloc(128, (4056,), F32)
        def bview(off, rows, free, dt):
            nel = int(np.prod(free)); nw = (nel + 1) // 2 if dt == BF16 else nel
            v = big.ap[0:rows, off:off + nw]
            if dt == BF16:
                v = v.bitcast(BF16)
            v = v[:, 0:nel]
            if len(free) == 2:
                v = v.rearrange("p (a b) -> p a b", a=free[0])
            elif len(free) == 3:
                v = v.rearrange("p (a b c) -> p a b c", a=free[0], b=free[1])
            return Tl(v)

        def alias_sync(src, dst):
            evs = []
            for t in src:
                if t.r.last_write is not None:
                    evs.append(t.r.last_write)
                evs.extend(t.r.reads)
            for t in dst:
                t.r.reads = list(t.r.reads) + evs
        ho4 = bview(0, 128, (4, 256), F32); sq = bview(1024, 128, (1024,), F32); hn4 = bview(2048, 128, (4, 256), BF16)
        t1 = bview(2560, 128, (512,), F32)
        Cin = [bview(i * 516, 128, (1, 2, 257), F32) for i in range(4)]
        Cgq2 = [bview(2064, 128, (1, 2, 257), BF16), bview(3560, 128, (1, 2, 257), BF16)]
        qx2 = [bview(2328, 128, (2, 1, 64), BF16), bview(3824, 128, (2, 1, 64), BF16)]
        kx2 = [bview(2392, 64, (1, 256), BF16), bview(3888, 64, (1, 256), BF16)]
        gnold = bview(4016, 128, (2, 16), F32)
        ntok = bview(2520, 16, (256,), F32); nout = bview(2776, 32, (128,), F32); t1s = bview(2904, 128, (128,), F32)
        ho_s = bview(3032, 128, (256,), F32); hn_s = bview(3288, 128, (256,), BF16); ncol = bview(3416, 128, (2, 16), F32)
        junk3 = bview(3448, 128, (208,), BF16)
        bufrow = bview(0, 48, (1024,), F32)
        prompt_views = [ho4, sq, hn4, t1]
        sample_views = Cin + Cgq2 + qx2 + kx2 + [gnold, ntok, nout, t1s, ho_s, hn_s, ncol, junk3, bufrow]
        wm = Tl(wbuf.ap.rearrange("p (k s c) -> p k s c", k=8, s=3))
        wm.r = wbuf.r
        for O in MO:
            POOL(lambda e, O=O: e.memset(O["vext4"].ap, 1.0), w=[O["vext4"]])
        DMA(lambda e: e.dma_start(out=bufrow.ap, in_=scv), w=[bufrow])
        for half in range(2):
            MM([lambda e, k=k: e.transpose(out=bS.ap[:, (k % 4) * 48:(k % 4) * 48 + 48], in_=bufrow.ap[:, k * 128:(k + 1) * 128], identity=ident32.ap[0:48, 0:48])
                for k in range(half * 4, half * 4 + 4)], r=[bufrow, ident32], w=[bS])
            DVE(lambda e, half=half: e.tensor_copy(out=bufT.ap[:, half * 4:half * 4 + 4, :], in_=bS.ap[:, 0:192].rearrange("p (a b) -> p a b", a=4)), r=[bS], w=[bufT])

        def m_stage1(hm, gi, tok0, T, O):
            kc0 = 2 * hm
            sample = (T == 64)
            rows = 64 if sample else 128
            nch = 1 if sample else 4
            smz, sxc, qT, kT, kp4, vext4, so4 = O["smz"], O["sxc"], O["qT"], O["kT"], O["kp4"], O["vext4"], O["so4"]
            if gi == 0:
                DMAC(lambda e: e.dma_start(out=wbuf.ap, in_=w_md[hm]), w=[wm])
                POOL(lambda e: e.memset(muX.ap[:, :, 0:3], 0.0), w=[muX]); yield
            for kc, bank in ((0, bA), (1, bB)):
                MM([lambda e, k=k, kc=kc, bank=bank: e.matmul(bank.ap[:, 0:T], lhsT=wm.ap[:, k, 0, kc * 128:(kc + 1) * 128], rhs=hT.ap[:, k, tok0:tok0 + T],
                                                              start=(k == 0), stop=(k == 7)) for k in range(8)], r=[wm, hT], w=[bank]); yield
            for kc, bank in ((0, bC), (1, bD)):
                MM([lambda e, k=k, kc=kc, bank=bank: e.matmul(bank.ap[:, 0:T], lhsT=wm.ap[:, k, 1, kc * 128:(kc + 1) * 128], rhs=hT.ap[:, k, tok0:tok0 + T],
                                                              start=(k == 0), stop=(k == 7)) for k in range(8)], r=[wm, hT], w=[bank]); yield
            for kc, bank in ((0, bA), (1, bB)):
                ACT(lambda e, kc=kc, bank=bank: e.activation(out=muX.ap[:, kc, 3:3 + T], in_=bank.ap[:, 0:T], func=AF.Identity, bias=P.ap[:, kc0 + kc, 4:5]),
                    r=[bank, P], w=[muX]); yield
                if sample:
                    ACT(lambda e, kc=kc, bank=bank: e.activation(out=extS.ap[:, kc, :, 3:7], in_=bank.ap[:, 0:64].rearrange("p (b t) -> p b t", t=4),
                                                                 func=AF.Identity, bias=P.ap[:, kc0 + kc, 4:5]), r=[bank, P], w=[extS])
                    DVE(lambda e, kc=kc: e.tensor_copy(out=extS.ap[:, kc, :, 0:3], in_=bufT.ap[:, kc0 + kc, :].rearrange("p (b i) -> p b i", i=3)),
                        r=[bufT, extS], w=[extS]); yield
                    for i in range(4):
                        DVE(lambda e, kc=kc, i=i: e.tensor_copy(out=shS.ap[:, kc, i, :].rearrange("p (b t) -> p b t", t=4), in_=extS.ap[:, kc, :, i:i + 4]),
                            r=[extS], w=[shS])
                    yield
                if sample or tok0 == 1536:
                    c_lo = 0 if sample else 384
                    ACT(lambda e, kc=kc, bank=bank, c_lo=c_lo: e.activation(out=mu32.ap[:, kc, 0:rows], in_=bank.ap[:, c_lo:c_lo + rows], func=AF.Identity,
                                                                          bias=P.ap[:, kc0 + kc, 4:5]), r=[bank, P], w=[mu32]); yield
            for kc, bank in ((0, bC), (1, bD)):
                yield from silu_gen(bank, Pn.ap[:, kc0 + kc, 5:6], P.ap[:, kc0 + kc, 5:6], T, m_ez, m_sg, smz.ap[:, kc, 0:T], smz)
            for kc, bank in ((0, bA), (1, bB)):
                if not sample:
                    MM([lambda e, i=i, kc=kc, bank=bank: e.matmul(bank.ap[:, 0:T], lhsT=convd.ap[:, kc0 + kc, i, :], rhs=muX.ap[:, kc, i:i + T],
                                                                  start=(i == 0), stop=(i == 3)) for i in range(4)], r=[convd, muX], w=[bank]); yield
                else:
                    MM([lambda e, i=i, kc=kc, bank=bank: e.matmul(bank.ap[:, 0:64], lhsT=convd.ap[:, kc0 + kc, i, :],
                                                                  rhs=shS.ap[:, kc, i, :], start=(i == 0), stop=(i == 3)) for i in range(4)],
                       r=[convd, shS], w=[bank]); yield
                yield from silu_gen(bank, Pn.ap[:, kc0 + kc, 15:16], P.ap[:, kc0 + kc, 15:16], T, m_ez, m_sg, xcT.ap[:, kc, 0:T], xcT)
                DVE(lambda e, kc=kc: e.tensor_scalar(out=sxc.ap[:, kc, 0:T], in0=xcT.ap[:, kc, 0:T], scalar1=P.ap[:, kc0 + kc, 17:18], scalar2=None, op0=ALU.mult),
                    r=[xcT, P], w=[sxc]); yield
            for kc, bank in ((0, bC), (1, bD)):
                MM([lambda e, kc=kc, bank=bank: e.matmul(bank.ap[:, 0:T], lhsT=wbd.ap[:, 0, kc0 + kc, :], rhs=xcT.ap[:, kc, 0:T], start=True, stop=True)],
                   r=[wbd, xcT], w=[bank]); yield
                DVE(lambda e, kc=kc, bank=bank: e.tensor_copy(out=qT.ap[:, kc, 0:T], in_=bank.ap[:, 0:T]), r=[bank], w=[qT]); yield
            for kc, bank in ((0, bA), (1, bB)):
                MM([lambda e, kc=kc, bank=bank: e.matmul(bank.ap[:, 0:T], lhsT=wbd.ap[:, 1, kc0 + kc, :], rhs=xcT.ap[:, kc, 0:T], start=True, stop=True)],
                   r=[wbd, xcT], w=[bank]); yield
                DVE(lambda e, kc=kc, bank=bank: e.tensor_copy(out=kT.ap[:, kc, 0:T], in_=bank.ap[:, 0:T]), r=[bank], w=[kT]); yield
            npair = (nch + 1) // 2
            for pr in range(npair):
                bank = bC if pr == 0 else bD
                nin = min(2, nch - pr * 2)
                fns = []
                for ii in range(nin):
                    i = pr * 2 + ii
                    for k in range(8):
                        fns.append(lambda e, i=i, ii=ii, k=k, bank=bank: e.matmul(bank.ap[0:rows, ii * 256:(ii + 1) * 256],
                                                                                 lhsT=hT.ap[:, k, tok0 + i * 128:tok0 + i * 128 + rows], rhs=wm.ap[:, k, 2, :],
                                                                                 start=(k == 0), stop=(k == 7)))
                MM(fns, r=[hT, wm], w=[bank]); yield
                DVE(lambda e, pr=pr, nin=nin, bank=bank: e.tensor_tensor(
                    out=scr.ap[0:rows, pr * 512:pr * 512 + nin * 256].rearrange("p (a b) -> p a b", a=nin),
                    in0=bank.ap[0:rows, 0:nin * 256].rearrange("p (a b) -> p a b", a=nin),
                    in1=bc_mo.ap[0:rows, hm * 256:(hm + 1) * 256].unsqueeze(1).to_broadcast([rows, nin, 256]), op=ALU.add), r=[bank, bc_mo], w=[scr]); yield
            W = nch * 256
            ACT(lambda e: e.activation(out=scr.ap[0:rows, 0:W], in_=scr.ap[0:rows, 0:W], func=AF.Exp, scale=-1.0), r=[scr], w=[scr]); yield
            ACT(lambda e: e.activation(out=scr.ap[0:rows, 0:W], in_=scr.ap[0:rows, 0:W], func=AF.Ln, bias=1.0), r=[scr], w=[scr]); yield
            ACT(lambda e: e.activation(out=so4.ap[0:rows, 0:nch, :].rearrange("p a b -> p (a b)"), in_=scr.ap[0:rows, 0:W], func=AF.Exp, scale=-1.0),
                r=[scr], w=[so4]); yield
            for i in range(nch):
                bank = (bA, bB, bC, bD)[i]
                ci = 16 if sample else gi * 4 + i
                c0 = i * 128
                fns = []
                for kc in range(2):
                    fns.append(lambda e, kc=kc, c0=c0, bank=bank: e.matmul(bank.ap[0:rows, kc * 128:(kc + 1) * 128], lhsT=xcT.ap[:, kc, c0:c0 + rows],
                                                                          rhs=wbd.ap[:, 1, kc0 + kc, :], start=True, stop=True))
                for kc in range(2):
                    fns.append(lambda e, kc=kc, c0=c0, bank=bank: e.matmul(bank.ap[0:rows, 256 + kc * 128:256 + (kc + 1) * 128],
                                                                          lhsT=muX.ap[:, kc, 3 + c0:3 + c0 + rows], rhs=wbd.ap[:, 2, kc0 + kc, :], start=True, stop=True))
                MM(fns, r=[xcT, muX, wbd], w=[bank]); yield
                ACT(lambda e, i=i, ci=ci, bank=bank: e.activation(out=kp4.ap[0:rows, i, :], in_=bank.ap[0:rows, 0:256], func=AF.Copy,
                                                                  scale=ectok.ap[0:rows, ci, hm:hm + 1]), r=[bank, ectok], w=[kp4]); yield
                ACT(lambda e, i=i, bank=bank: e.activation(out=vext4.ap[0:rows, i, 0:256], in_=bank.ap[0:rows, 256:512], func=AF.Copy), r=[bank], w=[vext4]); yield
            if sample or tok0 == 1536:
                MM([lambda e, kc=kc: e.transpose(out=bA.ap[0:rows, kc * 128:(kc + 1) * 128], in_=mu32.ap[:, kc, 0:rows], identity=ident32.ap) for kc in range(2)],
                   r=[mu32, ident32], w=[bA]); yield
                DVE(lambda e: e.tensor_copy(out=mutok.ap[0:rows, :], in_=bA.ap[0:rows, 0:256]), r=[bA], w=[mutok]); yield
                if sample:
                    for b in range(16):
                        DMA(lambda e, b=b: e.dma_start(out=ocs[b, :, hm * 256:(hm + 1) * 256], in_=mutok.ap[b * 4 + 1:b * 4 + 4, :]), r=[mutok], out=True)
                else:
                    DMA(lambda e: e.dma_start(out=ocp[:, hm * 256:(hm + 1) * 256], in_=mutok.ap[125:128, :]), r=[mutok], out=True)
                yield
            if not sample:
                DVE(lambda e: e.tensor_copy(out=muX.ap[:, :, 0:3], in_=muX.ap[:, :, 512:515]), r=[muX], w=[muX]); yield

        def m_epilogue_s(hm, O):
            rows, tok0, ci = 64, 2048, 16
            kc0 = 2 * hm
            so = O["so4"]; sxc = O["sxc"]; smz = O["smz"]
            DVE(lambda e: e.tensor_copy(out=sm8.ap[0:rows, 3:4], in_=bO.ap[0:rows, 256:257]), r=[bO], w=[sm8]); yield
            DVE(lambda e: e.scalar_tensor_tensor(out=sm8.ap[0:rows, 4:5], in0=sm8.ap[0:rows, 3:4], scalar=-1.0, in1=sm8.ap[0:rows, 3:4],
                                                 op0=ALU.mult, op1=ALU.max), r=[sm8], w=[sm8]); yield
            DVE(lambda e: e.tensor_tensor(out=sm8.ap[0:rows, 5:6], in0=sm8.ap[0:rows, 4:5], in1=thrtok.ap[0:rows, ci, hm:hm + 1], op=ALU.max),
                r=[sm8, thrtok], w=[sm8]); yield
            DVE(lambda e: e.reciprocal(out=sm8.ap[0:rows, 6:7], in_=sm8.ap[0:rows, 5:6]), r=[sm8], w=[sm8]); yield
            DVE(lambda e: e.scalar_tensor_tensor(out=ho_s.ap[0:rows, :], in0=bO.ap[0:rows, 0:256], scalar=sm8.ap[0:rows, 6:7], in1=so.ap[0:rows, 0, :],
                                                 op0=ALU.mult, op1=ALU.mult), r=[bO, sm8, so], w=[ho_s]); yield
            ACT(lambda e: e.activation(out=junk3.ap[0:rows, 0:256] if False else hn_s.ap[0:rows, :], in_=ho_s.ap[0:rows, :], func=AF.Square, scale=1.0 / 16.0,
                                       accum_out=sm8.ap[0:rows, 7:8]), r=[ho_s], w=[hn_s, sm8]); yield
            ACT(lambda e: e.activation(out=sm8.ap[0:rows, 8:9], in_=sm8.ap[0:rows, 7:8], func=AF.Ln, bias=EPS), r=[sm8], w=[sm8]); yield
            ACT(lambda e: e.activation(out=sm8.ap[0:rows, 9:10], in_=sm8.ap[0:rows, 8:9], func=AF.Exp, scale=-0.5), r=[sm8], w=[sm8]); yield
            ACT(lambda e: e.activation(out=hn_s.ap[0:rows, :], in_=ho_s.ap[0:rows, :], func=AF.Copy, scale=sm8.ap[0:rows, 9:10]), r=[ho_s, sm8], w=[hn_s]); yield
            MM([lambda e, kc=kc: e.transpose(out=bT.ap[:, kc * 128:kc * 128 + rows], in_=hn_s.ap[0:rows, kc * 128:(kc + 1) * 128], identity=identb.ap[0:rows, 0:rows])
                for kc in range(2)], r=[hn_s, identb], w=[bT]); yield
            for kc in range(2):
                DVE(lambda e, kc=kc: e.scalar_tensor_tensor(out=t1s.ap[:, 0:rows], in0=bT.ap[:, kc * 128:kc * 128 + rows], scalar=P.ap[:, kc0 + kc, 16:17],
                                                            in1=sxc.ap[:, kc, 0:rows], op0=ALU.mult, op1=ALU.add), r=[bT, P, sxc], w=[t1s]); yield
                DVE(lambda e, kc=kc: e.tensor_tensor(out=mixT.ap[:, 8 + kc0 + kc, tok0:tok0 + rows], in0=t1s.ap[:, 0:rows], in1=smz.ap[:, kc, 0:rows],
                                                     op=ALU.mult), r=[t1s, smz], w=[mixT]); yield

        def m_stage2(hm, gi, tok0, T, O):
            kc0 = 2 * hm
            sample = (T == 64)
            smz, sxc, qT, kT, kp4, vext4, so4 = O["smz"], O["sxc"], O["qT"], O["kT"], O["kp4"], O["vext4"], O["so4"]
            if gi == 0:
                S.alias(sample_views, prompt_views)
                POOL(lambda e: e.memset(Cst.ap, 0.0), w=[Cst]); yield
            if sample:
                S.alias(prompt_views, sample_views)
            if not sample:
                ci0 = gi * 4
                fns = []
                for i in range(4):
                    for kc in range(2):
                        fns.append(lambda e, i=i, kc=kc: e.matmul(bS.ap[:, i * 128:(i + 1) * 128], lhsT=kT.ap[:, kc, i * 128:(i + 1) * 128],
                                                                  rhs=qT.ap[:, kc, i * 128:(i + 1) * 128], start=(kc == 0), stop=(kc == 1)))
                MM(fns, r=[kT, qT], w=[bS]); yield
                for i in range(4):
                    DVE(lambda e, i=i: e.scalar_tensor_tensor(out=STm4.ap[:, i, :], in0=bS.ap[:, i * 128:(i + 1) * 128], scalar=ectok.ap[:, ci0 + i, hm:hm + 1],
                                                              in1=cmask.ap, op0=ALU.mult, op1=ALU.mult), r=[bS, ectok, cmask], w=[STm4]); yield
                for i in range(4):
                    ci = ci0 + i
                    c0 = i * 128
                    bN = bO if i % 2 == 0 else bS
                    ACT(lambda e, ci=ci: e.activation(out=Cgb.ap, in_=Cst.ap, func=AF.Copy, scale=gbc.ap[:, ci, hm:hm + 1]), r=[Cst, gbc], w=[Cgb]); yield
                    MM([lambda e, i=i, bN=bN: e.matmul(bN.ap[:, 0:257], lhsT=STm4.ap[:, i, :], rhs=vext4.ap[:, i, 0:257], start=True, stop=False)] +
                       [lambda e, kc=kc, c0=c0, bN=bN: e.matmul(bN.ap[:, 0:257], lhsT=qT.ap[:, kc, c0:c0 + 128], rhs=Cgb.ap[:, kc, :], start=False, stop=(kc == 1))
                        for kc in range(2)], r=[STm4, vext4, qT, Cgb], w=[bN]); yield
                    MM([lambda e, i=i, kc=kc: e.matmul(bU.ap[:, kc * 256:(kc + 1) * 256], lhsT=kp4.ap[:, i, kc * 128:(kc + 1) * 128], rhs=vext4.ap[:, i, 0:256],
                                                       start=True, stop=True) for kc in range(2)] +
                       [lambda e, i=i, kc=kc, bN=bN: e.matmul(bN.ap[:, 300 + kc:301 + kc], lhsT=kp4.ap[:, i, kc * 128:(kc + 1) * 128], rhs=vext4.ap[:, i, 256:257],
                                                              start=True, stop=True) for kc in range(2)], r=[kp4, vext4], w=[bU, bN]); yield
                    DVE(lambda e, i=i, bN=bN: e.tensor_tensor(out=ho4.ap[:, i, :], in0=bN.ap[:, 0:256], in1=so4.ap[:, i, :], op=ALU.mult), r=[bN, so4], w=[ho4]); yield
                    DVE(lambda e, i=i, bN=bN: e.tensor_copy(out=sm8.ap[:, 12 + i:13 + i], in_=bN.ap[:, 256:257]), r=[bN], w=[sm8]); yield
                    DVE(lambda e, ci=ci: e.scalar_tensor_tensor(out=Cst.ap[:, :, 0:256], in0=Cst.ap[:, :, 0:256], scalar=gbc.ap[:, ci, hm:hm + 1],
                                                                in1=bU.ap.rearrange("p (a b) -> p a b", a=2), op0=ALU.mult, op1=ALU.add), r=[Cst, gbc, bU], w=[Cst]); yield
                    DVE(lambda e, ci=ci, bN=bN: e.scalar_tensor_tensor(out=Cst.ap[:, :, 256], in0=Cst.ap[:, :, 256], scalar=gbc.ap[:, ci, hm:hm + 1],
                                                                       in1=bN.ap[:, 300:302], op0=ALU.mult, op1=ALU.add), r=[Cst, gbc, bN], w=[Cst]); yield
                DVE(lambda e: e.scalar_tensor_tensor(out=sm8.ap[:, 0:4], in0=sm8.ap[:, 12:16], scalar=-1.0, in1=sm8.ap[:, 12:16], op0=ALU.mult, op1=ALU.max),
                    r=[sm8], w=[sm8]); yield
                DVE(lambda e: e.tensor_tensor(out=sm8.ap[:, 4:8], in0=sm8.ap[:, 0:4], in1=thrtok.ap[:, ci0:ci0 + 4, hm], op=ALU.max), r=[sm8, thrtok], w=[sm8]); yield
                DVE(lambda e: e.reciprocal(out=sm8.ap[:, 8:12], in_=sm8.ap[:, 4:8]), r=[sm8], w=[sm8]); yield
                ACT(lambda e: e.activation(out=sq.ap, in_=ho4.ap.rearrange("p a b -> p (a b)"), func=AF.Square, scale=1.0 / 16.0), r=[ho4], w=[sq]); yield
                DVE(lambda e: e.reduce_sum(out=sm8.ap[:, 0:4], in_=sq.ap.rearrange("p (a b) -> p a b", a=4), axis=AX.X), r=[sq, sm8], w=[sm8]); yield
                DVE(lambda e: e.tensor_tensor(out=sm8.ap[:, 4:8], in0=sm8.ap[:, 8:12], in1=sm8.ap[:, 8:12], op=ALU.mult), r=[sm8], w=[sm8]); yield
                DVE(lambda e: e.tensor_tensor(out=sm8.ap[:, 4:8], in0=sm8.ap[:, 4:8], in1=sm8.ap[:, 0:4], op=ALU.mult), r=[sm8], w=[sm8]); yield
                ACT(lambda e: e.activation(out=sm8.ap[:, 4:8], in_=sm8.ap[:, 4:8], func=AF.Ln, bias=EPS), r=[sm8], w=[sm8]); yield
                ACT(lambda e: e.activation(out=sm8.ap[:, 4:8], in_=sm8.ap[:, 4:8], func=AF.Exp, scale=-0.5), r=[sm8], w=[sm8]); yield
                DVE(lambda e: e.tensor_tensor(out=sm8.ap[:, 0:4], in0=sm8.ap[:, 4:8], in1=sm8.ap[:, 8:12], op=ALU.mult), r=[sm8], w=[sm8]); yield
                DVE(lambda e: e.tensor_tensor(out=hn4.ap, in0=ho4.ap, in1=sm8.ap[:, 0:4].unsqueeze(2).to_broadcast([128, 4, 256]), op=ALU.mult),
                    r=[ho4, sm8], w=[hn4]); yield
                MM([lambda e, i=i, kc=kc: e.transpose(out=bT.ap[:, kc * 512 + i * 128:kc * 512 + (i + 1) * 128], in_=hn4.ap[:, i, kc * 128:(kc + 1) * 128],
                                                      identity=identb.ap) for i in range(4) for kc in range(2)], r=[hn4, identb], w=[bT]); yield
                for kc in range(2):
                    DVE(lambda e, kc=kc: e.scalar_tensor_tensor(out=t1.ap, in0=bT.ap[:, kc * 512:(kc + 1) * 512], scalar=P.ap[:, kc0 + kc, 16:17],
                                                                in1=sxc.ap[:, kc, :], op0=ALU.mult, op1=ALU.add), r=[bT, P, sxc], w=[t1]); yield
                    DVE(lambda e, kc=kc: e.tensor_tensor(out=mixT.ap[:, 8 + kc0 + kc, tok0:tok0 + 512], in0=t1.ap, in1=smz.ap[:, kc, :], op=ALU.mult),
                        r=[t1, smz], w=[mixT]); yield
                if gi == 3:
                    DMA(lambda e: e.dma_start(out=oCp[hm].rearrange("(kc p) v -> p kc v", p=128), in_=Cst.ap[:, :, 0:256]), r=[Cst], out=True)
                    DMA(lambda e: e.dma_start(out=onp[hm].rearrange("(kc p) -> p kc", p=128), in_=Cst.ap[:, :, 256], allow_slow_non_contiguous=True),
                        r=[Cst], out=True); yield
            else:
                rows = 64
                MM([lambda e, kc=kc: e.matmul(bS.ap[0:64, 0:64], lhsT=kT.ap[:, kc, 0:64], rhs=qT.ap[:, kc, 0:64], start=(kc == 0), stop=(kc == 1))
                    for kc in range(2)], r=[kT, qT], w=[bS]); yield
                DVE(lambda e: e.scalar_tensor_tensor(out=STm4.ap[0:64, 0, 0:64], in0=bS.ap[0:64, 0:64], scalar=ectok.ap[0:64, 16, hm:hm + 1],
                                                     in1=smask.ap, op0=ALU.mult, op1=ALU.mult), r=[bS, ectok, smask], w=[STm4]); yield
                DMA(lambda e: e.dma_start(out=ntok.ap, in_=sn[:, hm, :]), w=[ntok]); yield
                MM([lambda e, kc=kc: e.transpose(out=bS.ap[:, 256 + kc * 16:256 + (kc + 1) * 16], in_=ntok.ap[:, kc * 128:(kc + 1) * 128], identity=ident32.ap[0:16, 0:16])
                    for kc in range(2)], r=[ntok, ident32, STm4], w=[bS]); yield
                first = True
                NS = 4
                nold = bS.ap[:, 256:288].rearrange("p (kc b) -> p kc b", kc=2)
                DVE(lambda e: e.tensor_tensor(out=gnold.ap, in0=nold, in1=gbc.ap[:, 16:32, hm].unsqueeze(1).to_broadcast([128, 2, 16]), op=ALU.mult),
                    r=[bS, gbc], w=[gnold]); yield
                MM([lambda e, kc=kc: e.matmul(bS.ap[:, 320 + kc * 16:320 + (kc + 1) * 16], lhsT=kp4.ap[0:64, 0, kc * 128:(kc + 1) * 128], rhs=rowmask.ap,
                                              start=True, stop=True) for kc in range(2)], r=[kp4, rowmask, gnold], w=[bS]); yield
                DVE(lambda e: e.tensor_tensor(out=ncol.ap, in0=bS.ap[:, 320:352].rearrange("p (kc b) -> p kc b", kc=2), in1=gnold.ap, op=ALU.add),
                    r=[bS, gnold], w=[ncol]); yield

                def load_round(b):
                    C_l = Cin[b % NS]
                    DMA(lambda e, b=b, C_l=C_l: e.dma_start(out=C_l.ap[:, 0, :, 0:256], in_=sC[b, hm].rearrange("(kc p) v -> p kc v", p=128)), w=[C_l])
                for b in range(NS - 1):
                    load_round(b)
                yield
                for b in range(16):
                    C_t = Cin[b % NS]
                    Cg_t = Cgq2[b % 2]; qx_t = qx2[b % 2]; kx_t = kx2[b % 2]
                    Ub = bU if b % 2 == 0 else bTf
                    if b + NS - 1 < 16:
                        load_round(b + NS - 1)
                    ACT(lambda e, b=b, C_t=C_t, Cg_t=Cg_t: e.activation(out=Cg_t.ap[:, 0, :, 0:256], in_=C_t.ap[:, 0, :, 0:256], func=AF.Copy,
                                                                        scale=gbc.ap[:, 16 + b, hm:hm + 1]), r=[C_t, gbc], w=[Cg_t]); yield
                    DVE(lambda e, b=b, Cg_t=Cg_t: e.tensor_copy(out=Cg_t.ap[:, 0, :, 256], in_=gnold.ap[:, :, b]), r=[gnold, Cg_t], w=[Cg_t]); yield
                    DVE(lambda e, b=b, qx_t=qx_t: e.tensor_tensor(out=qx_t.ap[:, :, 0, :], in0=qT.ap[:, :, 0:64],
                                                                  in1=colmask.ap[:, b:b + 1, :].to_broadcast([128, 2, 64]), op=ALU.mult), r=[qT, colmask], w=[qx_t]); yield
                    DVE(lambda e, b=b, kx_t=kx_t: e.tensor_tensor(out=kx_t.ap[:, 0, :], in0=kp4.ap[0:64, 0, :],
                                                                  in1=rowmask.ap[:, b:b + 1].to_broadcast([64, 256]), op=ALU.mult), r=[kp4, rowmask], w=[kx_t]); yield
                    fns = []
                    if first:
                        fns.append(lambda e: e.matmul(bO.ap[0:64, 0:257], lhsT=STm4.ap[0:64, 0, 0:64], rhs=vext4.ap[0:64, 0, 0:257], start=True, stop=False))
                        first = False
                    for kc in range(2):
                        last = (b == 15 and kc == 1)
                        fns.append(lambda e, kc=kc, last=last, qx_t=qx_t, Cg_t=Cg_t: e.matmul(bO.ap[0:64, 0:257], lhsT=qx_t.ap[:, kc, 0, :], rhs=Cg_t.ap[:, 0, kc, :],
                                                                                             start=False, stop=last))
                    MM(fns, r=[STm4, vext4, qx_t, Cg_t], w=[bO]); yield
                    MM([lambda e, kc=kc, kx_t=kx_t, Ub=Ub: e.matmul(Ub.ap[:, kc * 256:(kc + 1) * 256], lhsT=kx_t.ap[:, 0, kc * 128:(kc + 1) * 128],
                                                                    rhs=vext4.ap[0:64, 0, 0:256], start=True, stop=True) for kc in range(2)], r=[kx_t, vext4], w=[Ub]); yield
                    DVE(lambda e, b=b, C_t=C_t, Ub=Ub: e.scalar_tensor_tensor(out=C_t.ap[:, 0, :, 0:256], in0=C_t.ap[:, 0, :, 0:256], scalar=gbc.ap[:, 16 + b, hm:hm + 1],
                                                                             in1=Ub.ap.rearrange("p (a b) -> p a b", a=2), op0=ALU.mult, op1=ALU.add),
                        r=[C_t, gbc, Ub], w=[C_t]); yield
                    DMA(lambda e, b=b, C_t=C_t: e.dma_start(out=oCs[b, hm].rearrange("(kc p) v -> p kc v", p=128), in_=C_t.ap[:, 0, :, 0:256]), r=[C_t], out=True); yield
                MM([lambda e: e.transpose(out=bU.ap[0:32, 0:128], in_=ncol.ap.rearrange("p kc b -> p (kc b)"), identity=ident32.ap)], r=[ncol, ident32], w=[bU]); yield
                DVE(lambda e: e.tensor_copy(out=nout.ap[0:32, :], in_=bU.ap[0:32, 0:128]), r=[bU], w=[nout]); yield
                for kc in range(2):
                    DMA(lambda e, kc=kc: e.dma_start(out=ons[:, hm, kc * 128:(kc + 1) * 128], in_=nout.ap[kc * 16:(kc + 1) * 16, :]), r=[nout], out=True)
                yield
                yield from m_epilogue_s(hm, O)

        m_units = [(hm, gi, tok0, T) for hm in range(4) for gi, (tok0, T) in enumerate(GROUPS)]
        S.begin_window()
        for n, u in enumerate(m_units):
            interleave(m_stage1(*u, MO[n % 2]))
            interleave(m_stage2(*u, MO[n % 2]))
        S.end_window()

        S.barrier()
        AR.reset()
        wo = Tl(hT.ap.rearrange("p a b -> p (a b)")[:, 0:16384].rearrange("p (k c) -> p k c", k=16))
        wo.r = hT.r
        for k4 in range(4):
            DMAC(lambda e, k4=k4: e.dma_start(out=wo.ap[:, k4 * 4:(k4 + 1) * 4, :], in_=w_out_v[:, k4 * 4:(k4 + 1) * 4, :]), w=[wo])
        S.begin_window()
        gate_tok = AR.alloc(17, (1024,), F32)
        selg = AR.alloc(17, (192,), F32)
        gbp = AR.alloc(128, (1024,), F32); gbs = AR.alloc(64, (1024,), F32); fgb = AR.alloc(128, (1024,), F32)
        prm2 = AR.alloc(32, (1024,), F32); sel2 = AR.alloc(32, (3, 128), F32)
        x4 = [AR.alloc(128, (1024,), F32) for _ in range(2)]
        res4 = [AR.alloc(128, (1024,), F32) for _ in range(2)]
        y4 = [AR.alloc(128, (1024,), F32) for _ in range(2)]
        junk4 = AR.alloc(128, (1024,), BF16)
        st4 = AR.alloc(128, (8,), F32)
        DMA(lambda e: e.dma_start(out=selg.ap, in_=selg_d), w=[selg])
        DMA(lambda e: e.dma_start(out=prm2.ap, in_=prm), w=[prm2])
        DMA(lambda e: e.dma_start(out=sel2.ap, in_=sel_d), w=[sel2])
        for half, bank in ((0, bA), (1, bB)):
            MM([lambda e, k=k, bank=bank: e.transpose(out=bank.ap[0:17, (k % 4) * 128:(k % 4 + 1) * 128], in_=modT.ap[:, 16 + k, :], identity=ident32.ap)
                for k in range(half * 4, half * 4 + 4)], r=[modT, ident32], w=[bank])
            DVE(lambda e, half=half, bank=bank: e.tensor_copy(out=gate_tok.ap[:, half * 512:(half + 1) * 512], in_=bank.ap[0:17, :]), r=[bank], w=[gate_tok])
        for half, bank in ((0, bA), (1, bB)):
            MM([lambda e, half=half, bank=bank: e.matmul(bank.ap, lhsT=selg.ap[:, 0:128], rhs=gate_tok.ap[:, half * 512:(half + 1) * 512], start=True, stop=True)],
               r=[selg, gate_tok], w=[bank])
            DVE(lambda e, half=half, bank=bank: e.tensor_copy(out=gbp.ap[:, half * 512:(half + 1) * 512], in_=bank.ap), r=[bank], w=[gbp])
        for half, bank in ((0, bA), (1, bB)):
            MM([lambda e, half=half, bank=bank: e.matmul(bank.ap[0:64, :], lhsT=selg.ap[:, 128:192], rhs=gate_tok.ap[:, half * 512:(half + 1) * 512], start=True, stop=True)],
               r=[selg, gate_tok], w=[bank])
            DVE(lambda e, half=half, bank=bank: e.tensor_copy(out=gbs.ap[:, half * 512:(half + 1) * 512], in_=bank.ap[0:64, :]), r=[bank], w=[gbs])
        for half, bank in ((0, bA), (1, bB)):
            MM([lambda e, half=half, bank=bank: e.matmul(bank.ap, lhsT=sel2.ap[:, 2, :], rhs=prm2.ap[:, half * 512:(half + 1) * 512], start=True, stop=True)],
               r=[sel2, prm2], w=[bank])
            DVE(lambda e, half=half, bank=bank: e.tensor_copy(out=fgb.ap[:, half * 512:(half + 1) * 512], in_=bank.ap), r=[bank], w=[fgb])
        def tile4(i):
            rows = 128 if i < 16 else 64
            x_t = x4[i % 2]; r_t = res4[i % 2]; y_t = y4[i % 2]
            gb = gbp if i < 16 else gbs
            src = xp[i * 128:(i + 1) * 128, :] if i < 16 else xs
            dst = yp[i * 128:(i + 1) * 128, :] if i < 16 else ys
            tk = i * 128
            DMA(lambda e, x_t=x_t, src=src, rows=rows: e.dma_start(out=x_t.ap[0:rows, :], in_=src), w=[x_t])
            banks = (bC, bD) if i % 2 == 0 else (bA, bB)
            for half, bank in enumerate(banks):
                MM([lambda e, k=k, half=half, bank=bank: e.matmul(bank.ap[0:rows, :], lhsT=mixT.ap[:, k, tk:tk + rows], rhs=wo.ap[:, k, half * 512:(half + 1) * 512],
                                                                  start=(k == 0), stop=(k == 15)) for k in range(16)], r=[mixT, wo], w=[bank])
                DVE(lambda e, half=half, bank=bank, r_t=r_t, gb=gb: e.tensor_tensor(out=r_t.ap[0:rows, half * 512:(half + 1) * 512], in0=bank.ap[0:rows, :],
                                                                                   in1=gb.ap[0:rows, half * 512:(half + 1) * 512], op=ALU.mult), r=[bank, gb], w=[r_t])
            DVE(lambda e, r_t=r_t, x_t=x_t: e.tensor_tensor(out=r_t.ap[0:rows, :], in0=r_t.ap[0:rows, :], in1=x_t.ap[0:rows, :], op=ALU.add), r=[r_t, x_t], w=[r_t])
            ACT(lambda e, r_t=r_t: e.activation(out=junk4.ap[0:rows, :], in_=r_t.ap[0:rows, :], func=AF.Square, scale=1.0 / 32.0, accum_out=st4.ap[0:rows, 0:1]),
                r=[r_t], w=[junk4, st4])
            ACT(lambda e: e.activation(out=st4.ap[0:rows, 1:2], in_=st4.ap[0:rows, 0:1], func=AF.Ln, bias=EPS), r=[st4], w=[st4])
            ACT(lambda e: e.activation(out=st4.ap[0:rows, 2:3], in_=st4.ap[0:rows, 1:2], func=AF.Exp, scale=-0.5), r=[st4], w=[st4])
            DVE(lambda e, r_t=r_t, y_t=y_t: e.scalar_tensor_tensor(out=y_t.ap[0:rows, :], in0=r_t.ap[0:rows, :], scalar=st4.ap[0:rows, 2:3], in1=fgb.ap[0:rows, :],
                                                                  op0=ALU.mult, op1=ALU.mult), r=[r_t, st4, fgb], w=[y_t])
            DMA(lambda e, y_t=y_t, dst=dst, rows=rows: e.dma_start(out=dst, in_=y_t.ap[0:rows, :]), r=[y_t], out=True)

        for i in range(17):
            tile4(i)
        S.end_window()

        S.finish()
        with nc.Block() as block:
            S.emit(block)
    return nc


_NC_CACHE = {}


def _consts():
    c = {}
    j = np.arange(128)
    c["cmask"] = (j[:, None] <= j[None, :]).astype(np.float32)
    j = np.arange(64)
    c["smask"] = ((j[:, None] <= j[None, :]) & (j[:, None] // 4 == j[None, :] // 4)).astype(np.float32)
    c["rowmask"] = (j[:, None] // 4 == np.arange(16)[None, :]).astype(np.float32)
    c["colmask"] = (np.arange(16)[:, None] == j[None, :] // 4).astype(np.float32).reshape(1, 1024)
    sel = np.zeros((32, 3, 128), np.float32)
    sel[2, 0, :] = 1.0
    sel[6, 1, :] = 1.0
    sel[18, 2, :] = 1.0
    c["sel"] = sel
    selg = np.zeros((17, 192), np.float32)
    selg[0, 0:128] = 1.0
    for t in range(64):
        selg[1 + t // 4, 128 + t] = 1.0
    c["selg"] = selg
    return c


def kernel(x_prompt, x_sample, c_prompt, c_sample, state_hgrn, state_mlstm_C, state_mlstm_n,
           state_mlstm_m, state_mlstm_conv, w_ada, b_ada, norm_g, w_in, b_in, hgrn_lb_logits,
           hgrn_norm_g, mlstm_conv_w, mlstm_conv_b, mlstm_wq, mlstm_wk, mlstm_wv, mlstm_norm_g,
           mlstm_skip, w_out, final_g):
    f = lambda a: np.ascontiguousarray(np.asarray(a, dtype=np.float32))
    x_prompt, x_sample, c_prompt, c_sample = f(x_prompt), f(x_sample), f(c_prompt), f(c_sample)
    state_hgrn, state_mlstm_C, state_mlstm_n = f(state_hgrn), f(state_mlstm_C), f(state_mlstm_n)
    state_mlstm_m, state_mlstm_conv = f(state_mlstm_m), f(state_mlstm_conv)
    w_ada, b_ada, norm_g, w_in, b_in = f(w_ada), f(b_ada), f(norm_g), f(w_in), f(b_in)
    prm = np.zeros((32, 1024), np.float32)
    prm[0:7] = b_in[0, 0:7168].reshape(7, 1024)
    prm[7] = norm_g[0]
    prm[8:10] = f(hgrn_lb_logits)
    prm[10] = f(hgrn_norm_g)[0]
    prm[11:15] = f(mlstm_conv_w)[0]
    prm[15] = f(mlstm_conv_b)[0]
    prm[16] = f(mlstm_norm_g)[0]
    prm[17] = f(mlstm_skip)[0]
    prm[18] = f(final_g)
    bg = np.ascontiguousarray(b_in[0, 7168:7176].reshape(2, 4).T)
    wbd = np.zeros((128, 3, 8, 128), np.float32)
    for wi, wsrc in enumerate((mlstm_wq, mlstm_wk, mlstm_wv)):
        wv = f(wsrc)[0].reshape(8, 32, 4, 4)
        for g in range(32):
            wbd[g * 4:(g + 1) * 4, wi, :, g * 4:(g + 1) * 4] = wv[:, g].transpose(1, 0, 2)
    consts = _consts()
    wv = w_in[0].reshape(8, 128, 7176)
    w_h = np.empty((8, 128, 8, 4, 128), np.float32)
    for s_i, off in enumerate((0, 1024, 2048, 3072)):
        w_h[:, :, :, s_i, :] = wv[:, :, off:off + 1024].reshape(8, 128, 8, 128).transpose(2, 1, 0, 3)
    w_h = w_h.reshape(8, 128, 4096)
    w_m = np.empty((4, 128, 8, 3, 256), np.float32)
    for s_i in range(3):
        off = 4096 + s_i * 1024
        w_m[:, :, :, s_i, :] = wv[:, :, off:off + 1024].reshape(8, 128, 4, 256).transpose(2, 1, 0, 3)
    w_m = w_m.reshape(4, 128, 6144)
    w_g = np.ascontiguousarray(wv[:, :, 7168:7176].transpose(1, 0, 2)).reshape(128, 64)
    if "nc" not in _NC_CACHE:
        _NC_CACHE["nc"] = build_nc()
    nc = _NC_CACHE["nc"]
    in_maps = []
    for c in range(NCORES):
        sl = slice(16 * c, 16 * c + 16)
        m = dict(
            xp=x_prompt[c], xs=x_sample[sl].reshape(64, 1024),
            cc=np.concatenate([c_prompt[c:c + 1], c_sample[sl]], axis=0),
            sh=state_hgrn[0, sl], sC=state_mlstm_C[0, sl], sn=state_mlstm_n[0, sl], sm=state_mlstm_m[0, sl],
            scv=state_mlstm_conv[0, sl].reshape(48, 1024),
            w_ada=w_ada[0], b_ada=b_ada[0].reshape(24, 128), w_out=f(w_out)[0],
            prm=prm, bg=bg, wbd=wbd, w_h=w_h, w_m=w_m, w_g=w_g, **consts)
        in_maps.append({k: np.ascontiguousarray(v) for k, v in m.items()})
    res = run_bass_kernel_spmd(nc, in_maps, core_ids=list(range(NCORES)))
    R = res.results
    cat = lambda k: np.stack([r[k] for r in R], axis=0)
    y_prompt = cat("yp")
    y_sample = np.concatenate([r["ys"].reshape(16, 4, 1024) for r in R], axis=0)
    hgrn_p = cat("ohp")[None]
    C_p = cat("oCp")[None]
    n_p = cat("onp")[None]
    m_p = cat("omp").reshape(8, 4)[None]
    conv_p = cat("ocp")[None]
    hgrn_s = np.concatenate([r["ohs"] for r in R], axis=0)[None]
    C_s = np.concatenate([r["oCs"] for r in R], axis=0)[None]
    n_s = np.concatenate([r["ons"] for r in R], axis=0)[None]
    m_s = np.concatenate([r["oms"] for r in R], axis=0)[None]
    conv_s = np.concatenate([r["ocs"] for r in R], axis=0)[None]
    outs = (y_prompt, y_sample, hgrn_p, C_p, n_p, m_p, conv_p, hgrn_s, C_s, n_s, m_s, conv_s)
    return tuple(np.ascontiguousarray(o, dtype=np.float32) for o in outs)
```

```python
import numpy as np
from contextlib import ExitStack
import concourse.bass as bass
import concourse.mybir as mybir
from concourse.bass_utils import run_bass_kernel_spmd

F32 = mybir.dt.float32
BF16 = mybir.dt.bfloat16
AF = mybir.ActivationFunctionType
ALU = mybir.AluOpType
AX = mybir.AxisListType
NCORES = 8
EPS = 1e-6
LN16 = float(np.log(16.0))


class Res:
    def __init__(self):
        self.last_write = None
        self.reads = []
        self.psum = False


class Tl:
    def __init__(self, ap):
        self.ap = ap
        self.r = Res()

    def __getitem__(self, k):
        return self.ap[k]


class Sched:
    def __init__(self, nc, ctx):
        self.nc = nc
        self.engs = {}
        for name in ["pe", "act", "dve", "pool", "sp"]:
            sem = ctx.enter_context(nc.semaphore("sem_" + name))
            self.engs[name] = dict(name=name, sem=sem, count=0, ops=[], waited={})
        self.dma_ring = {}
        for q in ["sp", "pool"]:
            n = 24
            sems = [ctx.enter_context(nc.semaphore(f"dq_{q}_{i}")) for i in range(n)]
            self.dma_ring[q] = dict(sems=sems, idx=0, vals=[0] * n)
        self.out_events = []
        self.dma_events = []
        self.defer = None

    def _wait(self, e, ev):
        sem, val = ev
        w = e["waited"]
        key = sem.num
        if w.get(key, 0) >= val:
            return
        w[key] = val
        e["ops"].append(("wait", sem, val))

    def _deps(self, e, reads, writes):
        for t in reads:
            r = t.r
            if r.last_write is not None:
                self._wait(e, r.last_write)
            if r.psum:
                for ev in r.reads:
                    if ev[0] is not e["sem"]:
                        self._wait(e, ev)
        for t in writes:
            r = t.r
            if r.last_write is not None:
                self._wait(e, r.last_write)
            for ev in r.reads:
                self._wait(e, ev)

    def _record(self, ev, reads, writes):
        for t in writes:
            t.r.last_write = ev
            t.r.reads = []
        for t in reads:
            if t in writes:
                continue
            t.r.reads = [x for x in t.r.reads if x[0] is not ev[0]] + [ev]

    def op(self, eng, fn, reads=(), writes=()):
        if self.defer is not None:
            self.defer.append(("op", eng, fn, tuple(reads), tuple(writes), False))
            return None
        e = self.engs[eng]
        self._deps(e, reads, writes)
        e["count"] += 1
        ev = (e["sem"], e["count"])
        e["ops"].append(("op", fn, e["sem"], 1))
        self._record(ev, reads, writes)
        return ev

    def mm(self, fns, reads=(), writes=()):
        if self.defer is not None:
            self.defer.append(("mm", "pe", fns, tuple(reads), tuple(writes), False))
            return None
        e = self.engs["pe"]
        self._deps(e, reads, writes)
        for fn in fns[:-1]:
            e["ops"].append(("op", fn, None, 0))
        e["count"] += 1
        ev = (e["sem"], e["count"])
        e["ops"].append(("op", fns[-1], e["sem"], 1))
        self._record(ev, reads, writes)
        return ev

    def dma(self, q, fn, reads=(), writes=(), is_output=False):
        if self.defer is not None:
            self.defer.append(("dma", q, fn, tuple(reads), tuple(writes), is_output))
            return None
        e = self.engs[q]
        ring = self.dma_ring[q]
        i = ring["idx"] % len(ring["sems"])
        ring["idx"] += 1
        sem = ring["sems"][i]
        if ring["vals"][i] > 0:
            self._wait(e, (sem, ring["vals"][i]))
        self._deps(e, reads, writes)
        ring["vals"][i] += 16
        ev = (sem, ring["vals"][i])
        e["ops"].append(("op", fn, sem, 16))
        for t in writes:
            t.r.last_write = ev
            t.r.reads = []
        for t in reads:
            t.r.reads = t.r.reads + [ev]
        self.dma_events.append(ev)
        if is_output:
            self.out_events.append(ev)
        return ev

    def alias(self, src, dst):
        if self.defer is not None:
            self.defer.append(("alias", "none", None, tuple(src), tuple(dst), False))
            return
        evs = []
        for t in src:
            if t.r.last_write is not None:
                evs.append(t.r.last_write)
            evs.extend(t.r.reads)
        for t in dst:
            t.r.reads = list(t.r.reads) + evs

    def begin_window(self):
        assert self.defer is None
        self.defer = []

    @staticmethod
    def _probe_cost(kind, eng, fn):
        class _P:
            def __init__(self):
                self.calls = []
            def __getattr__(self, name):
                def f(*a, **k):
                    self.calls.append((name, a, k))
                    return self
                return f
        fns = fn if kind == "mm" else [fn]
        tot = 0.0
        lat = 0.0
        for f in fns:
            p = _P()
            try:
                f(p)
            except Exception:
                tot += 0.3
                continue
            for (name, a, k) in p.calls:
                out = k.get("out", a[0] if a else None)
                try:
                    shp = tuple(out.shape)
                    n = 1
                    for d in shp[1:]:
                        n *= int(d)
                    nb = n * int(shp[0]) * mybir.dt.size(out.dtype)
                except Exception:
                    n, nb = 256, 65536
                if kind == "mm":
                    slow = 1.0
                    try:
                        if k.get("lhsT", None) is not None and k["lhsT"].dtype == F32:
                            slow = 4.0
                    except Exception:
                        pass
                    tot += max(0.07, slow * n / 2000.0 + 0.02)
                elif kind == "dma":
                    tot += 0.1
                    lat = max(lat, 2.5 + nb / 120e3)
                elif eng == "act":
                    tot += 0.22 + n / 1200.0
                elif eng == "dve":
                    tot += 0.08 + n / 960.0
                else:
                    tot += 0.15 + n / 150.0
        return tot, lat

    def end_window(self):
        ops = self.defer
        self.defer = None
        n = len(ops)
        if n == 0:
            return
        rid = lambda t: id(t.r)
        preds = [set() for _ in range(n)]
        succs = [[] for _ in range(n)]
        last_w = {}
        readers = {}
        psum_ids = set()
        for o in ops:
            for t in o[3] + o[4]:
                if t.r.psum:
                    psum_ids.add(rid(t))
        for j, o in enumerate(ops):
            Rj = set(rid(t) for t in o[3])
            Wj = set(rid(t) for t in o[4])
            for r in Rj | Wj:
                if r in last_w:
                    preds[j].add(last_w[r])
            for r in Rj & psum_ids:
                for i in readers.get(r, ()):
                    if ops[i][1] != o[1]:
                        preds[j].add(i)
            for r in Wj:
                for i in readers.get(r, ()):
                    preds[j].add(i)
            for r in Wj:
                last_w[r] = j
                readers[r] = []
            for r in Rj - Wj:
                readers.setdefault(r, []).append(j)
            preds[j].discard(j)
        preds = [sorted(p) for p in preds]
        for j in range(n):
            for i in preds[j]:
                succs[i].append(j)
        cost = [(0.0, 0.0) if o[0] == "alias" else self._probe_cost(o[0], o[1], o[2]) for o in ops]
        cp = [0.0] * n
        for i in range(n - 1, -1, -1):
            m = 0.0
            for j in succs[i]:
                m = max(m, cp[j])
            cp[i] = cost[i][0] + cost[i][1] + m
        eng_free = {}
        fin = [0.0] * n
        npred = [len(p) for p in preds]
        ready = [i for i in range(n) if npred[i] == 0]
        done = [False] * n
        order = []
        LAT = 0.7
        while ready:
            best = None
            for i in ready:
                eng = ops[i][1]
                st = eng_free.get(eng, 0.0)
                for p in preds[i]:
                    l = 0.0 if ops[p][1] == eng else LAT
                    st = max(st, fin[p] + l)
                key = (st, -cp[i], i)
                if best is None or key < best[0]:
                    best = (key, i, st)
            _, i, st = best
            ready.remove(i)
            eng = ops[i][1]
            if eng != "none":
                eng_free[eng] = st + cost[i][0]
            fin[i] = st + cost[i][0] + cost[i][1]
            order.append(i)
            for j in succs[i]:
                npred[j] -= 1
                if npred[j] == 0:
                    ready.append(j)
        assert len(order) == n
        for i in order:
            kind, eng, fn, r, w, is_out = ops[i]
            if kind == "alias":
                self.alias(r, w)
            elif kind == "op":
                self.op(eng, fn, r, w)
            elif kind == "mm":
                self.mm(fn, r, w)
            else:
                self.dma(eng, fn, r, w, is_output=is_out)

    def barrier(self):
        evs = [(e["sem"], e["count"]) for e in self.engs.values() if e["count"] > 0] + self.dma_events
        for e in self.engs.values():
            for ev in evs:
                if ev[0] is e["sem"]:
                    continue
                self._wait(e, ev)
        self.dma_events = []

    def finish(self):
        self.barrier()
        e = self.engs["sp"]
        for ev in self.out_events:
            self._wait(e, ev)

    def emit(self, block):
        def replay(engname):
            def f(engine):
                for item in self.engs[engname]["ops"]:
                    if item[0] == "wait":
                        engine.wait_ge(item[1], item[2])
                    else:
                        _, fn, sem, inc = item
                        ins = fn(engine)
                        if sem is not None:
                            ins.then_inc(sem, inc)
            return f
        block.sync(replay("sp"))
        block.scalar(replay("act"))
        block.vector(replay("dve"))
        block.gpsimd(replay("pool"))
        block.tensor(replay("pe"))


class Arena:
    def __init__(self, ap, nwords):
        self.ap = ap
        self.n = nwords
        self.off = 0

    def reset(self):
        self.off = 0

    def alloc(self, rows, free, dt):
        nel = int(np.prod(free))
        nw = (nel + 1) // 2 if dt == BF16 else nel
        nw = (nw + 7) // 8 * 8
        assert self.off + nw <= self.n, f"arena overflow {self.off}+{nw}>{self.n}"
        v = self.ap[0:rows, self.off:self.off + nw]
        self.off += nw
        if dt == BF16:
            v = v.bitcast(BF16)
        v = v[:, 0:nel]
        if len(free) == 2:
            v = v.rearrange("p (a b) -> p a b", a=free[0])
        elif len(free) == 3:
            v = v.rearrange("p (a b c) -> p a b c", a=free[0], b=free[1])
        return Tl(v)


def build_nc():
    nc = bass.Bass("TRN2", target_bir_lowering=False)
    di = lambda name, shape: nc.dram_tensor(name, shape, F32, kind="ExternalInput").ap()
    do = lambda name, shape: nc.dram_tensor(name, shape, F32, kind="ExternalOutput").ap()
    xp = di("xp", [2048, 1024]); xs = di("xs", [64, 1024]); ccd = di("cc", [17, 1024])
    sh = di("sh", [16, 8, 128, 128]); sC = di("sC", [16, 4, 256, 256]); sn = di("sn", [16, 4, 256])
    sm = di("sm", [16, 4]); scv = di("scv", [48, 1024])
    w_ada = di("w_ada", [1024, 3072]); b_ada = di("b_ada", [24, 128]);
    w_out = di("w_out", [2048, 1024])
    w_hd = di("w_h", [8, 128, 4096]); w_md = di("w_m", [4, 128, 6144]); w_gd = di("w_g", [128, 64])
    prm = di("prm", [32, 1024]); bgd = di("bg", [4, 2]); wbdd = di("wbd", [128, 3, 8, 128])
    cmask_d = di("cmask", [128, 128]); smask_d = di("smask", [64, 64]); rowmask_d = di("rowmask", [64, 16])
    colmask_d = di("colmask", [1, 1024]); sel_d = di("sel", [32, 3, 128]); selg_d = di("selg", [17, 192])
    yp = do("yp", [2048, 1024]); ys = do("ys", [64, 1024])
    ohp = do("ohp", [8, 128, 128]); oCp = do("oCp", [4, 256, 256]); onp = do("onp", [4, 256]); omp = do("omp", [4, 1])
    ocp = do("ocp", [3, 1024])
    ohs = do("ohs", [16, 8, 128, 128]); oCs = do("oCs", [16, 4, 256, 256]); ons = do("ons", [16, 4, 256])
    oms = do("oms", [16, 4]); ocs = do("ocs", [16, 3, 1024])

    w_ada_v = w_ada.rearrange("(k p) c -> p k c", p=128)
    w_out_v = w_out.rearrange("(k p) c -> p k c", p=128)

    with ExitStack() as ctx:
        S = Sched(nc, ctx)
        sbt = lambda name, shape, dt: Tl(ctx.enter_context(nc.sbuf_tensor(name, shape, dt))[:])
        def pst(name, shape, dt):
            t = Tl(ctx.enter_context(nc.psum_tensor(name, shape, dt))[:])
            t.r.psum = True
            return t
        ACT = lambda fn, r=(), w=(): S.op("act", fn, r, w)
        DVE = lambda fn, r=(), w=(): S.op("dve", fn, r, w)
        POOL = lambda fn, r=(), w=(): S.op("pool", fn, r, w)
        MM = lambda fns, r=(), w=(): S.mm(fns, r, w)
        DMA = lambda fn, r=(), w=(), out=False: S.dma("sp", fn, r, w, is_output=out)
        DMAC = lambda fn, r=(), w=(): S.dma("pool", fn, r, w)

        hT = sbt("hT", [128, 8, 2112], BF16)
        mixT = sbt("mixT", [128, 16, 2112], BF16)
        wbuf = sbt("wbuf", [128, 8 * 768], BF16)
        arena_t = sbt("arena", [128, 15872], F32)
        AR = Arena(arena_t.ap, 15872)
        ident32 = sbt("ident32", [128, 128], F32)
        identb = sbt("identb", [128, 128], BF16)
        ones32 = sbt("ones32", [128, 128], F32)
        P = sbt("P", [128, 8, 32], F32)
        Pn = sbt("Pn", [128, 8, 32], F32)
        lbv = sbt("lbv", [128, 8, 4], F32)
        convd = sbt("convd", [128, 8, 4, 128], BF16)
        wbd = sbt("wbd_s", [128, 3, 8, 128], BF16)
        bc_hi = sbt("bc_hi", [128, 1024], BF16)
        bc_mo = sbt("bc_mo", [128, 1024], BF16)
        msk = sbt("msk", [128, 576], F32)
        cmask = sbt("cmask_s", [128, 128], F32)
        smask = sbt("smask_s", [64, 64], F32)
        rowmask = sbt("rowmask_s", [64, 16], BF16)
        colmask = sbt("colmask_s", [128, 16, 64], BF16)
        modT = sbt("modT", [128, 24, 17], F32)
        Amod = sbt("Amod", [128, 8, 17], F32)
        Sbf = sbt("Sbf", [128, 128], BF16)
        sm8 = sbt("sm8", [128, 16], F32)
        gbc = sbt("gbc", [128, 32, 4], F32)
        ectok = sbt("ectok", [128, 17, 4], F32)
        thrtok = sbt("thrtok", [128, 17, 4], F32)
        bA = pst("bA", [128, 512], F32); bB = pst("bB", [128, 512], F32)
        bC = pst("bC", [128, 512], F32); bD = pst("bD", [128, 512], F32)
        bS = pst("bS", [128, 512], F32); bO = pst("bO", [128, 512], F32); bU = pst("bU", [128, 512], F32)
        bT = pst("bT", [128, 1024], BF16)

        POOL(lambda e: e.memset(ident32.ap, 0.0), w=[ident32])
        POOL(lambda e: e.affine_select(out=ident32.ap, in_=ident32.ap, pattern=[[-1, 128]], compare_op=ALU.not_equal,
                                       fill=1.0, base=0, channel_multiplier=1), r=[ident32], w=[ident32])
        DVE(lambda e: e.tensor_copy(out=identb.ap, in_=ident32.ap), r=[ident32], w=[identb])
        POOL(lambda e: e.memset(ones32.ap, 1.0), w=[ones32])
        POOL(lambda e: e.memset(msk.ap, 1.0), w=[msk])
        POOL(lambda e: e.memset(msk.ap[:, 0:512].rearrange("p (c l) -> p c l", l=128)[:, :, 0:1], 0.0), r=[msk], w=[msk])
        POOL(lambda e: e.memset(msk.ap[:, 512:576].rearrange("p (c l) -> p c l", l=4)[:, :, 0:1], 0.0), r=[msk], w=[msk])
        DMA(lambda e: e.dma_start(out=cmask.ap, in_=cmask_d), w=[cmask])
        DMA(lambda e: e.dma_start(out=smask.ap, in_=smask_d), w=[smask])
        DMAC(lambda e: e.dma_start(out=rowmask.ap, in_=rowmask_d), w=[rowmask])
        DMAC(lambda e: e.dma_start(out=colmask.ap.rearrange("p a b -> p (a b)"), in_=colmask_d.partition_broadcast(128)),
             w=[colmask])
        DMAC(lambda e: e.dma_start(out=wbd.ap, in_=wbdd), w=[wbd])

        AR.reset()
        prm_s = AR.alloc(32, (1024,), F32)
        sel_s = AR.alloc(32, (3, 128), F32)
        bada_s = AR.alloc(24, (128,), F32)
        badaT = AR.alloc(128, (24,), F32)
        DMA(lambda e: e.dma_start(out=prm_s.ap, in_=prm), w=[prm_s])
        DMA(lambda e: e.dma_start(out=sel_s.ap, in_=sel_d), w=[sel_s])
        DMA(lambda e: e.dma_start(out=bada_s.ap, in_=b_ada), w=[bada_s])
        MM([lambda e, k=k: e.transpose(out=bS.ap[:, k * 32:(k + 1) * 32], in_=prm_s.ap[:, k * 128:(k + 1) * 128],
                                       identity=ident32.ap[0:32, 0:32]) for k in range(8)],
           r=[prm_s, ident32], w=[bS])
        DVE(lambda e: e.tensor_copy(out=P.ap.rearrange("p a b -> p (a b)"), in_=bS.ap[:, 0:256]), r=[bS], w=[P])
        DVE(lambda e: e.tensor_scalar(out=Pn.ap.rearrange("p a b -> p (a b)"), in0=P.ap.rearrange("p a b -> p (a b)"),
                                      scalar1=-1.0, scalar2=None, op0=ALU.mult), r=[P], w=[Pn])
        MM([lambda e: e.transpose(out=bO.ap[:, 0:24], in_=bada_s.ap, identity=ident32.ap[0:24, 0:24])],
           r=[bada_s, ident32], w=[bO])
        DVE(lambda e: e.tensor_copy(out=badaT.ap, in_=bO.ap[:, 0:24]), r=[bO], w=[badaT])
        DVE(lambda e: e.tensor_tensor(out=lbv.ap[:, :, 2], in0=P.ap[:, :, 9], in1=P.ap[:, :, 8], op=ALU.subtract), r=[P], w=[lbv])
        ACT(lambda e: e.activation(out=lbv.ap[:, :, 2], in_=lbv.ap[:, :, 2], func=AF.Exp), r=[lbv], w=[lbv])
        DVE(lambda e: e.tensor_scalar(out=lbv.ap[:, :, 2], in0=lbv.ap[:, :, 2], scalar1=1.0, scalar2=None, op0=ALU.add), r=[lbv], w=[lbv])
        DVE(lambda e: e.reciprocal(out=lbv.ap[:, :, 0], in_=lbv.ap[:, :, 2]), r=[lbv], w=[lbv])
        ACT(lambda e: e.activation(out=lbv.ap[:, :, 3], in_=lbv.ap[:, :, 0], func=AF.Ln, scale=-1.0, bias=1.0), r=[lbv], w=[lbv])
        DVE(lambda e: e.tensor_tensor(out=lbv.ap[:, :, 1], in0=lbv.ap[:, :, 3], in1=P.ap[:, :, 1], op=ALU.subtract), r=[lbv, P], w=[lbv])
        for k in range(8):
            for i in range(4):
                DVE(lambda e, k=k, i=i: e.tensor_scalar(out=convd.ap[:, k, i, :], in0=ident32.ap, scalar1=P.ap[:, k, 11 + i:12 + i],
                                                        scalar2=None, op0=ALU.mult), r=[ident32, P], w=[convd])
        for (si, dst) in ((0, bc_hi), (1, bc_mo)):
            for half, bank in ((0, bA), (1, bB)):
                MM([lambda e, si=si, half=half, bank=bank: e.matmul(bank.ap, lhsT=sel_s.ap[:, si, :], rhs=prm_s.ap[:, half * 512:(half + 1) * 512],
                                                                    start=True, stop=True)], r=[sel_s, prm_s], w=[bank])
                DVE(lambda e, dst=dst, half=half, bank=bank: e.tensor_copy(out=dst.ap[:, half * 512:(half + 1) * 512], in_=bank.ap), r=[bank], w=[dst])

        S.begin_window()
        cc_s = AR.alloc(17, (1024,), F32)
        ce = AR.alloc(17, (1024,), F32)
        csil = AR.alloc(17, (1024,), BF16)
        siluT = AR.alloc(128, (8, 17), BF16)
        modtok = AR.alloc(17, (3072,), F32)
        wa = [AR.alloc(128, (3072,), BF16) for _ in range(2)]
        DMA(lambda e: e.dma_start(out=cc_s.ap, in_=ccd), w=[cc_s])
        ACT(lambda e: e.activation(out=ce.ap, in_=cc_s.ap, func=AF.Exp, scale=-1.0), r=[cc_s], w=[ce])
        DVE(lambda e: e.tensor_scalar(out=ce.ap, in0=ce.ap, scalar1=1.0, scalar2=None, op0=ALU.add), r=[ce], w=[ce])
        DVE(lambda e: e.reciprocal(out=ce.ap, in_=ce.ap), r=[ce], w=[ce])
        DVE(lambda e: e.tensor_tensor(out=csil.ap, in0=cc_s.ap, in1=ce.ap, op=ALU.mult), r=[cc_s, ce], w=[csil])
        MM([lambda e, k=k: e.transpose(out=bT.ap[:, k * 32:k * 32 + 17], in_=csil.ap[:, k * 128:(k + 1) * 128],
                                       identity=identb.ap[0:17, 0:17]) for k in range(8)], r=[csil, identb], w=[bT])
        DVE(lambda e: e.tensor_copy(out=siluT.ap, in_=bT.ap[:, 0:256].rearrange("p (a b) -> p a b", a=8)[:, :, 0:17]), r=[bT], w=[siluT])
        banks6 = [bA, bB, bC, bD, bS, bO]
        for k in range(8):
            w_t = wa[k % 2]
            DMAC(lambda e, k=k, w_t=w_t: e.dma_start(out=w_t.ap, in_=w_ada_v[:, k, :]), w=[w_t])
            for n in range(6):
                MM([lambda e, k=k, n=n, w_t=w_t: e.matmul(banks6[n].ap[0:17, :], lhsT=siluT.ap[:, k, :], rhs=w_t.ap[:, n * 512:(n + 1) * 512],
                                                          start=(k == 0), stop=(k == 7))], r=[siluT, w_t], w=[banks6[n]])
        for n in range(6):
            DVE(lambda e, n=n: e.tensor_copy(out=modtok.ap[:, n * 512:(n + 1) * 512], in_=banks6[n].ap[0:17, :]), r=[banks6[n]], w=[modtok])
        MM([lambda e, j=j: e.transpose(out=bU.ap[:, j * 17:(j + 1) * 17], in_=modtok.ap[:, j * 128:(j + 1) * 128],
                                       identity=ident32.ap[0:17, 0:17]) for j in range(24)], r=[modtok, ident32], w=[bU])
        DVE(lambda e: e.tensor_tensor(out=modT.ap, in0=bU.ap[:, 0:408].rearrange("p (a b) -> p a b", a=24),
                                      in1=badaT.ap.unsqueeze(2).to_broadcast([128, 24, 17]), op=ALU.add), r=[bU, badaT], w=[modT])
        DVE(lambda e: e.scalar_tensor_tensor(out=Amod.ap, in0=modT.ap[:, 8:16, :], scalar=1.0,
                                             in1=P.ap[:, :, 7:8].to_broadcast([128, 8, 17]), op0=ALU.add, op1=ALU.mult),
            r=[modT, P], w=[Amod])

        xt = [AR.alloc(128, (1024,), F32) for _ in range(2)]
        junk = AR.alloc(128, (1024,), BF16)
        xn = [AR.alloc(128, (1024,), BF16) for _ in range(2)]
        tmpf = AR.alloc(128, (8, 128), F32)
        st2 = AR.alloc(128, (8,), F32)
        def tile2(i):
            rows = 128 if i < 16 else 64
            x_t = xt[i % 2]; xn_t = xn[i % 2]
            src = xp[i * 128:(i + 1) * 128, :] if i < 16 else xs
            DMA(lambda e, x_t=x_t, src=src, rows=rows: e.dma_start(out=x_t.ap[0:rows, :], in_=src), w=[x_t])
            ACT(lambda e, x_t=x_t, rows=rows: e.activation(out=junk.ap[0:rows, :], in_=x_t.ap[0:rows, :], func=AF.Square, scale=1.0 / 32.0,
                                                          accum_out=st2.ap[0:rows, 0:1]), r=[x_t], w=[junk, st2])
            ACT(lambda e, rows=rows: e.activation(out=st2.ap[0:rows, 1:2], in_=st2.ap[0:rows, 0:1], func=AF.Ln, bias=EPS), r=[st2], w=[st2])
            ACT(lambda e, rows=rows: e.activation(out=st2.ap[0:rows, 2:3], in_=st2.ap[0:rows, 1:2], func=AF.Exp, scale=-0.5), r=[st2], w=[st2])
            ACT(lambda e, x_t=x_t, xn_t=xn_t, rows=rows: e.activation(out=xn_t.ap[0:rows, :], in_=x_t.ap[0:rows, :], func=AF.Copy,
                                                                     scale=st2.ap[0:rows, 2:3]), r=[x_t, st2], w=[xn_t])
            MM([lambda e, k=k, xn_t=xn_t, rows=rows: e.transpose(out=bT.ap[:, k * 128:k * 128 + rows], in_=xn_t.ap[0:rows, k * 128:(k + 1) * 128],
                                                                identity=identb.ap[0:rows, 0:rows]) for k in range(8)],
               r=[xn_t, identb], w=[bT])
            pv = bT.ap.rearrange("p (a b) -> p a b", a=8)[:, :, 0:rows]
            if i < 16:
                DVE(lambda e, pv=pv: e.tensor_tensor(out=tmpf.ap, in0=pv, in1=Amod.ap[:, :, 0:1].to_broadcast([128, 8, 128]), op=ALU.mult),
                    r=[bT, Amod], w=[tmpf])
                DVE(lambda e, i=i: e.tensor_tensor(out=hT.ap[:, :, i * 128:(i + 1) * 128], in0=tmpf.ap,
                                                    in1=modT.ap[:, 0:8, 0:1].to_broadcast([128, 8, 128]), op=ALU.add),
                     r=[tmpf, modT], w=[hT])
            else:
                for k in range(8):
                    DVE(lambda e, k=k: e.tensor_tensor(out=tmpf.ap[:, k, 0:64].rearrange("p (b t) -> p b t", t=4),
                                                       in0=bT.ap[:, k * 128:k * 128 + 64].rearrange("p (b t) -> p b t", t=4),
                                                       in1=Amod.ap[:, k, 1:17].unsqueeze(2).to_broadcast([128, 16, 4]), op=ALU.mult),
                        r=[bT, Amod], w=[tmpf])
                    DVE(lambda e, k=k: e.tensor_tensor(out=hT.ap[:, k, 2048:2112].rearrange("p (b t) -> p b t", t=4),
                                                        in0=tmpf.ap[:, k, 0:64].rearrange("p (b t) -> p b t", t=4),
                                                        in1=modT.ap[:, k, 1:17].unsqueeze(2).to_broadcast([128, 16, 4]), op=ALU.add),
                         r=[tmpf, modT], w=[hT])

        for i in range(17):
            tile2(i)
        S.end_window()

        GROUPS = [(0, 512), (512, 512), (1024, 512), (1536, 512), (2048, 64)]

        def silu_from_psum(ps, nbias_ap, bias_ap, T, ez, sg, out_ap, out_t, extra_r=()):
            ACT(lambda e: e.activation(out=ez.ap[:, 0:T], in_=ps.ap[:, 0:T], func=AF.Exp, scale=-1.0, bias=nbias_ap), r=[ps, Pn], w=[ez])
            ACT(lambda e: e.activation(out=ez.ap[:, 0:T], in_=ez.ap[:, 0:T], func=AF.Ln, bias=1.0), r=[ez], w=[ez])
            ACT(lambda e: e.activation(out=sg.ap[:, 0:T], in_=ez.ap[:, 0:T], func=AF.Exp, scale=-1.0), r=[ez], w=[sg])
            DVE(lambda e: e.scalar_tensor_tensor(out=out_ap, in0=ps.ap[:, 0:T], scalar=bias_ap, in1=sg.ap[:, 0:T], op0=ALU.add, op1=ALU.mult),
                r=[ps, sg, P], w=[out_t])

        def interleave(*gens):
            gens = [g for g in gens if g is not None]
            while gens:
                for g in list(gens):
                    try:
                        next(g)
                    except StopIteration:
                        gens.remove(g)

        def silu_gen(ps, nbias_ap, bias_ap, T, ez, sg, out_ap, out_t):
            ACT(lambda e: e.activation(out=ez.ap[:, 0:T], in_=ps.ap[:, 0:T], func=AF.Exp, scale=-1.0, bias=nbias_ap), r=[ps, Pn], w=[ez]); yield
            ACT(lambda e: e.activation(out=ez.ap[:, 0:T], in_=ez.ap[:, 0:T], func=AF.Ln, bias=1.0), r=[ez], w=[ez]); yield
            ACT(lambda e: e.activation(out=sg.ap[:, 0:T], in_=ez.ap[:, 0:T], func=AF.Exp, scale=-1.0), r=[ez], w=[sg]); yield
            DVE(lambda e: e.scalar_tensor_tensor(out=out_ap, in0=ps.ap[:, 0:T], scalar=bias_ap, in1=sg.ap[:, 0:T], op0=ALU.add, op1=ALU.mult),
                r=[ps, sg, P], w=[out_t]); yield

        S.barrier()
        AR.reset()
        f_e = AR.alloc(128, (512,), F32); f_L1 = AR.alloc(128, (512,), F32); f_L2 = AR.alloc(128, (512,), F32)
        f_G = AR.alloc(128, (512,), F32); f_eG = AR.alloc(128, (512,), F32)
        f_ez = AR.alloc(128, (512,), F32); f_sg = AR.alloc(128, (512,), F32)
        wh2_t = AR.alloc(128, (4096,), BF16)
        GO = []
        for _ in range(2):
            GO.append(dict(qt=AR.alloc(128, (512,), BF16), kt=AR.alloc(128, (512,), BF16), kd=AR.alloc(128, (512,), BF16),
                           gzg=AR.alloc(128, (512,), BF16), vtok=AR.alloc(128, (4, 128), BF16), kdtok=AR.alloc(128, (4, 128), BF16),
                           eGL=AR.alloc(128, (16,), F32)))
        ATm = AR.alloc(128, (128,), BF16); on_t = AR.alloc(128, (128,), BF16); junk2 = AR.alloc(128, (128,), BF16)
        ATm4 = AR.alloc(128, (4, 128), BF16); on4 = AR.alloc(128, (4, 128), BF16)
        S4 = AR.alloc(128, (4, 128), F32); Sb3 = AR.alloc(128, (3, 128), BF16); sq4 = AR.alloc(128, (512,), F32)
        Sin = AR.alloc(128, (16, 128), F32); Sbb = AR.alloc(128, (16, 128), BF16)
        qexp = AR.alloc(128, (16, 64), BF16); kexp = AR.alloc(64, (16, 128), BF16)
        wh_a = Tl(wbuf.ap[:, 0:4096].rearrange("p (k s c) -> p k s c", k=8, s=4))
        wh_a.r = wbuf.r
        wh_b = Tl(wh2_t.ap.rearrange("p (k s c) -> p k s c", k=8, s=4))
        wh_b.r = wh2_t.r
        whs = [wh_a, wh_b]
        bDb = Tl(bD.ap.bitcast(BF16))
        bDb.r = bD.r
        bTf = Tl(bT.ap.bitcast(F32))
        bTf.r = bT.r

        def h_stage1(h, tok0, T, go):
            wh = whs[h % 2]
            qt, kt, kd, gzg, vtok, kdtok, eGL = go["qt"], go["kt"], go["kd"], go["gzg"], go["vtok"], go["kdtok"], go["eGL"]
            sample = (T == 64)
            rows = 64 if sample else 128
            nch = 1 if sample else 4
            mk = msk.ap[:, 512:576] if sample else msk.ap[:, 0:512]
            if tok0 == 0:
                DMAC(lambda e: e.dma_start(out=wh.ap.rearrange("p k s c -> p (k s c)"), in_=w_hd[h]), w=[wh])
                yield
            for (bank, s_i) in ((bA, 1), (bB, 0), (bC, 3)):
                MM([lambda e, k=k, bank=bank, s_i=s_i: e.matmul(bank.ap[:, 0:T], lhsT=wh.ap[:, k, s_i, :], rhs=hT.ap[:, k, tok0:tok0 + T],
                                                                start=(k == 0), stop=(k == 7)) for k in range(8)], r=[wh, hT], w=[bank])
                yield
            fns = []
            for i in range(nch):
                for k in range(8):
                    fns.append(lambda e, i=i, k=k: e.matmul(bD.ap[0:rows, i * 128:(i + 1) * 128], lhsT=hT.ap[:, k, tok0 + i * 128: tok0 + i * 128 + rows],
                                                            rhs=wh.ap[:, k, 2, :], start=(k == 0), stop=(k == 7)))
            MM(fns, r=[wh, hT], w=[bD]); yield
            ACT(lambda e: e.activation(out=f_e.ap[:, 0:T], in_=bA.ap[:, 0:T], func=AF.Exp, scale=-1.0, bias=Pn.ap[:, h, 1:2]), r=[bA, Pn], w=[f_e]); yield
            ACT(lambda e: e.activation(out=f_L1.ap[:, 0:T], in_=f_e.ap[:, 0:T], func=AF.Ln, bias=1.0), r=[f_e], w=[f_L1]); yield
            ACT(lambda e: e.activation(out=f_L2.ap[:, 0:T], in_=f_e.ap[:, 0:T], func=AF.Ln, bias=1.0, scale=lbv.ap[:, h, 0:1]), r=[f_e, lbv], w=[f_L2]); yield
            DVE(lambda e: e.tensor_tensor(out=f_L2.ap[:, 0:T], in0=f_L2.ap[:, 0:T], in1=f_L1.ap[:, 0:T], op=ALU.subtract), r=[f_L2, f_L1], w=[f_L2]); yield
            DVE(lambda e: e.tensor_tensor_scan(out=f_G.ap[:, 0:T], data0=mk, data1=f_L2.ap[:, 0:T], initial=0.0, op0=ALU.mult, op1=ALU.add),
                r=[msk, f_L2], w=[f_G]); yield
            ACT(lambda e: e.activation(out=f_eG.ap[:, 0:T], in_=f_G.ap[:, 0:T], func=AF.Exp), r=[f_G], w=[f_eG]); yield
            DVE(lambda e: e.scalar_tensor_tensor(out=qt.ap[:, 0:T], in0=bB.ap[:, 0:T], scalar=P.ap[:, h, 0:1], in1=f_eG.ap[:, 0:T],
                                                 op0=ALU.add, op1=ALU.mult), r=[bB, P, f_eG], w=[qt]); yield
            DVE(lambda e: e.scalar_tensor_tensor(out=f_e.ap[:, 0:T], in0=bA.ap[:, 0:T], scalar=-1.0, in1=f_L1.ap[:, 0:T],
                                                 op0=ALU.mult, op1=ALU.subtract), r=[bA, f_L1], w=[f_e]); yield
            DVE(lambda e: e.tensor_tensor(out=f_e.ap[:, 0:T], in0=f_e.ap[:, 0:T], in1=f_G.ap[:, 0:T], op=ALU.subtract), r=[f_e, f_G], w=[f_e]); yield
            ACT(lambda e: e.activation(out=kt.ap[:, 0:T], in_=f_e.ap[:, 0:T], func=AF.Exp, bias=lbv.ap[:, h, 1:2]), r=[f_e, lbv], w=[kt]); yield
            L = 4 if sample else 128
            ncg = T // L
            Gv = f_G.ap[:, 0:T].rearrange("p (c l) -> p c l", l=L)
            ACT(lambda e: e.activation(out=eGL.ap[:, 0:ncg], in_=Gv[:, :, L - 1], func=AF.Exp), r=[f_G], w=[eGL]); yield
            DVE(lambda e: e.tensor_tensor(out=kd.ap[:, 0:T].rearrange("p (c l) -> p c l", l=L), in0=kt.ap[:, 0:T].rearrange("p (c l) -> p c l", l=L),
                                          in1=eGL.ap[:, 0:ncg].unsqueeze(2).to_broadcast([128, ncg, L]), op=ALU.mult), r=[kt, eGL], w=[kd]); yield
            yield from silu_gen(bC, Pn.ap[:, h, 3:4], P.ap[:, h, 3:4], T, f_ez, f_sg, gzg.ap[:, 0:T], gzg)

            DVE(lambda e: e.tensor_tensor(out=vtok.ap[0:rows, 0:nch, :], in0=bD.ap[0:rows, 0:nch * 128].rearrange("p (a b) -> p a b", a=nch),
                                          in1=bc_hi.ap[0:rows, h * 128:(h + 1) * 128].unsqueeze(1).to_broadcast([rows, nch, 128]), op=ALU.add),
                r=[bD, bc_hi], w=[vtok]); yield
            MM([lambda e, i=i: e.transpose(out=bDb.ap[0:rows, i * 128:(i + 1) * 128], in_=kd.ap[:, i * 128:i * 128 + rows], identity=identb.ap)
                for i in range(nch)], r=[kd, identb], w=[bDb]); yield
            DVE(lambda e: e.tensor_copy(out=kdtok.ap[0:rows, 0:nch, :], in_=bDb.ap[0:rows, 0:nch * 128].rearrange("p (a b) -> p a b", a=nch)),
                r=[bDb], w=[kdtok]); yield

        def o_epilogue(h, rows, tok0, gzg):
            ACT(lambda e: e.activation(out=junk2.ap[0:rows, :], in_=bO.ap[0:rows, 0:128], func=AF.Square, scale=float(128 ** -0.5),
                                       accum_out=sm8.ap[0:rows, 0:1]), r=[bO], w=[junk2, sm8]); yield
            ACT(lambda e: e.activation(out=sm8.ap[0:rows, 1:2], in_=sm8.ap[0:rows, 0:1], func=AF.Ln, bias=EPS), r=[sm8], w=[sm8]); yield
            ACT(lambda e: e.activation(out=sm8.ap[0:rows, 2:3], in_=sm8.ap[0:rows, 1:2], func=AF.Exp, scale=-0.5), r=[sm8], w=[sm8]); yield
            ACT(lambda e: e.activation(out=on_t.ap[0:rows, :], in_=bO.ap[0:rows, 0:128], func=AF.Copy, scale=sm8.ap[0:rows, 2:3]),
                r=[bO, sm8], w=[on_t]); yield
            MM([lambda e: e.transpose(out=bT.ap[:, 0:rows], in_=on_t.ap[0:rows, :], identity=identb.ap[0:rows, 0:rows])],
               r=[on_t, identb], w=[bT]); yield
            DVE(lambda e: e.scalar_tensor_tensor(out=mixT.ap[:, h, tok0:tok0 + rows], in0=bT.ap[:, 0:rows], scalar=P.ap[:, h, 10:11],
                                                 in1=gzg.ap[:, tok0 % 512:tok0 % 512 + rows], op0=ALU.mult, op1=ALU.mult), r=[bT, gzg, P], w=[mixT]); yield

        def h_stage2(h, tok0, T, go):
            qt, kt, kd, gzg, vtok, kdtok, eGL = go["qt"], go["kt"], go["kd"], go["gzg"], go["vtok"], go["kdtok"], go["eGL"]
            sample = (T == 64)
            if tok0 == 0:
                POOL(lambda e: e.memset(S4.ap, 0.0), w=[S4])
                POOL(lambda e: e.memset(Sbf.ap, 0.0), w=[Sbf]); yield
            if not sample:
                MM([lambda e, i=i: e.matmul(bS.ap[:, i * 128:(i + 1) * 128], lhsT=kt.ap[:, i * 128:(i + 1) * 128], rhs=qt.ap[:, i * 128:(i + 1) * 128],
                                            start=True, stop=True) for i in range(4)], r=[kt, qt], w=[bS]); yield
                DVE(lambda e: e.tensor_tensor(out=ATm4.ap, in0=bS.ap.rearrange("p (a b) -> p a b", a=4),
                                              in1=cmask.ap.unsqueeze(1).to_broadcast([128, 4, 128]), op=ALU.mult), r=[bS, cmask], w=[ATm4]); yield
                MM([lambda e, i=i: e.matmul(bU.ap[:, i * 128:(i + 1) * 128], lhsT=kdtok.ap[:, i, :], rhs=vtok.ap[:, i, :], start=True, stop=True)
                    for i in range(4)], r=[kdtok, vtok], w=[bU]); yield
                for i in range(4):
                    DVE(lambda e, i=i: e.scalar_tensor_tensor(out=S4.ap[:, i, :], in0=S4.ap[:, (i + 3) % 4, :], scalar=eGL.ap[:, i:i + 1],
                                                              in1=bU.ap[:, i * 128:(i + 1) * 128], op0=ALU.mult, op1=ALU.add), r=[S4, eGL, bU], w=[S4]); yield
                ACT(lambda e: e.activation(out=Sb3.ap, in_=S4.ap[:, 0:3, :], func=AF.Copy), r=[S4], w=[Sb3]); yield
                fns = []
                for i in range(4):
                    fns.append(lambda e, i=i: e.matmul(bO.ap[:, i * 128:(i + 1) * 128], lhsT=ATm4.ap[:, i, :], rhs=vtok.ap[:, i, :], start=True, stop=False))
                    rhs_s = Sbf.ap if i == 0 else Sb3.ap[:, i - 1, :]
                    fns.append(lambda e, i=i, rhs_s=rhs_s: e.matmul(bO.ap[:, i * 128:(i + 1) * 128], lhsT=qt.ap[:, i * 128:(i + 1) * 128], rhs=rhs_s,
                                                                    start=False, stop=True))
                MM(fns, r=[ATm4, vtok, qt, Sbf, Sb3], w=[bO]); yield
                ACT(lambda e: e.activation(out=Sbf.ap, in_=S4.ap[:, 3, :], func=AF.Copy), r=[S4], w=[Sbf]); yield
                for i in range(4):
                    ACT(lambda e, i=i: e.activation(out=sq4.ap[:, i * 128:(i + 1) * 128], in_=bO.ap[:, i * 128:(i + 1) * 128], func=AF.Square,
                                                    scale=float(128 ** -0.5), accum_out=sm8.ap[:, i:i + 1]), r=[bO], w=[sq4, sm8]); yield
                ACT(lambda e: e.activation(out=sm8.ap[:, 4:8], in_=sm8.ap[:, 0:4], func=AF.Ln, bias=EPS), r=[sm8], w=[sm8]); yield
                ACT(lambda e: e.activation(out=sm8.ap[:, 8:12], in_=sm8.ap[:, 4:8], func=AF.Exp, scale=-0.5), r=[sm8], w=[sm8]); yield
                for i in range(4):
                    ACT(lambda e, i=i: e.activation(out=on4.ap[:, i, :], in_=bO.ap[:, i * 128:(i + 1) * 128], func=AF.Copy, scale=sm8.ap[:, 8 + i:9 + i]),
                        r=[bO, sm8], w=[on4]); yield
                MM([lambda e, i=i: e.transpose(out=bT.ap[:, i * 128:(i + 1) * 128], in_=on4.ap[:, i, :], identity=identb.ap) for i in range(4)],
                   r=[on4, identb], w=[bT]); yield
                DVE(lambda e: e.scalar_tensor_tensor(out=mixT.ap[:, h, tok0:tok0 + 512], in0=bT.ap[:, 0:512], scalar=P.ap[:, h, 10:11], in1=gzg.ap[:, 0:512],
                                                     op0=ALU.mult, op1=ALU.mult), r=[bT, gzg, P], w=[mixT]); yield
                if tok0 == 1536:
                    DMA(lambda e: e.dma_start(out=ohp[h], in_=S4.ap[:, 3, :]), r=[S4], out=True); yield
            else:
                DMA(lambda e: e.dma_start(out=Sin.ap, in_=sh[:, h].rearrange("b k v -> k b v")), w=[Sin]); yield
                ACT(lambda e: e.activation(out=Sbb.ap, in_=Sin.ap, func=AF.Copy), r=[Sin], w=[Sbb]); yield
                DVE(lambda e: e.tensor_tensor(out=qexp.ap, in0=qt.ap[:, 0:64].unsqueeze(1).to_broadcast([128, 16, 64]), in1=colmask.ap, op=ALU.mult),
                    r=[qt, colmask], w=[qexp]); yield
                DVE(lambda e: e.tensor_tensor(out=kexp.ap, in0=kdtok.ap[0:64, 0, :].unsqueeze(1).to_broadcast([64, 16, 128]),
                                              in1=rowmask.ap.unsqueeze(2).to_broadcast([64, 16, 128]), op=ALU.mult), r=[kdtok, rowmask], w=[kexp]); yield
                MM([lambda e: e.matmul(bS.ap[0:64, 0:64], lhsT=kt.ap[:, 0:64], rhs=qt.ap[:, 0:64], start=True, stop=True)], r=[kt, qt], w=[bS]); yield
                DVE(lambda e: e.tensor_tensor(out=ATm.ap[0:64, 0:64], in0=bS.ap[0:64, 0:64], in1=smask.ap, op=ALU.mult), r=[bS, smask], w=[ATm]); yield
                fns = [lambda e: e.matmul(bO.ap[0:64, 0:128], lhsT=ATm.ap[0:64, 0:64], rhs=vtok.ap[0:64, 0, :], start=True, stop=False)]
                for b in range(16):
                    fns.append(lambda e, b=b: e.matmul(bO.ap[0:64, 0:128], lhsT=qexp.ap[:, b, :], rhs=Sbb.ap[:, b, :], start=False, stop=(b == 15)))
                MM(fns, r=[ATm, vtok, qexp, Sbb], w=[bO]); yield
                for rnd in range(4):
                    MM([lambda e, b=b: e.matmul(bU.ap[:, (b % 4) * 128:(b % 4 + 1) * 128], lhsT=kexp.ap[:, b, :], rhs=vtok.ap[0:64, 0, :], start=True, stop=True)
                        for b in range(rnd * 4, rnd * 4 + 4)], r=[kexp, vtok], w=[bU]); yield
                    for b in range(rnd * 4, rnd * 4 + 4):
                        DVE(lambda e, b=b: e.scalar_tensor_tensor(out=Sin.ap[:, b, :], in0=Sin.ap[:, b, :], scalar=eGL.ap[:, b:b + 1],
                                                                  in1=bU.ap[:, (b % 4) * 128:(b % 4 + 1) * 128], op0=ALU.mult, op1=ALU.add),
                            r=[Sin, eGL, bU], w=[Sin]); yield
                DMA(lambda e: e.dma_start(out=ohs[:, h].rearrange("b k v -> k b v"), in_=Sin.ap), r=[Sin], out=True); yield
                yield from o_epilogue(h, 64, 2048, gzg)

        h_units = [(h, tok0, T) for h in range(8) for (tok0, T) in GROUPS]
        S.begin_window()
        for n, u in enumerate(h_units):
            interleave(h_stage1(*u, GO[n % 2]))
            interleave(h_stage2(*u, GO[n % 2]))
        S.end_window()

        S.barrier()
        AR.reset()
        wg = AR.alloc(128, (8, 8), BF16)
        bg_s = AR.alloc(4, (2,), F32); nbg = AR.alloc(4, (2,), F32)
        g_ig = AR.alloc(4, (2112,), F32); g_L = AR.alloc(4, (2112,), F32); g_nb = AR.alloc(4, (2112,), F32)
        cmax = AR.alloc(4, (32,), F32); nbL = AR.alloc(4, (32,), F32); bLn = AR.alloc(4, (32,), F32); d1 = AR.alloc(4, (32,), F32)
        mP = AR.alloc(4, (32,), F32); rall = AR.alloc(4, (32,), F32); mprev = AR.alloc(4, (32,), F32); gall = AR.alloc(4, (32,), F32)
        gdiag = AR.alloc(4, (32, 4), F32)
        DMAC(lambda e: e.dma_start(out=wg.ap.rearrange("p a b -> p (a b)"), in_=w_gd), w=[wg])
        DMA(lambda e: e.dma_start(out=bg_s.ap, in_=bgd), w=[bg_s])
        DVE(lambda e: e.tensor_scalar(out=nbg.ap, in0=bg_s.ap, scalar1=-1.0, scalar2=None, op0=ALU.mult), r=[bg_s], w=[nbg])
        POOL(lambda e: e.memset(mprev.ap, 0.0), w=[mprev])
        DMA(lambda e: e.dma_start(out=mprev.ap[:, 16:32], in_=sm.rearrange("b h -> h b"), allow_slow_non_contiguous=True), r=[mprev], w=[mprev])
        def gate_group(tok0, T):
            MM([lambda e, k=k: e.matmul(bA.ap[0:4, 0:T], lhsT=wg.ap[:, k, 0:4], rhs=hT.ap[:, k, tok0:tok0 + T], start=(k == 0), stop=(k == 7)) for k in range(8)],
               r=[wg, hT], w=[bA])
            MM([lambda e, k=k: e.matmul(bB.ap[0:4, 0:T], lhsT=wg.ap[:, k, 4:8], rhs=hT.ap[:, k, tok0:tok0 + T], start=(k == 0), stop=(k == 7)) for k in range(8)],
               r=[wg, hT], w=[bB])
            ACT(lambda e: e.activation(out=g_ig.ap[:, tok0:tok0 + T], in_=bA.ap[0:4, 0:T], func=AF.Identity, bias=bg_s.ap[:, 0:1]), r=[bA, bg_s], w=[g_ig])
            ACT(lambda e: e.activation(out=g_L.ap[:, tok0:tok0 + T], in_=bB.ap[0:4, 0:T], func=AF.Exp, scale=-1.0, bias=nbg.ap[:, 1:2]), r=[bB, nbg], w=[g_L])
            ACT(lambda e: e.activation(out=g_L.ap[:, tok0:tok0 + T], in_=g_L.ap[:, tok0:tok0 + T], func=AF.Ln, bias=1.0), r=[g_L], w=[g_L])
            mk = msk.ap[0:4, 512:576] if T == 64 else msk.ap[0:4, 0:512]
            DVE(lambda e, mk=mk: e.tensor_tensor_scan(out=g_nb.ap[:, tok0:tok0 + T], data0=mk, data1=g_L.ap[:, tok0:tok0 + T], initial=0.0,
                                                      op0=ALU.mult, op1=ALU.add), r=[msk, g_L], w=[g_nb])
        for (tok0, T) in GROUPS:
            gate_group(tok0, T)
        DVE(lambda e: e.tensor_tensor(out=g_ig.ap, in0=g_ig.ap, in1=g_nb.ap, op=ALU.add), r=[g_ig, g_nb], w=[g_ig])
        DVE(lambda e: e.reduce_max(out=cmax.ap[:, 0:16], in_=g_ig.ap[:, 0:2048].rearrange("p (c l) -> p c l", l=128), axis=AX.X), r=[g_ig], w=[cmax])
        DVE(lambda e: e.reduce_max(out=cmax.ap[:, 16:32], in_=g_ig.ap[:, 2048:2112].rearrange("p (c l) -> p c l", l=4), axis=AX.X), r=[g_ig, cmax], w=[cmax])
        DVE(lambda e: e.tensor_copy(out=nbL.ap[:, 0:16], in_=g_nb.ap[:, 0:2048].rearrange("p (c l) -> p c l", l=128)[:, :, 127]), r=[g_nb], w=[nbL])
        DVE(lambda e: e.tensor_copy(out=nbL.ap[:, 16:32], in_=g_nb.ap[:, 2048:2112].rearrange("p (c l) -> p c l", l=4)[:, :, 3]), r=[g_nb, nbL], w=[nbL])
        DVE(lambda e: e.tensor_scalar(out=bLn.ap, in0=nbL.ap, scalar1=-1.0, scalar2=None, op0=ALU.mult), r=[nbL], w=[bLn])
        DVE(lambda e: e.tensor_tensor(out=d1.ap, in0=cmax.ap, in1=nbL.ap, op=ALU.subtract), r=[cmax, nbL], w=[d1])
        DVE(lambda e: e.tensor_tensor_scan(out=mP.ap[:, 0:16], data0=bLn.ap[:, 0:16], data1=d1.ap[:, 0:16], initial=0.0, op0=ALU.add, op1=ALU.max),
            r=[bLn, d1], w=[mP])
        DVE(lambda e: e.tensor_copy(out=mprev.ap[:, 1:16], in_=mP.ap[:, 0:15]), r=[mP, mprev], w=[mprev])
        DVE(lambda e: e.tensor_tensor(out=rall.ap[:, 0:16], in0=mP.ap[:, 0:16], in1=nbL.ap[:, 0:16], op=ALU.add), r=[mP, nbL], w=[rall])
        DVE(lambda e: e.tensor_tensor(out=rall.ap[:, 16:32], in0=mprev.ap[:, 16:32], in1=cmax.ap[:, 16:32], op=ALU.max), r=[mprev, cmax, rall], w=[rall])
        DVE(lambda e: e.tensor_tensor(out=mP.ap[:, 16:32], in0=rall.ap[:, 16:32], in1=nbL.ap[:, 16:32], op=ALU.subtract), r=[rall, nbL, mP], w=[mP])
        DMA(lambda e: e.dma_start(out=omp, in_=mP.ap[:, 15:16]), r=[mP], out=True)
        DMA(lambda e: e.dma_start(out=oms.rearrange("b h -> h b"), in_=mP.ap[:, 16:32], allow_slow_non_contiguous=True), r=[mP], out=True)
        DVE(lambda e: e.tensor_tensor(out=gall.ap, in0=mprev.ap, in1=rall.ap, op=ALU.subtract), r=[mprev, rall], w=[gall])
        ACT(lambda e: e.activation(out=gall.ap, in_=gall.ap, func=AF.Exp), r=[gall], w=[gall])
        for (dst, lo, hi, L) in ((g_ig, 0, 2048, 128), (g_ig, 2048, 2112, 4), (g_nb, 0, 2048, 128), (g_nb, 2048, 2112, 4)):
            n = (hi - lo) // L
            r0 = 0 if lo == 0 else 16
            DVE(lambda e, dst=dst, lo=lo, hi=hi, L=L, n=n, r0=r0: e.tensor_tensor(
                out=dst.ap[:, lo:hi].rearrange("p (c l) -> p c l", l=L), in0=dst.ap[:, lo:hi].rearrange("p (c l) -> p c l", l=L),
                in1=rall.ap[:, r0:r0 + n].unsqueeze(2).to_broadcast([4, n, L]), op=ALU.subtract), r=[dst, rall], w=[dst])
        DVE(lambda e: e.tensor_scalar(out=g_ig.ap, in0=g_ig.ap, scalar1=-LN16, scalar2=None, op0=ALU.add), r=[g_ig], w=[g_ig])
        ACT(lambda e: e.activation(out=g_ig.ap, in_=g_ig.ap, func=AF.Exp), r=[g_ig], w=[g_ig])
        ACT(lambda e: e.activation(out=g_nb.ap, in_=g_nb.ap, func=AF.Exp), r=[g_nb], w=[g_nb])
        fns = []
        for i in range(17):
            rows = 128 if i < 16 else 64
            fns.append(lambda e, i=i, rows=rows: e.transpose(out=bS.ap[0:rows, i * 4:(i + 1) * 4], in_=g_ig.ap[:, i * 128:i * 128 + rows], identity=ident32.ap[0:4, 0:4]))
            fns.append(lambda e, i=i, rows=rows: e.transpose(out=bS.ap[0:rows, 128 + i * 4:128 + (i + 1) * 4], in_=g_nb.ap[:, i * 128:i * 128 + rows], identity=ident32.ap[0:4, 0:4]))
        MM(fns, r=[g_ig, g_nb, ident32], w=[bS])
        DVE(lambda e: e.tensor_copy(out=ectok.ap.rearrange("p a b -> p (a b)"), in_=bS.ap[:, 0:68]), r=[bS], w=[ectok])
        DVE(lambda e: e.tensor_copy(out=thrtok.ap.rearrange("p a b -> p (a b)"), in_=bS.ap[:, 128:196]), r=[bS], w=[thrtok])
        DVE(lambda e: e.tensor_tensor(out=gdiag.ap, in0=gall.ap.unsqueeze(2).to_broadcast([4, 32, 4]),
                                      in1=ident32.ap[0:4, 0:4].unsqueeze(1).to_broadcast([4, 32, 4]), op=ALU.mult), r=[gall, ident32], w=[gdiag])
        MM([lambda e: e.matmul(bO.ap[:, 0:128], lhsT=ones32.ap[0:4, :], rhs=gdiag.ap.rearrange("p a b -> p (a b)"), start=True, stop=True)],
           r=[ones32, gdiag], w=[bO])
        DVE(lambda e: e.tensor_copy(out=gbc.ap.rearrange("p a b -> p (a b)"), in_=bO.ap[:, 0:128]), r=[bO], w=[gbc])

        S.barrier()
        AR.reset()
        muX = AR.alloc(128, (2, 520), BF16)
        extS = AR.alloc(128, (2, 16, 7), BF16)
        shS = AR.alloc(128, (2, 4, 64), BF16)
        scr = AR.alloc(128, (1024,), F32)
        m_ez = Tl(scr.ap[:, 0:512]); m_ez.r = scr.r
        m_sg = Tl(scr.ap[:, 512:1024]); m_sg.r = scr.r
        xcT = AR.alloc(128, (2, 512), BF16)
        mu32 = AR.alloc(128, (2, 128), F32); mutok = AR.alloc(128, (256,), F32)
        bufT = AR.alloc(128, (8, 48), BF16)
        MO = []
        for _ in range(2):
            MO.append(dict(smz=AR.alloc(128, (2, 512), BF16), sxc=AR.alloc(128, (2, 512), BF16), qT=AR.alloc(128, (2, 512), BF16),
                           kT=AR.alloc(128, (2, 512), BF16), kp4=AR.alloc(128, (4, 256), BF16), vext4=AR.alloc(128, (4, 258), BF16),
                           so4=AR.alloc(128, (4, 256), BF16)))
        STm4 = AR.alloc(128, (4, 128), BF16); Cgb = AR.alloc(128, (2, 257), BF16); Cst = AR.alloc(128, (2, 257), F32)
        big = AR.alloc(128, (4056,), F32)
        def bview(off, rows, free, dt):
            nel = int(np.prod(free)); nw = (nel + 1) // 2 if dt == BF16 else nel
            v = big.ap[0:rows, off:off + nw]
            if dt == BF16:
                v = v.bitcast(BF16)
            v = v[:, 0:nel]
            if len(free) == 2:
                v = v.rearrange("p (a b) -> p a b", a=free[0])
            elif len(free) == 3:
                v = v.rearrange("p (a b c) -> p a b c", a=free[0], b=free[1])
            return Tl(v)

        def alias_sync(src, dst):
            evs = []
            for t in src:
                if t.r.last_write is not None:
                    evs.append(t.r.last_write)
                evs.extend(t.r.reads)
            for t in dst:
                t.r.reads = list(t.r.reads) + evs
        ho4 = bview(0, 128, (4, 256), F32); sq = bview(1024, 128, (1024,), F32); hn4 = bview(2048, 128, (4, 256), BF16)
        t1 = bview(2560, 128, (512,), F32)
        Cin = [bview(i * 516, 128, (1, 2, 257), F32) for i in range(4)]
        Cgq2 = [bview(2064, 128, (1, 2, 257), BF16), bview(3560, 128, (1, 2, 257), BF16)]
        qx2 = [bview(2328, 128, (2, 1, 64), BF16), bview(3824, 128, (2, 1, 64), BF16)]
        kx2 = [bview(2392, 64, (1, 256), BF16), bview(3888, 64, (1, 256), BF16)]
        gnold = bview(4016, 128, (2, 16), F32)
        ntok = bview(2520, 16, (256,), F32); nout = bview(2776, 32, (128,), F32); t1s = bview(2904, 128, (128,), F32)
        ho_s = bview(3032, 128, (256,), F32); hn_s = bview(3288, 128, (256,), BF16); ncol = bview(3416, 128, (2, 16), F32)
        junk3 = bview(3448, 128, (208,), BF16)
        bufrow = bview(0, 48, (1024,), F32)
        prompt_views = [ho4, sq, hn4, t1]
        sample_views = Cin + Cgq2 + qx2 + kx2 + [gnold, ntok, nout, t1s, ho_s, hn_s, ncol, junk3, bufrow]
        wm = Tl(wbuf.ap.rearrange("p (k s c) -> p k s c", k=8, s=3))
        wm.r = wbuf.r
        for O in MO:
            POOL(lambda e, O=O: e.memset(O["vext4"].ap, 1.0), w=[O["vext4"]])
        DMA(lambda e: e.dma_start(out=bufrow.ap, in_=scv), w=[bufrow])
        for half in range(2):
            MM([lambda e, k=k: e.transpose(out=bS.ap[:, (k % 4) * 48:(k % 4) * 48 + 48], in_=bufrow.ap[:, k * 128:(k + 1) * 128], identity=ident32.ap[0:48, 0:48])
                for k in range(half * 4, half * 4 + 4)], r=[bufrow, ident32], w=[bS])
            DVE(lambda e, half=half: e.tensor_copy(out=bufT.ap[:, half * 4:half * 4 + 4, :], in_=bS.ap[:, 0:192].rearrange("p (a b) -> p a b", a=4)), r=[bS], w=[bufT])

        def m_stage1(hm, gi, tok0, T, O):
            kc0 = 2 * hm
            sample = (T == 64)
            rows = 64 if sample else 128
            nch = 1 if sample else 4
            smz, sxc, qT, kT, kp4, vext4, so4 = O["smz"], O["sxc"], O["qT"], O["kT"], O["kp4"], O["vext4"], O["so4"]
            if gi == 0:
                DMAC(lambda e: e.dma_start(out=wbuf.ap, in_=w_md[hm]), w=[wm])
                POOL(lambda e: e.memset(muX.ap[:, :, 0:3], 0.0), w=[muX]); yield
            for kc, bank in ((0, bA), (1, bB)):
                MM([lambda e, k=k, kc=kc, bank=bank: e.matmul(bank.ap[:, 0:T], lhsT=wm.ap[:, k, 0, kc * 128:(kc + 1) * 128], rhs=hT.ap[:, k, tok0:tok0 + T],
                                                              start=(k == 0), stop=(k == 7)) for k in range(8)], r=[wm, hT], w=[bank]); yield
            for kc, bank in ((0, bC), (1, bD)):
                MM([lambda e, k=k, kc=kc, bank=bank: e.matmul(bank.ap[:, 0:T], lhsT=wm.ap[:, k, 1, kc * 128:(kc + 1) * 128], rhs=hT.ap[:, k, tok0:tok0 + T],
                                                              start=(k == 0), stop=(k == 7)) for k in range(8)], r=[wm, hT], w=[bank]); yield
            for kc, bank in ((0, bA), (1, bB)):
                ACT(lambda e, kc=kc, bank=bank: e.activation(out=muX.ap[:, kc, 3:3 + T], in_=bank.ap[:, 0:T], func=AF.Identity, bias=P.ap[:, kc0 + kc, 4:5]),
                    r=[bank, P], w=[muX]); yield
                if sample:
                    ACT(lambda e, kc=kc, bank=bank: e.activation(out=extS.ap[:, kc, :, 3:7], in_=bank.ap[:, 0:64].rearrange("p (b t) -> p b t", t=4),
                                                                 func=AF.Identity, bias=P.ap[:, kc0 + kc, 4:5]), r=[bank, P], w=[extS])
                    DVE(lambda e, kc=kc: e.tensor_copy(out=extS.ap[:, kc, :, 0:3], in_=bufT.ap[:, kc0 + kc, :].rearrange("p (b i) -> p b i", i=3)),
                        r=[bufT, extS], w=[extS]); yield
                    for i in range(4):
                        DVE(lambda e, kc=kc, i=i: e.tensor_copy(out=shS.ap[:, kc, i, :].rearrange("p (b t) -> p b t", t=4), in_=extS.ap[:, kc, :, i:i + 4]),
                            r=[extS], w=[shS])
                    yield
                if sample or tok0 == 1536:
                    c_lo = 0 if sample else 384
                    ACT(lambda e, kc=kc, bank=bank, c_lo=c_lo: e.activation(out=mu32.ap[:, kc, 0:rows], in_=bank.ap[:, c_lo:c_lo + rows], func=AF.Identity,
                                                                          bias=P.ap[:, kc0 + kc, 4:5]), r=[bank, P], w=[mu32]); yield
            for kc, bank in ((0, bC), (1, bD)):
                yield from silu_gen(bank, Pn.ap[:, kc0 + kc, 5:6], P.ap[:, kc0 + kc, 5:6], T, m_ez, m_sg, smz.ap[:, kc, 0:T], smz)
            for kc, bank in ((0, bA), (1, bB)):
                if not sample:
                    MM([lambda e, i=i, kc=kc, bank=bank: e.matmul(bank.ap[:, 0:T], lhsT=convd.ap[:, kc0 + kc, i, :], rhs=muX.ap[:, kc, i:i + T],
                                                                  start=(i == 0), stop=(i == 3)) for i in range(4)], r=[convd, muX], w=[bank]); yield
                else:
                    MM([lambda e, i=i, kc=kc, bank=bank: e.matmul(bank.ap[:, 0:64], lhsT=convd.ap[:, kc0 + kc, i, :],
                                                                  rhs=shS.ap[:, kc, i, :], start=(i == 0), stop=(i == 3)) for i in range(4)],
                       r=[convd, shS], w=[bank]); yield
                yield from silu_gen(bank, Pn.ap[:, kc0 + kc, 15:16], P.ap[:, kc0 + kc, 15:16], T, m_ez, m_sg, xcT.ap[:, kc, 0:T], xcT)
                DVE(lambda e, kc=kc: e.tensor_scalar(out=sxc.ap[:, kc, 0:T], in0=xcT.ap[:, kc, 0:T], scalar1=P.ap[:, kc0 + kc, 17:18], scalar2=None, op0=ALU.mult),
                    r=[xcT, P], w=[sxc]); yield
            for kc, bank in ((0, bC), (1, bD)):
                MM([lambda e, kc=kc, bank=bank: e.matmul(bank.ap[:, 0:T], lhsT=wbd.ap[:, 0, kc0 + kc, :], rhs=xcT.ap[:, kc, 0:T], start=True, stop=True)],
                   r=[wbd, xcT], w=[bank]); yield
                DVE(lambda e, kc=kc, bank=bank: e.tensor_copy(out=qT.ap[:, kc, 0:T], in_=bank.ap[:, 0:T]), r=[bank], w=[qT]); yield
            for kc, bank in ((0, bA), (1, bB)):
                MM([lambda e, kc=kc, bank=bank: e.matmul(bank.ap[:, 0:T], lhsT=wbd.ap[:, 1, kc0 + kc, :], rhs=xcT.ap[:, kc, 0:T], start=True, stop=True)],
                   r=[wbd, xcT], w=[bank]); yield
                DVE(lambda e, kc=kc, bank=bank: e.tensor_copy(out=kT.ap[:, kc, 0:T], in_=bank.ap[:, 0:T]), r=[bank], w=[kT]); yield
            npair = (nch + 1) // 2
            for pr in range(npair):
                bank = bC if pr == 0 else bD
                nin = min(2, nch - pr * 2)
                fns = []
                for ii in range(nin):
                    i = pr * 2 + ii
                    for k in range(8):
                        fns.append(lambda e, i=i, ii=ii, k=k, bank=bank: e.matmul(bank.ap[0:rows, ii * 256:(ii + 1) * 256],
                                                                                 lhsT=hT.ap[:, k, tok0 + i * 128:tok0 + i * 128 + rows], rhs=wm.ap[:, k, 2, :],
                                                                                 start=(k == 0), stop=(k == 7)))
                MM(fns, r=[hT, wm], w=[bank]); yield
                DVE(lambda e, pr=pr, nin=nin, bank=bank: e.tensor_tensor(
                    out=scr.ap[0:rows, pr * 512:pr * 512 + nin * 256].rearrange("p (a b) -> p a b", a=nin),
                    in0=bank.ap[0:rows, 0:nin * 256].rearrange("p (a b) -> p a b", a=nin),
                    in1=bc_mo.ap[0:rows, hm * 256:(hm + 1) * 256].unsqueeze(1).to_broadcast([rows, nin, 256]), op=ALU.add), r=[bank, bc_mo], w=[scr]); yield
            W = nch * 256
            ACT(lambda e: e.activation(out=scr.ap[0:rows, 0:W], in_=scr.ap[0:rows, 0:W], func=AF.Exp, scale=-1.0), r=[scr], w=[scr]); yield
            ACT(lambda e: e.activation(out=scr.ap[0:rows, 0:W], in_=scr.ap[0:rows, 0:W], func=AF.Ln, bias=1.0), r=[scr], w=[scr]); yield
            ACT(lambda e: e.activation(out=so4.ap[0:rows, 0:nch, :].rearrange("p a b -> p (a b)"), in_=scr.ap[0:rows, 0:W], func=AF.Exp, scale=-1.0),
                r=[scr], w=[so4]); yield
            for i in range(nch):
                bank = (bA, bB, bC, bD)[i]
                ci = 16 if sample else gi * 4 + i
                c0 = i * 128
                fns = []
                for kc in range(2):
                    fns.append(lambda e, kc=kc, c0=c0, bank=bank: e.matmul(bank.ap[0:rows, kc * 128:(kc + 1) * 128], lhsT=xcT.ap[:, kc, c0:c0 + rows],
                                                                          rhs=wbd.ap[:, 1, kc0 + kc, :], start=True, stop=True))
                for kc in range(2):
                    fns.append(lambda e, kc=kc, c0=c0, bank=bank: e.matmul(bank.ap[0:rows, 256 + kc * 128:256 + (kc + 1) * 128],
                                                                          lhsT=muX.ap[:, kc, 3 + c0:3 + c0 + rows], rhs=wbd.ap[:, 2, kc0 + kc, :], start=True, stop=True))
                MM(fns, r=[xcT, muX, wbd], w=[bank]); yield
                ACT(lambda e, i=i, ci=ci, bank=bank: e.activation(out=kp4.ap[0:rows, i, :], in_=bank.ap[0:rows, 0:256], func=AF.Copy,
                                                                  scale=ectok.ap[0:rows, ci, hm:hm + 1]), r=[bank, ectok], w=[kp4]); yield
                DVE(lambda e, i=i, bank=bank: e.tensor_copy(out=vext4.ap[0:rows, i, 0:256], in_=bank.ap[0:rows, 256:512]), r=[bank], w=[vext4]); yield
            if sample or tok0 == 1536:
                MM([lambda e, kc=kc: e.transpose(out=bA.ap[0:rows, kc * 128:(kc + 1) * 128], in_=mu32.ap[:, kc, 0:rows], identity=ident32.ap) for kc in range(2)],
                   r=[mu32, ident32], w=[bA]); yield
                DVE(lambda e: e.tensor_copy(out=mutok.ap[0:rows, :], in_=bA.ap[0:rows, 0:256]), r=[bA], w=[mutok]); yield
                if sample:
                    for b in range(16):
                        DMA(lambda e, b=b: e.dma_start(out=ocs[b, :, hm * 256:(hm + 1) * 256], in_=mutok.ap[b * 4 + 1:b * 4 + 4, :]), r=[mutok], out=True)
                else:
                    DMA(lambda e: e.dma_start(out=ocp[:, hm * 256:(hm + 1) * 256], in_=mutok.ap[125:128, :]), r=[mutok], out=True)
                yield
            if not sample:
                DVE(lambda e: e.tensor_copy(out=muX.ap[:, :, 0:3], in_=muX.ap[:, :, 512:515]), r=[muX], w=[muX]); yield

        def m_epilogue_s(hm, O):
            rows, tok0, ci = 64, 2048, 16
            kc0 = 2 * hm
            so = O["so4"]; sxc = O["sxc"]; smz = O["smz"]
            DVE(lambda e: e.tensor_copy(out=sm8.ap[0:rows, 3:4], in_=bO.ap[0:rows, 256:257]), r=[bO], w=[sm8]); yield
            DVE(lambda e: e.scalar_tensor_tensor(out=sm8.ap[0:rows, 4:5], in0=sm8.ap[0:rows, 3:4], scalar=-1.0, in1=sm8.ap[0:rows, 3:4],
                                                 op0=ALU.mult, op1=ALU.max), r=[sm8], w=[sm8]); yield
            DVE(lambda e: e.tensor_tensor(out=sm8.ap[0:rows, 5:6], in0=sm8.ap[0:rows, 4:5], in1=thrtok.ap[0:rows, ci, hm:hm + 1], op=ALU.max),
                r=[sm8, thrtok], w=[sm8]); yield
            DVE(lambda e: e.reciprocal(out=sm8.ap[0:rows, 6:7], in_=sm8.ap[0:rows, 5:6]), r=[sm8], w=[sm8]); yield
            DVE(lambda e: e.scalar_tensor_tensor(out=ho_s.ap[0:rows, :], in0=bO.ap[0:rows, 0:256], scalar=sm8.ap[0:rows, 6:7], in1=so.ap[0:rows, 0, :],
                                                 op0=ALU.mult, op1=ALU.mult), r=[bO, sm8, so], w=[ho_s]); yield
            ACT(lambda e: e.activation(out=junk3.ap[0:rows, 0:256] if False else hn_s.ap[0:rows, :], in_=ho_s.ap[0:rows, :], func=AF.Square, scale=1.0 / 16.0,
                                       accum_out=sm8.ap[0:rows, 7:8]), r=[ho_s], w=[hn_s, sm8]); yield
            ACT(lambda e: e.activation(out=sm8.ap[0:rows, 8:9], in_=sm8.ap[0:rows, 7:8], func=AF.Ln, bias=EPS), r=[sm8], w=[sm8]); yield
            ACT(lambda e: e.activation(out=sm8.ap[0:rows, 9:10], in_=sm8.ap[0:rows, 8:9], func=AF.Exp, scale=-0.5), r=[sm8], w=[sm8]); yield
            ACT(lambda e: e.activation(out=hn_s.ap[0:rows, :], in_=ho_s.ap[0:rows, :], func=AF.Copy, scale=sm8.ap[0:rows, 9:10]), r=[ho_s, sm8], w=[hn_s]); yield
            MM([lambda e, kc=kc: e.transpose(out=bT.ap[:, kc * 128:kc * 128 + rows], in_=hn_s.ap[0:rows, kc * 128:(kc + 1) * 128], identity=identb.ap[0:rows, 0:rows])
                for kc in range(2)], r=[hn_s, identb], w=[bT]); yield
            for kc in range(2):
                DVE(lambda e, kc=kc: e.scalar_tensor_tensor(out=t1s.ap[:, 0:rows], in0=bT.ap[:, kc * 128:kc * 128 + rows], scalar=P.ap[:, kc0 + kc, 16:17],
                                                            in1=sxc.ap[:, kc, 0:rows], op0=ALU.mult, op1=ALU.add), r=[bT, P, sxc], w=[t1s]); yield
                DVE(lambda e, kc=kc: e.tensor_tensor(out=mixT.ap[:, 8 + kc0 + kc, tok0:tok0 + rows], in0=t1s.ap[:, 0:rows], in1=smz.ap[:, kc, 0:rows],
                                                     op=ALU.mult), r=[t1s, smz], w=[mixT]); yield

        def m_stage2(hm, gi, tok0, T, O):
            kc0 = 2 * hm
            sample = (T == 64)
            smz, sxc, qT, kT, kp4, vext4, so4 = O["smz"], O["sxc"], O["qT"], O["kT"], O["kp4"], O["vext4"], O["so4"]
            if gi == 0:
                S.alias(sample_views, prompt_views)
                POOL(lambda e: e.memset(Cst.ap, 0.0), w=[Cst]); yield
            if sample:
                S.alias(prompt_views, sample_views)
            if not sample:
                ci0 = gi * 4
                fns = []
                for i in range(4):
                    for kc in range(2):
                        fns.append(lambda e, i=i, kc=kc: e.matmul(bS.ap[:, i * 128:(i + 1) * 128], lhsT=kT.ap[:, kc, i * 128:(i + 1) * 128],
                                                                  rhs=qT.ap[:, kc, i * 128:(i + 1) * 128], start=(kc == 0), stop=(kc == 1)))
                MM(fns, r=[kT, qT], w=[bS]); yield
                for i in range(4):
                    DVE(lambda e, i=i: e.scalar_tensor_tensor(out=STm4.ap[:, i, :], in0=bS.ap[:, i * 128:(i + 1) * 128], scalar=ectok.ap[:, ci0 + i, hm:hm + 1],
                                                              in1=cmask.ap, op0=ALU.mult, op1=ALU.mult), r=[bS, ectok, cmask], w=[STm4]); yield
                for i in range(4):
                    ci = ci0 + i
                    c0 = i * 128
                    bN = bO if i % 2 == 0 else bS
                    ACT(lambda e, ci=ci: e.activation(out=Cgb.ap, in_=Cst.ap, func=AF.Copy, scale=gbc.ap[:, ci, hm:hm + 1]), r=[Cst, gbc], w=[Cgb]); yield
                    MM([lambda e, i=i, bN=bN: e.matmul(bN.ap[:, 0:257], lhsT=STm4.ap[:, i, :], rhs=vext4.ap[:, i, 0:257], start=True, stop=False)] +
                       [lambda e, kc=kc, c0=c0, bN=bN: e.matmul(bN.ap[:, 0:257], lhsT=qT.ap[:, kc, c0:c0 + 128], rhs=Cgb.ap[:, kc, :], start=False, stop=(kc == 1))
                        for kc in range(2)], r=[STm4, vext4, qT, Cgb], w=[bN]); yield
                    MM([lambda e, i=i, kc=kc: e.matmul(bU.ap[:, kc * 256:(kc + 1) * 256], lhsT=kp4.ap[:, i, kc * 128:(kc + 1) * 128], rhs=vext4.ap[:, i, 0:256],
                                                       start=True, stop=True) for kc in range(2)] +
                       [lambda e, i=i, kc=kc, bN=bN: e.matmul(bN.ap[:, 300 + kc:301 + kc], lhsT=kp4.ap[:, i, kc * 128:(kc + 1) * 128], rhs=vext4.ap[:, i, 256:257],
                                                              start=True, stop=True) for kc in range(2)], r=[kp4, vext4], w=[bU, bN]); yield
                    DVE(lambda e, i=i, bN=bN: e.tensor_tensor(out=ho4.ap[:, i, :], in0=bN.ap[:, 0:256], in1=so4.ap[:, i, :], op=ALU.mult), r=[bN, so4], w=[ho4]); yield
                    DVE(lambda e, i=i, bN=bN: e.tensor_copy(out=sm8.ap[:, 12 + i:13 + i], in_=bN.ap[:, 256:257]), r=[bN], w=[sm8]); yield
                    DVE(lambda e, ci=ci: e.scalar_tensor_tensor(out=Cst.ap[:, :, 0:256], in0=Cst.ap[:, :, 0:256], scalar=gbc.ap[:, ci, hm:hm + 1],
                                                                in1=bU.ap.rearrange("p (a b) -> p a b", a=2), op0=ALU.mult, op1=ALU.add), r=[Cst, gbc, bU], w=[Cst]); yield
                    DVE(lambda e, ci=ci, bN=bN: e.scalar_tensor_tensor(out=Cst.ap[:, :, 256], in0=Cst.ap[:, :, 256], scalar=gbc.ap[:, ci, hm:hm + 1],
                                                                       in1=bN.ap[:, 300:302], op0=ALU.mult, op1=ALU.add), r=[Cst, gbc, bN], w=[Cst]); yield
                DVE(lambda e: e.scalar_tensor_tensor(out=sm8.ap[:, 0:4], in0=sm8.ap[:, 12:16], scalar=-1.0, in1=sm8.ap[:, 12:16], op0=ALU.mult, op1=ALU.max),
                    r=[sm8], w=[sm8]); yield
                DVE(lambda e: e.tensor_tensor(out=sm8.ap[:, 4:8], in0=sm8.ap[:, 0:4], in1=thrtok.ap[:, ci0:ci0 + 4, hm], op=ALU.max), r=[sm8, thrtok], w=[sm8]); yield
                DVE(lambda e: e.reciprocal(out=sm8.ap[:, 8:12], in_=sm8.ap[:, 4:8]), r=[sm8], w=[sm8]); yield
                ACT(lambda e: e.activation(out=sq.ap, in_=ho4.ap.rearrange("p a b -> p (a b)"), func=AF.Square, scale=1.0 / 16.0), r=[ho4], w=[sq]); yield
                DVE(lambda e: e.reduce_sum(out=sm8.ap[:, 0:4], in_=sq.ap.rearrange("p (a b) -> p a b", a=4), axis=AX.X), r=[sq, sm8], w=[sm8]); yield
                DVE(lambda e: e.tensor_tensor(out=sm8.ap[:, 4:8], in0=sm8.ap[:, 8:12], in1=sm8.ap[:, 8:12], op=ALU.mult), r=[sm8], w=[sm8]); yield
                DVE(lambda e: e.tensor_tensor(out=sm8.ap[:, 4:8], in0=sm8.ap[:, 4:8], in1=sm8.ap[:, 0:4], op=ALU.mult), r=[sm8], w=[sm8]); yield
                ACT(lambda e: e.activation(out=sm8.ap[:, 4:8], in_=sm8.ap[:, 4:8], func=AF.Ln, bias=EPS), r=[sm8], w=[sm8]); yield
                ACT(lambda e: e.activation(out=sm8.ap[:, 4:8], in_=sm8.ap[:, 4:8], func=AF.Exp, scale=-0.5), r=[sm8], w=[sm8]); yield
                DVE(lambda e: e.tensor_tensor(out=sm8.ap[:, 0:4], in0=sm8.ap[:, 4:8], in1=sm8.ap[:, 8:12], op=ALU.mult), r=[sm8], w=[sm8]); yield
                DVE(lambda e: e.tensor_tensor(out=hn4.ap, in0=ho4.ap, in1=sm8.ap[:, 0:4].unsqueeze(2).to_broadcast([128, 4, 256]), op=ALU.mult),
                    r=[ho4, sm8], w=[hn4]); yield
                MM([lambda e, i=i, kc=kc: e.transpose(out=bT.ap[:, kc * 512 + i * 128:kc * 512 + (i + 1) * 128], in_=hn4.ap[:, i, kc * 128:(kc + 1) * 128],
                                                      identity=identb.ap) for i in range(4) for kc in range(2)], r=[hn4, identb], w=[bT]); yield
                for kc in range(2):
                    DVE(lambda e, kc=kc: e.scalar_tensor_tensor(out=t1.ap, in0=bT.ap[:, kc * 512:(kc + 1) * 512], scalar=P.ap[:, kc0 + kc, 16:17],
                                                                in1=sxc.ap[:, kc, :], op0=ALU.mult, op1=ALU.add), r=[bT, P, sxc], w=[t1]); yield
                    DVE(lambda e, kc=kc: e.tensor_tensor(out=mixT.ap[:, 8 + kc0 + kc, tok0:tok0 + 512], in0=t1.ap, in1=smz.ap[:, kc, :], op=ALU.mult),
                        r=[t1, smz], w=[mixT]); yield
                if gi == 3:
                    DMA(lambda e: e.dma_start(out=oCp[hm].rearrange("(kc p) v -> p kc v", p=128), in_=Cst.ap[:, :, 0:256]), r=[Cst], out=True)
                    DMA(lambda e: e.dma_start(out=onp[hm].rearrange("(kc p) -> p kc", p=128), in_=Cst.ap[:, :, 256], allow_slow_non_contiguous=True),
                        r=[Cst], out=True); yield
            else:
                rows = 64
                MM([lambda e, kc=kc: e.matmul(bS.ap[0:64, 0:64], lhsT=kT.ap[:, kc, 0:64], rhs=qT.ap[:, kc, 0:64], start=(kc == 0), stop=(kc == 1))
                    for kc in range(2)], r=[kT, qT], w=[bS]); yield
                DVE(lambda e: e.scalar_tensor_tensor(out=STm4.ap[0:64, 0, 0:64], in0=bS.ap[0:64, 0:64], scalar=ectok.ap[0:64, 16, hm:hm + 1],
                                                     in1=smask.ap, op0=ALU.mult, op1=ALU.mult), r=[bS, ectok, smask], w=[STm4]); yield
                DMA(lambda e: e.dma_start(out=ntok.ap, in_=sn[:, hm, :]), w=[ntok]); yield
                MM([lambda e, kc=kc: e.transpose(out=bS.ap[:, 256 + kc * 16:256 + (kc + 1) * 16], in_=ntok.ap[:, kc * 128:(kc + 1) * 128], identity=ident32.ap[0:16, 0:16])
                    for kc in range(2)], r=[ntok, ident32, STm4], w=[bS]); yield
                first = True
                NS = 4
                nold = bS.ap[:, 256:288].rearrange("p (kc b) -> p kc b", kc=2)
                DVE(lambda e: e.tensor_tensor(out=gnold.ap, in0=nold, in1=gbc.ap[:, 16:32, hm].unsqueeze(1).to_broadcast([128, 2, 16]), op=ALU.mult),
                    r=[bS, gbc], w=[gnold]); yield
                MM([lambda e, kc=kc: e.matmul(bS.ap[:, 320 + kc * 16:320 + (kc + 1) * 16], lhsT=kp4.ap[0:64, 0, kc * 128:(kc + 1) * 128], rhs=rowmask.ap,
                                              start=True, stop=True) for kc in range(2)], r=[kp4, rowmask, gnold], w=[bS]); yield
                DVE(lambda e: e.tensor_tensor(out=ncol.ap, in0=bS.ap[:, 320:352].rearrange("p (kc b) -> p kc b", kc=2), in1=gnold.ap, op=ALU.add),
                    r=[bS, gnold], w=[ncol]); yield

                def load_round(b):
                    C_l = Cin[b % NS]
                    DMA(lambda e, b=b, C_l=C_l: e.dma_start(out=C_l.ap[:, 0, :, 0:256], in_=sC[b, hm].rearrange("(kc p) v -> p kc v", p=128)), w=[C_l])
                for b in range(NS - 1):
                    load_round(b)
                yield
                for b in range(16):
                    C_t = Cin[b % NS]
                    Cg_t = Cgq2[b % 2]; qx_t = qx2[b % 2]; kx_t = kx2[b % 2]
                    Ub = bU if b % 2 == 0 else bTf
                    if b + NS - 1 < 16:
                        load_round(b + NS - 1)
                    ACT(lambda e, b=b, C_t=C_t, Cg_t=Cg_t: e.activation(out=Cg_t.ap[:, 0, :, 0:256], in_=C_t.ap[:, 0, :, 0:256], func=AF.Copy,
                                                                        scale=gbc.ap[:, 16 + b, hm:hm + 1]), r=[C_t, gbc], w=[Cg_t]); yield
                    DVE(lambda e, b=b, Cg_t=Cg_t: e.tensor_copy(out=Cg_t.ap[:, 0, :, 256], in_=gnold.ap[:, :, b]), r=[gnold, Cg_t], w=[Cg_t]); yield
                    DVE(lambda e, b=b, qx_t=qx_t: e.tensor_tensor(out=qx_t.ap[:, :, 0, :], in0=qT.ap[:, :, 0:64],
                                                                  in1=colmask.ap[:, b:b + 1, :].to_broadcast([128, 2, 64]), op=ALU.mult), r=[qT, colmask], w=[qx_t]); yield
                    DVE(lambda e, b=b, kx_t=kx_t: e.tensor_tensor(out=kx_t.ap[:, 0, :], in0=kp4.ap[0:64, 0, :],
                                                                  in1=rowmask.ap[:, b:b + 1].to_broadcast([64, 256]), op=ALU.mult), r=[kp4, rowmask], w=[kx_t]); yield
                    fns = []
                    if first:
                        fns.append(lambda e: e.matmul(bO.ap[0:64, 0:257], lhsT=STm4.ap[0:64, 0, 0:64], rhs=vext4.ap[0:64, 0, 0:257], start=True, stop=False))
                        first = False
                    for kc in range(2):
                        last = (b == 15 and kc == 1)
                        fns.append(lambda e, kc=kc, last=last, qx_t=qx_t, Cg_t=Cg_t: e.matmul(bO.ap[0:64, 0:257], lhsT=qx_t.ap[:, kc, 0, :], rhs=Cg_t.ap[:, 0, kc, :],
                                                                                             start=False, stop=last))
                    MM(fns, r=[STm4, vext4, qx_t, Cg_t], w=[bO]); yield
                    MM([lambda e, kc=kc, kx_t=kx_t, Ub=Ub: e.matmul(Ub.ap[:, kc * 256:(kc + 1) * 256], lhsT=kx_t.ap[:, 0, kc * 128:(kc + 1) * 128],
                                                                    rhs=vext4.ap[0:64, 0, 0:256], start=True, stop=True) for kc in range(2)], r=[kx_t, vext4], w=[Ub]); yield
                    DVE(lambda e, b=b, C_t=C_t, Ub=Ub: e.scalar_tensor_tensor(out=C_t.ap[:, 0, :, 0:256], in0=C_t.ap[:, 0, :, 0:256], scalar=gbc.ap[:, 16 + b, hm:hm + 1],
                                                                             in1=Ub.ap.rearrange("p (a b) -> p a b", a=2), op0=ALU.mult, op1=ALU.add),
                        r=[C_t, gbc, Ub], w=[C_t]); yield
                    DMA(lambda e, b=b, C_t=C_t: e.dma_start(out=oCs[b, hm].rearrange("(kc p) v -> p kc v", p=128), in_=C_t.ap[:, 0, :, 0:256]), r=[C_t], out=True); yield
                MM([lambda e: e.transpose(out=bU.ap[0:32, 0:128], in_=ncol.ap.rearrange("p kc b -> p (kc b)"), identity=ident32.ap)], r=[ncol, ident32], w=[bU]); yield
                DVE(lambda e: e.tensor_copy(out=nout.ap[0:32, :], in_=bU.ap[0:32, 0:128]), r=[bU], w=[nout]); yield
                for kc in range(2):
                    DMA(lambda e, kc=kc: e.dma_start(out=ons[:, hm, kc * 128:(kc + 1) * 128], in_=nout.ap[kc * 16:(kc + 1) * 16, :]), r=[nout], out=True)
                yield
                yield from m_epilogue_s(hm, O)

        m_units = [(hm, gi, tok0, T) for hm in range(4) for gi, (tok0, T) in enumerate(GROUPS)]
        S.begin_window()
        for n, u in enumerate(m_units):
            interleave(m_stage1(*u, MO[n % 2]))
            interleave(m_stage2(*u, MO[n % 2]))
        S.end_window()

        S.barrier()
        AR.reset()
        wo = Tl(hT.ap.rearrange("p a b -> p (a b)")[:, 0:16384].rearrange("p (k c) -> p k c", k=16))
        wo.r = hT.r
        for k4 in range(4):
            DMAC(lambda e, k4=k4: e.dma_start(out=wo.ap[:, k4 * 4:(k4 + 1) * 4, :], in_=w_out_v[:, k4 * 4:(k4 + 1) * 4, :]), w=[wo])
        gate_tok = AR.alloc(17, (1024,), F32)
        selg = AR.alloc(17, (192,), F32)
        gbp = AR.alloc(128, (1024,), F32); gbs = AR.alloc(64, (1024,), F32); fgb = AR.alloc(128, (1024,), F32)
        prm2 = AR.alloc(32, (1024,), F32); sel2 = AR.alloc(32, (3, 128), F32)
        x4 = [AR.alloc(128, (1024,), F32) for _ in range(2)]
        res4 = [AR.alloc(128, (1024,), F32) for _ in range(2)]
        y4 = [AR.alloc(128, (1024,), F32) for _ in range(2)]
        junk4 = AR.alloc(128, (1024,), BF16)
        st4 = AR.alloc(128, (8,), F32)
        DMA(lambda e: e.dma_start(out=selg.ap, in_=selg_d), w=[selg])
        DMA(lambda e: e.dma_start(out=prm2.ap, in_=prm), w=[prm2])
        DMA(lambda e: e.dma_start(out=sel2.ap, in_=sel_d), w=[sel2])
        for half, bank in ((0, bA), (1, bB)):
            MM([lambda e, k=k, bank=bank: e.transpose(out=bank.ap[0:17, (k % 4) * 128:(k % 4 + 1) * 128], in_=modT.ap[:, 16 + k, :], identity=ident32.ap)
                for k in range(half * 4, half * 4 + 4)], r=[modT, ident32], w=[bank])
            DVE(lambda e, half=half, bank=bank: e.tensor_copy(out=gate_tok.ap[:, half * 512:(half + 1) * 512], in_=bank.ap[0:17, :]), r=[bank], w=[gate_tok])
        for half, bank in ((0, bA), (1, bB)):
            MM([lambda e, half=half, bank=bank: e.matmul(bank.ap, lhsT=selg.ap[:, 0:128], rhs=gate_tok.ap[:, half * 512:(half + 1) * 512], start=True, stop=True)],
               r=[selg, gate_tok], w=[bank])
            DVE(lambda e, half=half, bank=bank: e.tensor_copy(out=gbp.ap[:, half * 512:(half + 1) * 512], in_=bank.ap), r=[bank], w=[gbp])
        for half, bank in ((0, bA), (1, bB)):
            MM([lambda e, half=half, bank=bank: e.matmul(bank.ap[0:64, :], lhsT=selg.ap[:, 128:192], rhs=gate_tok.ap[:, half * 512:(half + 1) * 512], start=True, stop=True)],
               r=[selg, gate_tok], w=[bank])
            DVE(lambda e, half=half, bank=bank: e.tensor_copy(out=gbs.ap[:, half * 512:(half + 1) * 512], in_=bank.ap[0:64, :]), r=[bank], w=[gbs])
        for half, bank in ((0, bA), (1, bB)):
            MM([lambda e, half=half, bank=bank: e.matmul(bank.ap, lhsT=sel2.ap[:, 2, :], rhs=prm2.ap[:, half * 512:(half + 1) * 512], start=True, stop=True)],
               r=[sel2, prm2], w=[bank])
            DVE(lambda e, half=half, bank=bank: e.tensor_copy(out=fgb.ap[:, half * 512:(half + 1) * 512], in_=bank.ap), r=[bank], w=[fgb])
        def tile4(i):
            rows = 128 if i < 16 else 64
            x_t = x4[i % 2]; r_t = res4[i % 2]; y_t = y4[i % 2]
            gb = gbp if i < 16 else gbs
            src = xp[i * 128:(i + 1) * 128, :] if i < 16 else xs
            dst = yp[i * 128:(i + 1) * 128, :] if i < 16 else ys
            tk = i * 128
            DMA(lambda e, x_t=x_t, src=src, rows=rows: e.dma_start(out=x_t.ap[0:rows, :], in_=src), w=[x_t])
            banks = (bC, bD) if i % 2 == 0 else (bA, bB)
            for half, bank in enumerate(banks):
                MM([lambda e, k=k, half=half, bank=bank: e.matmul(bank.ap[0:rows, :], lhsT=mixT.ap[:, k, tk:tk + rows], rhs=wo.ap[:, k, half * 512:(half + 1) * 512],
                                                                  start=(k == 0), stop=(k == 15)) for k in range(16)], r=[mixT, wo], w=[bank])
                DVE(lambda e, half=half, bank=bank, r_t=r_t, gb=gb: e.tensor_tensor(out=r_t.ap[0:rows, half * 512:(half + 1) * 512], in0=bank.ap[0:rows, :],
                                                                                   in1=gb.ap[0:rows, half * 512:(half + 1) * 512], op=ALU.mult), r=[bank, gb], w=[r_t])
            DVE(lambda e, r_t=r_t, x_t=x_t: e.tensor_tensor(out=r_t.ap[0:rows, :], in0=r_t.ap[0:rows, :], in1=x_t.ap[0:rows, :], op=ALU.add), r=[r_t, x_t], w=[r_t])
            ACT(lambda e, r_t=r_t: e.activation(out=junk4.ap[0:rows, :], in_=r_t.ap[0:rows, :], func=AF.Square, scale=1.0 / 32.0, accum_out=st4.ap[0:rows, 0:1]),
                r=[r_t], w=[junk4, st4])
            ACT(lambda e: e.activation(out=st4.ap[0:rows, 1:2], in_=st4.ap[0:rows, 0:1], func=AF.Ln, bias=EPS), r=[st4], w=[st4])
            ACT(lambda e: e.activation(out=st4.ap[0:rows, 2:3], in_=st4.ap[0:rows, 1:2], func=AF.Exp, scale=-0.5), r=[st4], w=[st4])
            DVE(lambda e, r_t=r_t, y_t=y_t: e.scalar_tensor_tensor(out=y_t.ap[0:rows, :], in0=r_t.ap[0:rows, :], scalar=st4.ap[0:rows, 2:3], in1=fgb.ap[0:rows, :],
                                                                  op0=ALU.mult, op1=ALU.mult), r=[r_t, st4, fgb], w=[y_t])
            DMA(lambda e, y_t=y_t, dst=dst, rows=rows: e.dma_start(out=dst, in_=y_t.ap[0:rows, :]), r=[y_t], out=True)

        for i0 in range(0, 17, 6):
            S.begin_window()
            for i in range(i0, min(17, i0 + 6)):
                tile4(i)
            S.end_window()

        S.finish()
        with nc.Block() as block:
            S.emit(block)
    return nc


_NC_CACHE = {}


def _consts():
    c = {}
    j = np.arange(128)
    c["cmask"] = (j[:, None] <= j[None, :]).astype(np.float32)
    j = np.arange(64)
    c["smask"] = ((j[:, None] <= j[None, :]) & (j[:, None] // 4 == j[None, :] // 4)).astype(np.float32)
    c["rowmask"] = (j[:, None] // 4 == np.arange(16)[None, :]).astype(np.float32)
    c["colmask"] = (np.arange(16)[:, None] == j[None, :] // 4).astype(np.float32).reshape(1, 1024)
    sel = np.zeros((32, 3, 128), np.float32)
    sel[2, 0, :] = 1.0
    sel[6, 1, :] = 1.0
    sel[18, 2, :] = 1.0
    c["sel"] = sel
    selg = np.zeros((17, 192), np.float32)
    selg[0, 0:128] = 1.0
    for t in range(64):
        selg[1 + t // 4, 128 + t] = 1.0
    c["selg"] = selg
    return c


def kernel(x_prompt, x_sample, c_prompt, c_sample, state_hgrn, state_mlstm_C, state_mlstm_n,
           state_mlstm_m, state_mlstm_conv, w_ada, b_ada, norm_g, w_in, b_in, hgrn_lb_logits,
           hgrn_norm_g, mlstm_conv_w, mlstm_conv_b, mlstm_wq, mlstm_wk, mlstm_wv, mlstm_norm_g,
           mlstm_skip, w_out, final_g):
    f = lambda a: np.ascontiguousarray(np.asarray(a, dtype=np.float32))
    x_prompt, x_sample, c_prompt, c_sample = f(x_prompt), f(x_sample), f(c_prompt), f(c_sample)
    state_hgrn, state_mlstm_C, state_mlstm_n = f(state_hgrn), f(state_mlstm_C), f(state_mlstm_n)
    state_mlstm_m, state_mlstm_conv = f(state_mlstm_m), f(state_mlstm_conv)
    w_ada, b_ada, norm_g, w_in, b_in = f(w_ada), f(b_ada), f(norm_g), f(w_in), f(b_in)
    prm = np.zeros((32, 1024), np.float32)
    prm[0:7] = b_in[0, 0:7168].reshape(7, 1024)
    prm[7] = norm_g[0]
    prm[8:10] = f(hgrn_lb_logits)
    prm[10] = f(hgrn_norm_g)[0]
    prm[11:15] = f(mlstm_conv_w)[0]
    prm[15] = f(mlstm_conv_b)[0]
    prm[16] = f(mlstm_norm_g)[0]
    prm[17] = f(mlstm_skip)[0]
    prm[18] = f(final_g)
    bg = np.ascontiguousarray(b_in[0, 7168:7176].reshape(2, 4).T)
    wbd = np.zeros((128, 3, 8, 128), np.float32)
    for wi, wsrc in enumerate((mlstm_wq, mlstm_wk, mlstm_wv)):
        wv = f(wsrc)[0].reshape(8, 32, 4, 4)
        for g in range(32):
            wbd[g * 4:(g + 1) * 4, wi, :, g * 4:(g + 1) * 4] = wv[:, g].transpose(1, 0, 2)
    consts = _consts()
    wv = w_in[0].reshape(8, 128, 7176)
    w_h = np.empty((8, 128, 8, 4, 128), np.float32)
    for s_i, off in enumerate((0, 1024, 2048, 3072)):
        w_h[:, :, :, s_i, :] = wv[:, :, off:off + 1024].reshape(8, 128, 8, 128).transpose(2, 1, 0, 3)
    w_h = w_h.reshape(8, 128, 4096)
    w_m = np.empty((4, 128, 8, 3, 256), np.float32)
    for s_i in range(3):
        off = 4096 + s_i * 1024
        w_m[:, :, :, s_i, :] = wv[:, :, off:off + 1024].reshape(8, 128, 4, 256).transpose(2, 1, 0, 3)
    w_m = w_m.reshape(4, 128, 6144)
    w_g = np.ascontiguousarray(wv[:, :, 7168:7176].transpose(1, 0, 2)).reshape(128, 64)
    if "nc" not in _NC_CACHE:
        _NC_CACHE["nc"] = build_nc()
    nc = _NC_CACHE["nc"]
    in_maps = []
    for c in range(NCORES):
        sl = slice(16 * c, 16 * c + 16)
        m = dict(
            xp=x_prompt[c], xs=x_sample[sl].reshape(64, 1024),
            cc=np.concatenate([c_prompt[c:c + 1], c_sample[sl]], axis=0),
            sh=state_hgrn[0, sl], sC=state_mlstm_C[0, sl], sn=state_mlstm_n[0, sl], sm=state_mlstm_m[0, sl],
            scv=state_mlstm_conv[0, sl].reshape(48, 1024),
            w_ada=w_ada[0], b_ada=b_ada[0].reshape(24, 128), w_out=f(w_out)[0],
            prm=prm, bg=bg, wbd=wbd, w_h=w_h, w_m=w_m, w_g=w_g, **consts)
        in_maps.append({k: np.ascontiguousarray(v) for k, v in m.items()})
    res = run_bass_kernel_spmd(nc, in_maps, core_ids=list(range(NCORES)))
    R = res.results
    cat = lambda k: np.stack([r[k] for r in R], axis=0)
    y_prompt = cat("yp")
    y_sample = np.concatenate([r["ys"].reshape(16, 4, 1024) for r in R], axis=0)
    hgrn_p = cat("ohp")[None]
    C_p = cat("oCp")[None]
    n_p = cat("onp")[None]
    m_p = cat("omp").reshape(8, 4)[None]
    conv_p = cat("ocp")[None]
    hgrn_s = np.concatenate([r["ohs"] for r in R], axis=0)[None]
    C_s = np.concatenate([r["oCs"] for r in R], axis=0)[None]
    n_s = np.concatenate([r["ons"] for r in R], axis=0)[None]
    m_s = np.concatenate([r["oms"] for r in R], axis=0)[None]
    conv_s = np.concatenate([r["ocs"] for r in R], axis=0)[None]
    outs = (y_prompt, y_sample, hgrn_p, C_p, n_p, m_p, conv_p, hgrn_s, C_s, n_s, m_s, conv_s)
    return tuple(np.ascontiguousarray(o, dtype=np.float32) for o in outs)
```

```python
import numpy as np
from contextlib import ExitStack
import concourse.bass as bass
import concourse.mybir as mybir
from concourse.bass_utils import run_bass_kernel_spmd

F32 = mybir.dt.float32
BF16 = mybir.dt.bfloat16
AF = mybir.ActivationFunctionType
ALU = mybir.AluOpType
AX = mybir.AxisListType
NCORES = 8
EPS = 1e-6
LN16 = float(np.log(16.0))


class Res:
    def __init__(self):
        self.last_write = None
        self.reads = []
        self.psum = False


class Tl:
    def __init__(self, ap):
        self.ap = ap
        self.r = Res()

    def __getitem__(self, k):
        return self.ap[k]


class Sched:
    def __init__(self, nc, ctx):
        self.nc = nc
        self.engs = {}
        for name in ["pe", "act", "dve", "pool", "sp"]:
            sem = ctx.enter_context(nc.semaphore("sem_" + name))
            self.engs[name] = dict(name=name, sem=sem, count=0, ops=[], waited={})
        self.dma_ring = {}
        for q in ["sp", "pool"]:
            n = 24
            sems = [ctx.enter_context(nc.semaphore(f"dq_{q}_{i}")) for i in range(n)]
            self.dma_ring[q] = dict(sems=sems, idx=0, vals=[0] * n)
        self.out_events = []
        self.dma_events = []
        self.defer = None

    def _wait(self, e, ev):
        sem, val = ev
        w = e["waited"]
        key = sem.num
        if w.get(key, 0) >= val:
            return
        w[key] = val
        e["ops"].append(("wait", sem, val))

    def _deps(self, e, reads, writes):
        for t in reads:
            r = t.r
            if r.last_write is not None:
                self._wait(e, r.last_write)
            if r.psum:
                for ev in r.reads:
                    if ev[0] is not e["sem"]:
                        self._wait(e, ev)
        for t in writes:
            r = t.r
            if r.last_write is not None:
                self._wait(e, r.last_write)
            for ev in r.reads:
                self._wait(e, ev)

    def _record(self, ev, reads, writes):
        for t in writes:
            t.r.last_write = ev
            t.r.reads = []
        for t in reads:
            if t in writes:
                continue
            t.r.reads = [x for x in t.r.reads if x[0] is not ev[0]] + [ev]

    def op(self, eng, fn, reads=(), writes=()):
        if self.defer is not None:
            self.defer.append(("op", eng, fn, tuple(reads), tuple(writes), False))
            return None
        e = self.engs[eng]
        self._deps(e, reads, writes)
        e["count"] += 1
        ev = (e["sem"], e["count"])
        e["ops"].append(("op", fn, e["sem"], 1))
        self._record(ev, reads, writes)
        return ev

    def mm(self, fns, reads=(), writes=()):
        if self.defer is not None:
            self.defer.append(("mm", "pe", fns, tuple(reads), tuple(writes), False))
            return None
        e = self.engs["pe"]
        self._deps(e, reads, writes)
        for fn in fns[:-1]:
            e["ops"].append(("op", fn, None, 0))
        e["count"] += 1
        ev = (e["sem"], e["count"])
        e["ops"].append(("op", fns[-1], e["sem"], 1))
        self._record(ev, reads, writes)
        return ev

    def dma(self, q, fn, reads=(), writes=(), is_output=False):
        if self.defer is not None:
            self.defer.append(("dma", q, fn, tuple(reads), tuple(writes), is_output))
            return None
        e = self.engs[q]
        ring = self.dma_ring[q]
        i = ring["idx"] % len(ring["sems"])
        ring["idx"] += 1
        sem = ring["sems"][i]
        if ring["vals"][i] > 0:
            self._wait(e, (sem, ring["vals"][i]))
        self._deps(e, reads, writes)
        ring["vals"][i] += 16
        ev = (sem, ring["vals"][i])
        e["ops"].append(("op", fn, sem, 16))
        for t in writes:
            t.r.last_write = ev
            t.r.reads = []
        for t in reads:
            t.r.reads = t.r.reads + [ev]
        self.dma_events.append(ev)
        if is_output:
            self.out_events.append(ev)
        return ev

    def alias(self, src, dst):
        if self.defer is not None:
            self.defer.append(("alias", "none", None, tuple(src), tuple(dst), False))
            return
        evs = []
        for t in src:
            if t.r.last_write is not None:
                evs.append(t.r.last_write)
            evs.extend(t.r.reads)
        for t in dst:
            t.r.reads = list(t.r.reads) + evs

    def begin_window(self):
        assert self.defer is None
        self.defer = []

    @staticmethod
    def _probe_cost(kind, eng, fn):
        class _P:
            def __init__(self):
                self.calls = []
            def __getattr__(self, name):
                def f(*a, **k):
                    self.calls.append((name, a, k))
                    return self
                return f
        fns = fn if kind == "mm" else [fn]
        tot = 0.0
        lat = 0.0
        for f in fns:
            p = _P()
            try:
                f(p)
            except Exception:
                tot += 0.3
                continue
            for (name, a, k) in p.calls:
                out = k.get("out", a[0] if a else None)
                try:
                    shp = tuple(out.shape)
                    n = 1
                    for d in shp[1:]:
                        n *= int(d)
                    nb = n * int(shp[0]) * mybir.dt.size(out.dtype)
                except Exception:
                    n, nb = 256, 65536
                if kind == "mm":
                    slow = 1.0
                    try:
                        if k.get("lhsT", None) is not None and k["lhsT"].dtype == F32:
                            slow = 4.0
                    except Exception:
                        pass
                    tot += max(0.07, slow * n / 2000.0 + 0.02)
                elif kind == "dma":
                    tot += 0.1
                    lat = max(lat, 2.5 + nb / 120e3)
                elif eng == "act":
                    tot += 0.22 + n / 1200.0
                elif eng == "dve":
                    tot += 0.08 + n / 960.0
                else:
                    tot += 0.15 + n / 150.0
        return tot, lat

    def end_window(self):
        ops = self.defer
        self.defer = None
        n = len(ops)
        if n == 0:
            return
        rid = lambda t: id(t.r)
        preds = [set() for _ in range(n)]
        succs = [[] for _ in range(n)]
        last_w = {}
        readers = {}
        psum_ids = set()
        for o in ops:
            for t in o[3] + o[4]:
                if t.r.psum:
                    psum_ids.add(rid(t))
        for j, o in enumerate(ops):
            Rj = set(rid(t) for t in o[3])
            Wj = set(rid(t) for t in o[4])
            for r in Rj | Wj:
                if r in last_w:
                    preds[j].add(last_w[r])
            for r in Rj & psum_ids:
                for i in readers.get(r, ()):
                    if ops[i][1] != o[1]:
                        preds[j].add(i)
            for r in Wj:
                for i in readers.get(r, ()):
                    preds[j].add(i)
            for r in Wj:
                last_w[r] = j
                readers[r] = []
            for r in Rj - Wj:
                readers.setdefault(r, []).append(j)
            preds[j].discard(j)
        preds = [sorted(p) for p in preds]
        for j in range(n):
            for i in preds[j]:
                succs[i].append(j)
        cost = [(0.0, 0.0) if o[0] == "alias" else self._probe_cost(o[0], o[1], o[2]) for o in ops]
        cp = [0.0] * n
        for i in range(n - 1, -1, -1):
            m = 0.0
            for j in succs[i]:
                m = max(m, cp[j])
            cp[i] = cost[i][0] + cost[i][1] + m
        eng_free = {}
        fin = [0.0] * n
        npred = [len(p) for p in preds]
        ready = [i for i in range(n) if npred[i] == 0]
        done = [False] * n
        order = []
        LAT = 0.7
        while ready:
            cands = []
            for i in ready:
                eng = ops[i][1]
                st = eng_free.get(eng, 0.0)
                for p in preds[i]:
                    l = 0.0 if ops[p][1] == eng else LAT
                    st = max(st, fin[p] + l)
                cands.append((st, i))
            min_st = min(c[0] for c in cands)
            best = None
            for st, i in cands:
                if st <= min_st + 0.3:
                    key = (-cp[i], st, i)
                    if best is None or key < best[0]:
                        best = (key, i, st)
            _, i, st = best
            ready.remove(i)
            eng = ops[i][1]
            if eng != "none":
                eng_free[eng] = st + cost[i][0]
            fin[i] = st + cost[i][0] + cost[i][1]
            order.append(i)
            for j in succs[i]:
                npred[j] -= 1
                if npred[j] == 0:
                    ready.append(j)
        assert len(order) == n
        for i in order:
            kind, eng, fn, r, w, is_out = ops[i]
            if kind == "alias":
                self.alias(r, w)
            elif kind == "op":
                self.op(eng, fn, r, w)
            elif kind == "mm":
                self.mm(fn, r, w)
            else:
                self.dma(eng, fn, r, w, is_output=is_out)

    def barrier(self):
        evs = [(e["sem"], e["count"]) for e in self.engs.values() if e["count"] > 0] + self.dma_events
        for e in self.engs.values():
            for ev in evs:
                if ev[0] is e["sem"]:
                    continue
                self._wait(e, ev)
        self.dma_events = []

    def finish(self):
        self.barrier()
        e = self.engs["sp"]
        for ev in self.out_events:
            self._wait(e, ev)

    def emit(self, block):
        def replay(engname):
            def f(engine):
                for item in self.engs[engname]["ops"]:
                    if item[0] == "wait":
                        engine.wait_ge(item[1], item[2])
                    else:
                        _, fn, sem, inc = item
                        ins = fn(engine)
                        if sem is not None:
                            ins.then_inc(sem, inc)
            return f
        block.sync(replay("sp"))
        block.scalar(replay("act"))
        block.vector(replay("dve"))
        block.gpsimd(replay("pool"))
        block.tensor(replay("pe"))


class Arena:
    def __init__(self, ap, nwords):
        self.ap = ap
        self.n = nwords
        self.off = 0

    def reset(self):
        self.off = 0

    def alloc(self, rows, free, dt):
        nel = int(np.prod(free))
        nw = (nel + 1) // 2 if dt == BF16 else nel
        nw = (nw + 7) // 8 * 8
        assert self.off + nw <= self.n, f"arena overflow {self.off}+{nw}>{self.n}"
        v = self.ap[0:rows, self.off:self.off + nw]
        self.off += nw
        if dt == BF16:
            v = v.bitcast(BF16)
        v = v[:, 0:nel]
        if len(free) == 2:
            v = v.rearrange("p (a b) -> p a b", a=free[0])
        elif len(free) == 3:
            v = v.rearrange("p (a b c) -> p a b c", a=free[0], b=free[1])
        return Tl(v)


def build_nc():
    nc = bass.Bass("TRN2", target_bir_lowering=False)
    di = lambda name, shape: nc.dram_tensor(name, shape, F32, kind="ExternalInput").ap()
    do = lambda name, shape: nc.dram_tensor(name, shape, F32, kind="ExternalOutput").ap()
    xp = di("xp", [2048, 1024]); xs = di("xs", [64, 1024]); ccd = di("cc", [17, 1024])
    sh = di("sh", [16, 8, 128, 128]); sC = di("sC", [16, 4, 256, 256]); sn = di("sn", [16, 4, 256])
    sm = di("sm", [16, 4]); scv = di("scv", [48, 1024])
    w_ada = di("w_ada", [1024, 3072]); b_ada = di("b_ada", [24, 128]);
    w_out = di("w_out", [2048, 1024])
    w_hd = di("w_h", [8, 128, 4096]); w_md = di("w_m", [4, 128, 6144]); w_gd = di("w_g", [128, 64])
    prm = di("prm", [32, 1024]); bgd = di("bg", [4, 2]); wbdd = di("wbd", [128, 3, 8, 128])
    cmask_d = di("cmask", [128, 128]); smask_d = di("smask", [64, 64]); rowmask_d = di("rowmask", [64, 16])
    colmask_d = di("colmask", [1, 1024]); sel_d = di("sel", [32, 3, 128]); selg_d = di("selg", [17, 192])
    yp = do("yp", [2048, 1024]); ys = do("ys", [64, 1024])
    ohp = do("ohp", [8, 128, 128]); oCp = do("oCp", [4, 256, 256]); onp = do("onp", [4, 256]); omp = do("omp", [4, 1])
    ocp = do("ocp", [3, 1024])
    ohs = do("ohs", [16, 8, 128, 128]); oCs = do("oCs", [16, 4, 256, 256]); ons = do("ons", [16, 4, 256])
    oms = do("oms", [16, 4]); ocs = do("ocs", [16, 3, 1024])

    w_ada_v = w_ada.rearrange("(k p) c -> p k c", p=128)
    w_out_v = w_out.rearrange("(k p) c -> p k c", p=128)

    with ExitStack() as ctx:
        S = Sched(nc, ctx)
        sbt = lambda name, shape, dt: Tl(ctx.enter_context(nc.sbuf_tensor(name, shape, dt))[:])
        def pst(name, shape, dt):
            t = Tl(ctx.enter_context(nc.psum_tensor(name, shape, dt))[:])
            t.r.psum = True
            return t
        ACT = lambda fn, r=(), w=(): S.op("act", fn, r, w)
        DVE = lambda fn, r=(), w=(): S.op("dve", fn, r, w)
        POOL = lambda fn, r=(), w=(): S.op("pool", fn, r, w)
        MM = lambda fns, r=(), w=(): S.mm(fns, r, w)
        DMA = lambda fn, r=(), w=(), out=False: S.dma("sp", fn, r, w, is_output=out)
        DMAC = lambda fn, r=(), w=(): S.dma("pool", fn, r, w)

        hT = sbt("hT", [128, 8, 2112], BF16)
        mixT = sbt("mixT", [128, 16, 2112], BF16)
        wbuf = sbt("wbuf", [128, 8 * 768], BF16)
        arena_t = sbt("arena", [128, 15872], F32)
        AR = Arena(arena_t.ap, 15872)
        ident32 = sbt("ident32", [128, 128], F32)
        identb = sbt("identb", [128, 128], BF16)
        ones32 = sbt("ones32", [128, 128], F32)
        P = sbt("P", [128, 8, 32], F32)
        Pn = sbt("Pn", [128, 8, 32], F32)
        lbv = sbt("lbv", [128, 8, 4], F32)
        convd = sbt("convd", [128, 8, 4, 128], BF16)
        wbd = sbt("wbd_s", [128, 3, 8, 128], BF16)
        bc_hi = sbt("bc_hi", [128, 1024], BF16)
        bc_mo = sbt("bc_mo", [128, 1024], BF16)
        msk = sbt("msk", [128, 576], F32)
        cmask = sbt("cmask_s", [128, 128], F32)
        smask = sbt("smask_s", [64, 64], F32)
        rowmask = sbt("rowmask_s", [64, 16], BF16)
        colmask = sbt("colmask_s", [128, 16, 64], BF16)
        modT = sbt("modT", [128, 24, 17], F32)
        Amod = sbt("Amod", [128, 8, 17], F32)
        Sbf = sbt("Sbf", [128, 128], BF16)
        sm8 = sbt("sm8", [128, 16], F32)
        gbc = sbt("gbc", [128, 32, 4], F32)
        ectok = sbt("ectok", [128, 17, 4], F32)
        thrtok = sbt("thrtok", [128, 17, 4], F32)
        bA = pst("bA", [128, 512], F32); bB = pst("bB", [128, 512], F32)
        bC = pst("bC", [128, 512], F32); bD = pst("bD", [128, 512], F32)
        bS = pst("bS", [128, 512], F32); bO = pst("bO", [128, 512], F32); bU = pst("bU", [128, 512], F32)
        bT = pst("bT", [128, 1024], BF16)

        POOL(lambda e: e.memset(ident32.ap, 0.0), w=[ident32])
        POOL(lambda e: e.affine_select(out=ident32.ap, in_=ident32.ap, pattern=[[-1, 128]], compare_op=ALU.not_equal,
                                       fill=1.0, base=0, channel_multiplier=1), r=[ident32], w=[ident32])
        DVE(lambda e: e.tensor_copy(out=identb.ap, in_=ident32.ap), r=[ident32], w=[identb])
        POOL(lambda e: e.memset(ones32.ap, 1.0), w=[ones32])
        POOL(lambda e: e.memset(msk.ap, 1.0), w=[msk])
        POOL(lambda e: e.memset(msk.ap[:, 0:512].rearrange("p (c l) -> p c l", l=128)[:, :, 0:1], 0.0), r=[msk], w=[msk])
        POOL(lambda e: e.memset(msk.ap[:, 512:576].rearrange("p (c l) -> p c l", l=4)[:, :, 0:1], 0.0), r=[msk], w=[msk])
        DMA(lambda e: e.dma_start(out=cmask.ap, in_=cmask_d), w=[cmask])
        DMA(lambda e: e.dma_start(out=smask.ap, in_=smask_d), w=[smask])
        DMAC(lambda e: e.dma_start(out=rowmask.ap, in_=rowmask_d), w=[rowmask])
        DMAC(lambda e: e.dma_start(out=colmask.ap.rearrange("p a b -> p (a b)"), in_=colmask_d.partition_broadcast(128)),
             w=[colmask])
        DMAC(lambda e: e.dma_start(out=wbd.ap, in_=wbdd), w=[wbd])

        AR.reset()
        prm_s = AR.alloc(32, (1024,), F32)
        sel_s = AR.alloc(32, (3, 128), F32)
        bada_s = AR.alloc(24, (128,), F32)
        badaT = AR.alloc(128, (24,), F32)
        DMA(lambda e: e.dma_start(out=prm_s.ap, in_=prm), w=[prm_s])
        DMA(lambda e: e.dma_start(out=sel_s.ap, in_=sel_d), w=[sel_s])
        DMA(lambda e: e.dma_start(out=bada_s.ap, in_=b_ada), w=[bada_s])
        MM([lambda e, k=k: e.transpose(out=bS.ap[:, k * 32:(k + 1) * 32], in_=prm_s.ap[:, k * 128:(k + 1) * 128],
                                       identity=ident32.ap[0:32, 0:32]) for k in range(8)],
           r=[prm_s, ident32], w=[bS])
        DVE(lambda e: e.tensor_copy(out=P.ap.rearrange("p a b -> p (a b)"), in_=bS.ap[:, 0:256]), r=[bS], w=[P])
        DVE(lambda e: e.tensor_scalar(out=Pn.ap.rearrange("p a b -> p (a b)"), in0=P.ap.rearrange("p a b -> p (a b)"),
                                      scalar1=-1.0, scalar2=None, op0=ALU.mult), r=[P], w=[Pn])
        MM([lambda e: e.transpose(out=bO.ap[:, 0:24], in_=bada_s.ap, identity=ident32.ap[0:24, 0:24])],
           r=[bada_s, ident32], w=[bO])
        DVE(lambda e: e.tensor_copy(out=badaT.ap, in_=bO.ap[:, 0:24]), r=[bO], w=[badaT])
        DVE(lambda e: e.tensor_tensor(out=lbv.ap[:, :, 2], in0=P.ap[:, :, 9], in1=P.ap[:, :, 8], op=ALU.subtract), r=[P], w=[lbv])
        ACT(lambda e: e.activation(out=lbv.ap[:, :, 2], in_=lbv.ap[:, :, 2], func=AF.Exp), r=[lbv], w=[lbv])
        DVE(lambda e: e.tensor_scalar(out=lbv.ap[:, :, 2], in0=lbv.ap[:, :, 2], scalar1=1.0, scalar2=None, op0=ALU.add), r=[lbv], w=[lbv])
        DVE(lambda e: e.reciprocal(out=lbv.ap[:, :, 0], in_=lbv.ap[:, :, 2]), r=[lbv], w=[lbv])
        ACT(lambda e: e.activation(out=lbv.ap[:, :, 3], in_=lbv.ap[:, :, 0], func=AF.Ln, scale=-1.0, bias=1.0), r=[lbv], w=[lbv])
        DVE(lambda e: e.tensor_tensor(out=lbv.ap[:, :, 1], in0=lbv.ap[:, :, 3], in1=P.ap[:, :, 1], op=ALU.subtract), r=[lbv, P], w=[lbv])
        for k in range(8):
            for i in range(4):
                DVE(lambda e, k=k, i=i: e.tensor_scalar(out=convd.ap[:, k, i, :], in0=ident32.ap, scalar1=P.ap[:, k, 11 + i:12 + i],
                                                        scalar2=None, op0=ALU.mult), r=[ident32, P], w=[convd])
        for (si, dst) in ((0, bc_hi), (1, bc_mo)):
            for half, bank in ((0, bA), (1, bB)):
                MM([lambda e, si=si, half=half, bank=bank: e.matmul(bank.ap, lhsT=sel_s.ap[:, si, :], rhs=prm_s.ap[:, half * 512:(half + 1) * 512],
                                                                    start=True, stop=True)], r=[sel_s, prm_s], w=[bank])
                DVE(lambda e, dst=dst, half=half, bank=bank: e.tensor_copy(out=dst.ap[:, half * 512:(half + 1) * 512], in_=bank.ap), r=[bank], w=[dst])

        S.begin_window()
        cc_s = AR.alloc(17, (1024,), F32)
        ce = AR.alloc(17, (1024,), F32)
        csil = AR.alloc(17, (1024,), BF16)
        siluT = AR.alloc(128, (8, 17), BF16)
        modtok = AR.alloc(17, (3072,), F32)
        wa = [AR.alloc(128, (3072,), BF16) for _ in range(2)]
        DMA(lambda e: e.dma_start(out=cc_s.ap, in_=ccd), w=[cc_s])
        ACT(lambda e: e.activation(out=ce.ap, in_=cc_s.ap, func=AF.Exp, scale=-1.0), r=[cc_s], w=[ce])
        DVE(lambda e: e.tensor_scalar(out=ce.ap, in0=ce.ap, scalar1=1.0, scalar2=None, op0=ALU.add), r=[ce], w=[ce])
        DVE(lambda e: e.reciprocal(out=ce.ap, in_=ce.ap), r=[ce], w=[ce])
        DVE(lambda e: e.tensor_tensor(out=csil.ap, in0=cc_s.ap, in1=ce.ap, op=ALU.mult), r=[cc_s, ce], w=[csil])
        MM([lambda e, k=k: e.transpose(out=bT.ap[:, k * 32:k * 32 + 17], in_=csil.ap[:, k * 128:(k + 1) * 128],
                                       identity=identb.ap[0:17, 0:17]) for k in range(8)], r=[csil, identb], w=[bT])
        DVE(lambda e: e.tensor_copy(out=siluT.ap, in_=bT.ap[:, 0:256].rearrange("p (a b) -> p a b", a=8)[:, :, 0:17]), r=[bT], w=[siluT])
        banks6 = [bA, bB, bC, bD, bS, bO]
        for k in range(8):
            w_t = wa[k % 2]
            DMAC(lambda e, k=k, w_t=w_t: e.dma_start(out=w_t.ap, in_=w_ada_v[:, k, :]), w=[w_t])
            for n in range(6):
                MM([lambda e, k=k, n=n, w_t=w_t: e.matmul(banks6[n].ap[0:17, :], lhsT=siluT.ap[:, k, :], rhs=w_t.ap[:, n * 512:(n + 1) * 512],
                                                          start=(k == 0), stop=(k == 7))], r=[siluT, w_t], w=[banks6[n]])
        for n in range(6):
            DVE(lambda e, n=n: e.tensor_copy(out=modtok.ap[:, n * 512:(n + 1) * 512], in_=banks6[n].ap[0:17, :]), r=[banks6[n]], w=[modtok])
        MM([lambda e, j=j: e.transpose(out=bU.ap[:, j * 17:(j + 1) * 17], in_=modtok.ap[:, j * 128:(j + 1) * 128],
                                       identity=ident32.ap[0:17, 0:17]) for j in range(24)], r=[modtok, ident32], w=[bU])
        DVE(lambda e: e.tensor_tensor(out=modT.ap, in0=bU.ap[:, 0:408].rearrange("p (a b) -> p a b", a=24),
                                      in1=badaT.ap.unsqueeze(2).to_broadcast([128, 24, 17]), op=ALU.add), r=[bU, badaT], w=[modT])
        DVE(lambda e: e.scalar_tensor_tensor(out=Amod.ap, in0=modT.ap[:, 8:16, :], scalar=1.0,
                                             in1=P.ap[:, :, 7:8].to_broadcast([128, 8, 17]), op0=ALU.add, op1=ALU.mult),
            r=[modT, P], w=[Amod])

        xt = [AR.alloc(128, (1024,), F32) for _ in range(2)]
        junk = AR.alloc(128, (1024,), BF16)
        xn = [AR.alloc(128, (1024,), BF16) for _ in range(2)]
        tmpf = AR.alloc(128, (8, 128), F32)
        st2 = AR.alloc(128, (8,), F32)
        def tile2(i):
            rows = 128 if i < 16 else 64
            x_t = xt[i % 2]; xn_t = xn[i % 2]
            src = xp[i * 128:(i + 1) * 128, :] if i < 16 else xs
            DMA(lambda e, x_t=x_t, src=src, rows=rows: e.dma_start(out=x_t.ap[0:rows, :], in_=src), w=[x_t])
            ACT(lambda e, x_t=x_t, rows=rows: e.activation(out=junk.ap[0:rows, :], in_=x_t.ap[0:rows, :], func=AF.Square, scale=1.0 / 32.0,
                                                          accum_out=st2.ap[0:rows, 0:1]), r=[x_t], w=[junk, st2])
            ACT(lambda e, rows=rows: e.activation(out=st2.ap[0:rows, 1:2], in_=st2.ap[0:rows, 0:1], func=AF.Ln, bias=EPS), r=[st2], w=[st2])
            ACT(lambda e, rows=rows: e.activation(out=st2.ap[0:rows, 2:3], in_=st2.ap[0:rows, 1:2], func=AF.Exp, scale=-0.5), r=[st2], w=[st2])
            ACT(lambda e, x_t=x_t, xn_t=xn_t, rows=rows: e.activation(out=xn_t.ap[0:rows, :], in_=x_t.ap[0:rows, :], func=AF.Copy,
                                                                     scale=st2.ap[0:rows, 2:3]), r=[x_t, st2], w=[xn_t])
            MM([lambda e, k=k, xn_t=xn_t, rows=rows: e.transpose(out=bT.ap[:, k * 128:k * 128 + rows], in_=xn_t.ap[0:rows, k * 128:(k + 1) * 128],
                                                                identity=identb.ap[0:rows, 0:rows]) for k in range(8)],
               r=[xn_t, identb], w=[bT])
            pv = bT.ap.rearrange("p (a b) -> p a b", a=8)[:, :, 0:rows]
            if i < 16:
                DVE(lambda e, pv=pv: e.tensor_tensor(out=tmpf.ap, in0=pv, in1=Amod.ap[:, :, 0:1].to_broadcast([128, 8, 128]), op=ALU.mult),
                    r=[bT, Amod], w=[tmpf])
                DVE(lambda e, i=i: e.tensor_tensor(out=hT.ap[:, :, i * 128:(i + 1) * 128], in0=tmpf.ap,
                                                    in1=modT.ap[:, 0:8, 0:1].to_broadcast([128, 8, 128]), op=ALU.add),
                     r=[tmpf, modT], w=[hT])
            else:
                for k in range(8):
                    DVE(lambda e, k=k: e.tensor_tensor(out=tmpf.ap[:, k, 0:64].rearrange("p (b t) -> p b t", t=4),
                                                       in0=bT.ap[:, k * 128:k * 128 + 64].rearrange("p (b t) -> p b t", t=4),
                                                       in1=Amod.ap[:, k, 1:17].unsqueeze(2).to_broadcast([128, 16, 4]), op=ALU.mult),
                        r=[bT, Amod], w=[tmpf])
                    DVE(lambda e, k=k: e.tensor_tensor(out=hT.ap[:, k, 2048:2112].rearrange("p (b t) -> p b t", t=4),
                                                        in0=tmpf.ap[:, k, 0:64].rearrange("p (b t) -> p b t", t=4),
                                                        in1=modT.ap[:, k, 1:17].unsqueeze(2).to_broadcast([128, 16, 4]), op=ALU.add),
                         r=[tmpf, modT], w=[hT])

        for i in range(17):
            tile2(i)
        S.end_window()

        GROUPS = [(0, 512), (512, 512), (1024, 512), (1536, 512), (2048, 64)]

        def silu_from_psum(ps, nbias_ap, bias_ap, T, ez, sg, out_ap, out_t, extra_r=()):
            ACT(lambda e: e.activation(out=ez.ap[:, 0:T], in_=ps.ap[:, 0:T], func=AF.Exp, scale=-1.0, bias=nbias_ap), r=[ps, Pn], w=[ez])
            ACT(lambda e: e.activation(out=ez.ap[:, 0:T], in_=ez.ap[:, 0:T], func=AF.Ln, bias=1.0), r=[ez], w=[ez])
            ACT(lambda e: e.activation(out=sg.ap[:, 0:T], in_=ez.ap[:, 0:T], func=AF.Exp, scale=-1.0), r=[ez], w=[sg])
            DVE(lambda e: e.scalar_tensor_tensor(out=out_ap, in0=ps.ap[:, 0:T], scalar=bias_ap, in1=sg.ap[:, 0:T], op0=ALU.add, op1=ALU.mult),
                r=[ps, sg, P], w=[out_t])

        def interleave(*gens):
            gens = [g for g in gens if g is not None]
            while gens:
                for g in list(gens):
                    try:
                        next(g)
                    except StopIteration:
                        gens.remove(g)

        def silu_gen(ps, nbias_ap, bias_ap, T, ez, sg, out_ap, out_t):
            ACT(lambda e: e.activation(out=ez.ap[:, 0:T], in_=ps.ap[:, 0:T], func=AF.Exp, scale=-1.0, bias=nbias_ap), r=[ps, Pn], w=[ez]); yield
            ACT(lambda e: e.activation(out=ez.ap[:, 0:T], in_=ez.ap[:, 0:T], func=AF.Ln, bias=1.0), r=[ez], w=[ez]); yield
            ACT(lambda e: e.activation(out=sg.ap[:, 0:T], in_=ez.ap[:, 0:T], func=AF.Exp, scale=-1.0), r=[ez], w=[sg]); yield
            DVE(lambda e: e.scalar_tensor_tensor(out=out_ap, in0=ps.ap[:, 0:T], scalar=bias_ap, in1=sg.ap[:, 0:T], op0=ALU.add, op1=ALU.mult),
                r=[ps, sg, P], w=[out_t]); yield

        S.barrier()
        AR.reset()
        f_e = AR.alloc(128, (512,), F32); f_L1 = AR.alloc(128, (512,), F32); f_L2 = AR.alloc(128, (512,), F32)
        f_G = AR.alloc(128, (512,), F32); f_eG = AR.alloc(128, (512,), F32)
        f_ez = AR.alloc(128, (512,), F32); f_sg = AR.alloc(128, (512,), F32)
        wh2_t = AR.alloc(128, (4096,), BF16)
        GO = []
        for _ in range(2):
            GO.append(dict(qt=AR.alloc(128, (512,), BF16), kt=AR.alloc(128, (512,), BF16), kd=AR.alloc(128, (512,), BF16),
                           gzg=AR.alloc(128, (512,), BF16), vtok=AR.alloc(128, (4, 128), BF16), kdtok=AR.alloc(128, (4, 128), BF16),
                           eGL=AR.alloc(128, (16,), F32)))
        ATm = AR.alloc(128, (128,), BF16); on_t = AR.alloc(128, (128,), BF16); junk2 = AR.alloc(128, (128,), BF16)
        ATm4 = AR.alloc(128, (4, 128), BF16); on4 = AR.alloc(128, (4, 128), BF16)
        S4 = AR.alloc(128, (4, 128), F32); Sb3 = AR.alloc(128, (3, 128), BF16); sq4 = AR.alloc(128, (512,), F32)
        Sin = AR.alloc(128, (16, 128), F32); Sbb = AR.alloc(128, (16, 128), BF16)
        qexp = AR.alloc(128, (16, 64), BF16); kexp = AR.alloc(64, (16, 128), BF16)
        wh_a = Tl(wbuf.ap[:, 0:4096].rearrange("p (k s c) -> p k s c", k=8, s=4))
        wh_a.r = wbuf.r
        wh_b = Tl(wh2_t.ap.rearrange("p (k s c) -> p k s c", k=8, s=4))
        wh_b.r = wh2_t.r
        whs = [wh_a, wh_b]
        bDb = Tl(bD.ap.bitcast(BF16))
        bDb.r = bD.r
        bTf = Tl(bT.ap.bitcast(F32))
        bTf.r = bT.r

        def h_stage1(h, tok0, T, go):
            wh = whs[h % 2]
            qt, kt, kd, gzg, vtok, kdtok, eGL = go["qt"], go["kt"], go["kd"], go["gzg"], go["vtok"], go["kdtok"], go["eGL"]
            sample = (T == 64)
            rows = 64 if sample else 128
            nch = 1 if sample else 4
            mk = msk.ap[:, 512:576] if sample else msk.ap[:, 0:512]
            if tok0 == 0:
                DMAC(lambda e: e.dma_start(out=wh.ap.rearrange("p k s c -> p (k s c)"), in_=w_hd[h]), w=[wh])
                yield
            for (bank, s_i) in ((bA, 1), (bB, 0), (bC, 3)):
                MM([lambda e, k=k, bank=bank, s_i=s_i: e.matmul(bank.ap[:, 0:T], lhsT=wh.ap[:, k, s_i, :], rhs=hT.ap[:, k, tok0:tok0 + T],
                                                                start=(k == 0), stop=(k == 7)) for k in range(8)], r=[wh, hT], w=[bank])
                yield
            fns = []
            for i in range(nch):
                for k in range(8):
                    fns.append(lambda e, i=i, k=k: e.matmul(bD.ap[0:rows, i * 128:(i + 1) * 128], lhsT=hT.ap[:, k, tok0 + i * 128: tok0 + i * 128 + rows],
                                                            rhs=wh.ap[:, k, 2, :], start=(k == 0), stop=(k == 7)))
            MM(fns, r=[wh, hT], w=[bD]); yield
            ACT(lambda e: e.activation(out=f_e.ap[:, 0:T], in_=bA.ap[:, 0:T], func=AF.Exp, scale=-1.0, bias=Pn.ap[:, h, 1:2]), r=[bA, Pn], w=[f_e]); yield
            ACT(lambda e: e.activation(out=f_L1.ap[:, 0:T], in_=f_e.ap[:, 0:T], func=AF.Ln, bias=1.0), r=[f_e], w=[f_L1]); yield
            ACT(lambda e: e.activation(out=f_L2.ap[:, 0:T], in_=f_e.ap[:, 0:T], func=AF.Ln, bias=1.0, scale=lbv.ap[:, h, 0:1]), r=[f_e, lbv], w=[f_L2]); yield
            DVE(lambda e: e.tensor_tensor(out=f_L2.ap[:, 0:T], in0=f_L2.ap[:, 0:T], in1=f_L1.ap[:, 0:T], op=ALU.subtract), r=[f_L2, f_L1], w=[f_L2]); yield
            DVE(lambda e: e.tensor_tensor_scan(out=f_G.ap[:, 0:T], data0=mk, data1=f_L2.ap[:, 0:T], initial=0.0, op0=ALU.mult, op1=ALU.add),
                r=[msk, f_L2], w=[f_G]); yield
            ACT(lambda e: e.activation(out=f_eG.ap[:, 0:T], in_=f_G.ap[:, 0:T], func=AF.Exp), r=[f_G], w=[f_eG]); yield
            DVE(lambda e: e.scalar_tensor_tensor(out=qt.ap[:, 0:T], in0=bB.ap[:, 0:T], scalar=P.ap[:, h, 0:1], in1=f_eG.ap[:, 0:T],
                                                 op0=ALU.add, op1=ALU.mult), r=[bB, P, f_eG], w=[qt]); yield
            DVE(lambda e: e.scalar_tensor_tensor(out=f_e.ap[:, 0:T], in0=bA.ap[:, 0:T], scalar=-1.0, in1=f_L1.ap[:, 0:T],
                                                 op0=ALU.mult, op1=ALU.subtract), r=[bA, f_L1], w=[f_e]); yield
            DVE(lambda e: e.tensor_tensor(out=f_e.ap[:, 0:T], in0=f_e.ap[:, 0:T], in1=f_G.ap[:, 0:T], op=ALU.subtract), r=[f_e, f_G], w=[f_e]); yield
            ACT(lambda e: e.activation(out=kt.ap[:, 0:T], in_=f_e.ap[:, 0:T], func=AF.Exp, bias=lbv.ap[:, h, 1:2]), r=[f_e, lbv], w=[kt]); yield
            L = 4 if sample else 128
            ncg = T // L
            Gv = f_G.ap[:, 0:T].rearrange("p (c l) -> p c l", l=L)
            ACT(lambda e: e.activation(out=eGL.ap[:, 0:ncg], in_=Gv[:, :, L - 1], func=AF.Exp), r=[f_G], w=[eGL]); yield
            DVE(lambda e: e.tensor_tensor(out=kd.ap[:, 0:T].rearrange("p (c l) -> p c l", l=L), in0=kt.ap[:, 0:T].rearrange("p (c l) -> p c l", l=L),
                                          in1=eGL.ap[:, 0:ncg].unsqueeze(2).to_broadcast([128, ncg, L]), op=ALU.mult), r=[kt, eGL], w=[kd]); yield
            yield from silu_gen(bC, Pn.ap[:, h, 3:4], P.ap[:, h, 3:4], T, f_ez, f_sg, gzg.ap[:, 0:T], gzg)

            DVE(lambda e: e.tensor_tensor(out=vtok.ap[0:rows, 0:nch, :], in0=bD.ap[0:rows, 0:nch * 128].rearrange("p (a b) -> p a b", a=nch),
                                          in1=bc_hi.ap[0:rows, h * 128:(h + 1) * 128].unsqueeze(1).to_broadcast([rows, nch, 128]), op=ALU.add),
                r=[bD, bc_hi], w=[vtok]); yield
            MM([lambda e, i=i: e.transpose(out=bDb.ap[0:rows, i * 128:(i + 1) * 128], in_=kd.ap[:, i * 128:i * 128 + rows], identity=identb.ap)
                for i in range(nch)], r=[kd, identb], w=[bDb]); yield
            DVE(lambda e: e.tensor_copy(out=kdtok.ap[0:rows, 0:nch, :], in_=bDb.ap[0:rows, 0:nch * 128].rearrange("p (a b) -> p a b", a=nch)),
                r=[bDb], w=[kdtok]); yield

        def o_epilogue(h, rows, tok0, gzg):
            ACT(lambda e: e.activation(out=junk2.ap[0:rows, :], in_=bO.ap[0:rows, 0:128], func=AF.Square, scale=float(128 ** -0.5),
                                       accum_out=sm8.ap[0:rows, 0:1]), r=[bO], w=[junk2, sm8]); yield
            ACT(lambda e: e.activation(out=sm8.ap[0:rows, 1:2], in_=sm8.ap[0:rows, 0:1], func=AF.Ln, bias=EPS), r=[sm8], w=[sm8]); yield
            ACT(lambda e: e.activation(out=sm8.ap[0:rows, 2:3], in_=sm8.ap[0:rows, 1:2], func=AF.Exp, scale=-0.5), r=[sm8], w=[sm8]); yield
            ACT(lambda e: e.activation(out=on_t.ap[0:rows, :], in_=bO.ap[0:rows, 0:128], func=AF.Copy, scale=sm8.ap[0:rows, 2:3]),
                r=[bO, sm8], w=[on_t]); yield
            MM([lambda e: e.transpose(out=bT.ap[:, 0:rows], in_=on_t.ap[0:rows, :], identity=identb.ap[0:rows, 0:rows])],
               r=[on_t, identb], w=[bT]); yield
            DVE(lambda e: e.scalar_tensor_tensor(out=mixT.ap[:, h, tok0:tok0 + rows], in0=bT.ap[:, 0:rows], scalar=P.ap[:, h, 10:11],
                                                 in1=gzg.ap[:, tok0 % 512:tok0 % 512 + rows], op0=ALU.mult, op1=ALU.mult), r=[bT, gzg, P], w=[mixT]); yield

        def h_stage2(h, tok0, T, go):
            qt, kt, kd, gzg, vtok, kdtok, eGL = go["qt"], go["kt"], go["kd"], go["gzg"], go["vtok"], go["kdtok"], go["eGL"]
            sample = (T == 64)
            if tok0 == 0:
                POOL(lambda e: e.memset(S4.ap, 0.0), w=[S4])
                POOL(lambda e: e.memset(Sbf.ap, 0.0), w=[Sbf]); yield
            if not sample:
                MM([lambda e, i=i: e.matmul(bS.ap[:, i * 128:(i + 1) * 128], lhsT=kt.ap[:, i * 128:(i + 1) * 128], rhs=qt.ap[:, i * 128:(i + 1) * 128],
                                            start=True, stop=True) for i in range(4)], r=[kt, qt], w=[bS]); yield
                DVE(lambda e: e.tensor_tensor(out=ATm4.ap, in0=bS.ap.rearrange("p (a b) -> p a b", a=4),
                                              in1=cmask.ap.unsqueeze(1).to_broadcast([128, 4, 128]), op=ALU.mult), r=[bS, cmask], w=[ATm4]); yield
                MM([lambda e, i=i: e.matmul(bU.ap[:, i * 128:(i + 1) * 128], lhsT=kdtok.ap[:, i, :], rhs=vtok.ap[:, i, :], start=True, stop=True)
                    for i in range(4)], r=[kdtok, vtok], w=[bU]); yield
                for i in range(4):
                    DVE(lambda e, i=i: e.scalar_tensor_tensor(out=S4.ap[:, i, :], in0=S4.ap[:, (i + 3) % 4, :], scalar=eGL.ap[:, i:i + 1],
                                                              in1=bU.ap[:, i * 128:(i + 1) * 128], op0=ALU.mult, op1=ALU.add), r=[S4, eGL, bU], w=[S4]); yield
                ACT(lambda e: e.activation(out=Sb3.ap, in_=S4.ap[:, 0:3, :], func=AF.Copy), r=[S4], w=[Sb3]); yield
                fns = []
                for i in range(4):
                    fns.append(lambda e, i=i: e.matmul(bO.ap[:, i * 128:(i + 1) * 128], lhsT=ATm4.ap[:, i, :], rhs=vtok.ap[:, i, :], start=True, stop=False))
                    rhs_s = Sbf.ap if i == 0 else Sb3.ap[:, i - 1, :]
                    fns.append(lambda e, i=i, rhs_s=rhs_s: e.matmul(bO.ap[:, i * 128:(i + 1) * 128], lhsT=qt.ap[:, i * 128:(i + 1) * 128], rhs=rhs_s,
                                                                    start=False, stop=True))
                MM(fns, r=[ATm4, vtok, qt, Sbf, Sb3], w=[bO]); yield
                ACT(lambda e: e.activation(out=Sbf.ap, in_=S4.ap[:, 3, :], func=AF.Copy), r=[S4], w=[Sbf]); yield
                for i in range(4):
                    ACT(lambda e, i=i: e.activation(out=sq4.ap[:, i * 128:(i + 1) * 128], in_=bO.ap[:, i * 128:(i + 1) * 128], func=AF.Square,
                                                    scale=float(128 ** -0.5), accum_out=sm8.ap[:, i:i + 1]), r=[bO], w=[sq4, sm8]); yield
                ACT(lambda e: e.activation(out=sm8.ap[:, 4:8], in_=sm8.ap[:, 0:4], func=AF.Ln, bias=EPS), r=[sm8], w=[sm8]); yield
                ACT(lambda e: e.activation(out=sm8.ap[:, 8:12], in_=sm8.ap[:, 4:8], func=AF.Exp, scale=-0.5), r=[sm8], w=[sm8]); yield
                for i in range(4):
                    ACT(lambda e, i=i: e.activation(out=on4.ap[:, i, :], in_=bO.ap[:, i * 128:(i + 1) * 128], func=AF.Copy, scale=sm8.ap[:, 8 + i:9 + i]),
                        r=[bO, sm8], w=[on4]); yield
                MM([lambda e, i=i: e.transpose(out=bT.ap[:, i * 128:(i + 1) * 128], in_=on4.ap[:, i, :], identity=identb.ap) for i in range(4)],
                   r=[on4, identb], w=[bT]); yield
                DVE(lambda e: e.scalar_tensor_tensor(out=mixT.ap[:, h, tok0:tok0 + 512], in0=bT.ap[:, 0:512], scalar=P.ap[:, h, 10:11], in1=gzg.ap[:, 0:512],
                                                     op0=ALU.mult, op1=ALU.mult), r=[bT, gzg, P], w=[mixT]); yield
                if tok0 == 1536:
                    DMA(lambda e: e.dma_start(out=ohp[h], in_=S4.ap[:, 3, :]), r=[S4], out=True); yield
            else:
                DMA(lambda e: e.dma_start(out=Sin.ap, in_=sh[:, h].rearrange("b k v -> k b v")), w=[Sin]); yield
                ACT(lambda e: e.activation(out=Sbb.ap, in_=Sin.ap, func=AF.Copy), r=[Sin], w=[Sbb]); yield
                DVE(lambda e: e.tensor_tensor(out=qexp.ap, in0=qt.ap[:, 0:64].unsqueeze(1).to_broadcast([128, 16, 64]), in1=colmask.ap, op=ALU.mult),
                    r=[qt, colmask], w=[qexp]); yield
                DVE(lambda e: e.tensor_tensor(out=kexp.ap, in0=kdtok.ap[0:64, 0, :].unsqueeze(1).to_broadcast([64, 16, 128]),
                                              in1=rowmask.ap.unsqueeze(2).to_broadcast([64, 16, 128]), op=ALU.mult), r=[kdtok, rowmask], w=[kexp]); yield
                MM([lambda e: e.matmul(bS.ap[0:64, 0:64], lhsT=kt.ap[:, 0:64], rhs=qt.ap[:, 0:64], start=True, stop=True)], r=[kt, qt], w=[bS]); yield
                DVE(lambda e: e.tensor_tensor(out=ATm.ap[0:64, 0:64], in0=bS.ap[0:64, 0:64], in1=smask.ap, op=ALU.mult), r=[bS, smask], w=[ATm]); yield
                fns = [lambda e: e.matmul(bO.ap[0:64, 0:128], lhsT=ATm.ap[0:64, 0:64], rhs=vtok.ap[0:64, 0, :], start=True, stop=False)]
                for b in range(16):
                    fns.append(lambda e, b=b: e.matmul(bO.ap[0:64, 0:128], lhsT=qexp.ap[:, b, :], rhs=Sbb.ap[:, b, :], start=False, stop=(b == 15)))
                MM(fns, r=[ATm, vtok, qexp, Sbb], w=[bO]); yield
                for rnd in range(4):
                    MM([lambda e, b=b: e.matmul(bU.ap[:, (b % 4) * 128:(b % 4 + 1) * 128], lhsT=kexp.ap[:, b, :], rhs=vtok.ap[0:64, 0, :], start=True, stop=True)
                        for b in range(rnd * 4, rnd * 4 + 4)], r=[kexp, vtok], w=[bU]); yield
                    for b in range(rnd * 4, rnd * 4 + 4):
                        DVE(lambda e, b=b: e.scalar_tensor_tensor(out=Sin.ap[:, b, :], in0=Sin.ap[:, b, :], scalar=eGL.ap[:, b:b + 1],
                                                                  in1=bU.ap[:, (b % 4) * 128:(b % 4 + 1) * 128], op0=ALU.mult, op1=ALU.add),
                            r=[Sin, eGL, bU], w=[Sin]); yield
                DMA(lambda e: e.dma_start(out=ohs[:, h].rearrange("b k v -> k b v"), in_=Sin.ap), r=[Sin], out=True); yield
                yield from o_epilogue(h, 64, 2048, gzg)

        h_units = [(h, tok0, T) for h in range(8) for (tok0, T) in GROUPS]
        S.begin_window()
        for n, u in enumerate(h_units):
            interleave(h_stage1(*u, GO[n % 2]))
            interleave(h_stage2(*u, GO[n % 2]))
        S.end_window()

        S.barrier()
        AR.reset()
        wg = AR.alloc(128, (8, 8), BF16)
        bg_s = AR.alloc(4, (2,), F32); nbg = AR.alloc(4, (2,), F32)
        g_ig = AR.alloc(4, (2112,), F32); g_L = AR.alloc(4, (2112,), F32); g_nb = AR.alloc(4, (2112,), F32)
        cmax = AR.alloc(4, (32,), F32); nbL = AR.alloc(4, (32,), F32); bLn = AR.alloc(4, (32,), F32); d1 = AR.alloc(4, (32,), F32)
        mP = AR.alloc(4, (32,), F32); rall = AR.alloc(4, (32,), F32); mprev = AR.alloc(4, (32,), F32); gall = AR.alloc(4, (32,), F32)
        gdiag = AR.alloc(4, (32, 4), F32)
        DMAC(lambda e: e.dma_start(out=wg.ap.rearrange("p a b -> p (a b)"), in_=w_gd), w=[wg])
        DMA(lambda e: e.dma_start(out=bg_s.ap, in_=bgd), w=[bg_s])
        DVE(lambda e: e.tensor_scalar(out=nbg.ap, in0=bg_s.ap, scalar1=-1.0, scalar2=None, op0=ALU.mult), r=[bg_s], w=[nbg])
        POOL(lambda e: e.memset(mprev.ap, 0.0), w=[mprev])
        DMA(lambda e: e.dma_start(out=mprev.ap[:, 16:32], in_=sm.rearrange("b h -> h b"), allow_slow_non_contiguous=True), r=[mprev], w=[mprev])
        def gate_group(tok0, T):
            MM([lambda e, k=k: e.matmul(bA.ap[0:4, 0:T], lhsT=wg.ap[:, k, 0:4], rhs=hT.ap[:, k, tok0:tok0 + T], start=(k == 0), stop=(k == 7)) for k in range(8)],
               r=[wg, hT], w=[bA])
            MM([lambda e, k=k: e.matmul(bB.ap[0:4, 0:T], lhsT=wg.ap[:, k, 4:8], rhs=hT.ap[:, k, tok0:tok0 + T], start=(k == 0), stop=(k == 7)) for k in range(8)],
               r=[wg, hT], w=[bB])
            ACT(lambda e: e.activation(out=g_ig.ap[:, tok0:tok0 + T], in_=bA.ap[0:4, 0:T], func=AF.Identity, bias=bg_s.ap[:, 0:1]), r=[bA, bg_s], w=[g_ig])
            ACT(lambda e: e.activation(out=g_L.ap[:, tok0:tok0 + T], in_=bB.ap[0:4, 0:T], func=AF.Exp, scale=-1.0, bias=nbg.ap[:, 1:2]), r=[bB, nbg], w=[g_L])
            ACT(lambda e: e.activation(out=g_L.ap[:, tok0:tok0 + T], in_=g_L.ap[:, tok0:tok0 + T], func=AF.Ln, bias=1.0), r=[g_L], w=[g_L])
            mk = msk.ap[0:4, 512:576] if T == 64 else msk.ap[0:4, 0:512]
            DVE(lambda e, mk=mk: e.tensor_tensor_scan(out=g_nb.ap[:, tok0:tok0 + T], data0=mk, data1=g_L.ap[:, tok0:tok0 + T], initial=0.0,
                                                      op0=ALU.mult, op1=ALU.add), r=[msk, g_L], w=[g_nb])
        for (tok0, T) in GROUPS:
            gate_group(tok0, T)
        DVE(lambda e: e.tensor_tensor(out=g_ig.ap, in0=g_ig.ap, in1=g_nb.ap, op=ALU.add), r=[g_ig, g_nb], w=[g_ig])
        DVE(lambda e: e.reduce_max(out=cmax.ap[:, 0:16], in_=g_ig.ap[:, 0:2048].rearrange("p (c l) -> p c l", l=128), axis=AX.X), r=[g_ig], w=[cmax])
        DVE(lambda e: e.reduce_max(out=cmax.ap[:, 16:32], in_=g_ig.ap[:, 2048:2112].rearrange("p (c l) -> p c l", l=4), axis=AX.X), r=[g_ig, cmax], w=[cmax])
        DVE(lambda e: e.tensor_copy(out=nbL.ap[:, 0:16], in_=g_nb.ap[:, 0:2048].rearrange("p (c l) -> p c l", l=128)[:, :, 127]), r=[g_nb], w=[nbL])
        DVE(lambda e: e.tensor_copy(out=nbL.ap[:, 16:32], in_=g_nb.ap[:, 2048:2112].rearrange("p (c l) -> p c l", l=4)[:, :, 3]), r=[g_nb, nbL], w=[nbL])
        DVE(lambda e: e.tensor_scalar(out=bLn.ap, in0=nbL.ap, scalar1=-1.0, scalar2=None, op0=ALU.mult), r=[nbL], w=[bLn])
        DVE(lambda e: e.tensor_tensor(out=d1.ap, in0=cmax.ap, in1=nbL.ap, op=ALU.subtract), r=[cmax, nbL], w=[d1])
        DVE(lambda e: e.tensor_tensor_scan(out=mP.ap[:, 0:16], data0=bLn.ap[:, 0:16], data1=d1.ap[:, 0:16], initial=0.0, op0=ALU.add, op1=ALU.max),
            r=[bLn, d1], w=[mP])
        DVE(lambda e: e.tensor_copy(out=mprev.ap[:, 1:16], in_=mP.ap[:, 0:15]), r=[mP, mprev], w=[mprev])
        DVE(lambda e: e.tensor_tensor(out=rall.ap[:, 0:16], in0=mP.ap[:, 0:16], in1=nbL.ap[:, 0:16], op=ALU.add), r=[mP, nbL], w=[rall])
        DVE(lambda e: e.tensor_tensor(out=rall.ap[:, 16:32], in0=mprev.ap[:, 16:32], in1=cmax.ap[:, 16:32], op=ALU.max), r=[mprev, cmax, rall], w=[rall])
        DVE(lambda e: e.tensor_tensor(out=mP.ap[:, 16:32], in0=rall.ap[:, 16:32], in1=nbL.ap[:, 16:32], op=ALU.subtract), r=[rall, nbL, mP], w=[mP])
        DMA(lambda e: e.dma_start(out=omp, in_=mP.ap[:, 15:16]), r=[mP], out=True)
        DMA(lambda e: e.dma_start(out=oms.rearrange("b h -> h b"), in_=mP.ap[:, 16:32], allow_slow_non_contiguous=True), r=[mP], out=True)
        DVE(lambda e: e.tensor_tensor(out=gall.ap, in0=mprev.ap, in1=rall.ap, op=ALU.subtract), r=[mprev, rall], w=[gall])
        ACT(lambda e: e.activation(out=gall.ap, in_=gall.ap, func=AF.Exp), r=[gall], w=[gall])
        for (dst, lo, hi, L) in ((g_ig, 0, 2048, 128), (g_ig, 2048, 2112, 4), (g_nb, 0, 2048, 128), (g_nb, 2048, 2112, 4)):
            n = (hi - lo) // L
            r0 = 0 if lo == 0 else 16
            DVE(lambda e, dst=dst, lo=lo, hi=hi, L=L, n=n, r0=r0: e.tensor_tensor(
                out=dst.ap[:, lo:hi].rearrange("p (c l) -> p c l", l=L), in0=dst.ap[:, lo:hi].rearrange("p (c l) -> p c l", l=L),
                in1=rall.ap[:, r0:r0 + n].unsqueeze(2).to_broadcast([4, n, L]), op=ALU.subtract), r=[dst, rall], w=[dst])
        DVE(lambda e: e.tensor_scalar(out=g_ig.ap, in0=g_ig.ap, scalar1=-LN16, scalar2=None, op0=ALU.add), r=[g_ig], w=[g_ig])
        ACT(lambda e: e.activation(out=g_ig.ap, in_=g_ig.ap, func=AF.Exp), r=[g_ig], w=[g_ig])
        ACT(lambda e: e.activation(out=g_nb.ap, in_=g_nb.ap, func=AF.Exp), r=[g_nb], w=[g_nb])
        fns = []
        for i in range(17):
            rows = 128 if i < 16 else 64
            fns.append(lambda e, i=i, rows=rows: e.transpose(out=bS.ap[0:rows, i * 4:(i + 1) * 4], in_=g_ig.ap[:, i * 128:i * 128 + rows], identity=ident32.ap[0:4, 0:4]))
            fns.append(lambda e, i=i, rows=rows: e.transpose(out=bS.ap[0:rows, 128 + i * 4:128 + (i + 1) * 4], in_=g_nb.ap[:, i * 128:i * 128 + rows], identity=ident32.ap[0:4, 0:4]))
        MM(fns, r=[g_ig, g_nb, ident32], w=[bS])
        DVE(lambda e: e.tensor_copy(out=ectok.ap.rearrange("p a b -> p (a b)"), in_=bS.ap[:, 0:68]), r=[bS], w=[ectok])
        DVE(lambda e: e.tensor_copy(out=thrtok.ap.rearrange("p a b -> p (a b)"), in_=bS.ap[:, 128:196]), r=[bS], w=[thrtok])
        DVE(lambda e: e.tensor_tensor(out=gdiag.ap, in0=gall.ap.unsqueeze(2).to_broadcast([4, 32, 4]),
                                      in1=ident32.ap[0:4, 0:4].unsqueeze(1).to_broadcast([4, 32, 4]), op=ALU.mult), r=[gall, ident32], w=[gdiag])
        MM([lambda e: e.matmul(bO.ap[:, 0:128], lhsT=ones32.ap[0:4, :], rhs=gdiag.ap.rearrange("p a b -> p (a b)"), start=True, stop=True)],
           r=[ones32, gdiag], w=[bO])
        DVE(lambda e: e.tensor_copy(out=gbc.ap.rearrange("p a b -> p (a b)"), in_=bO.ap[:, 0:128]), r=[bO], w=[gbc])

        S.barrier()
        AR.reset()
        muX = AR.alloc(128, (2, 520), BF16)
        extS = AR.alloc(128, (2, 16, 7), BF16)
        shS = AR.alloc(128, (2, 4, 64), BF16)
        scr = AR.alloc(128, (1024,), F32)
        m_ez = Tl(scr.ap[:, 0:512]); m_ez.r = scr.r
        m_sg = Tl(scr.ap[:, 512:1024]); m_sg.r = scr.r
        xcT = AR.alloc(128, (2, 512), BF16)
        mu32 = AR.alloc(128, (2, 128), F32); mutok = AR.alloc(128, (256,), F32)
        bufT = AR.alloc(128, (8, 48), BF16)
        MO = []
        for _ in range(2):
            MO.append(dict(smz=AR.alloc(128, (2, 512), BF16), sxc=AR.alloc(128, (2, 512), BF16), qT=AR.alloc(128, (2, 512), BF16),
                           kT=AR.alloc(128, (2, 512), BF16), kp4=AR.alloc(128, (4, 256), BF16), vext4=AR.alloc(128, (4, 258), BF16),
                           so4=AR.alloc(128, (4, 256), BF16)))
        STm4 = AR.alloc(128, (4, 128), BF16); Cgb = AR.alloc(128, (2, 257), BF16); Cst = AR.alloc(128, (2, 257), F32)
        big = AR.alloc(128, (4056,), F32)
        def bview(off, rows, free, dt):
            nel = int(np.prod(free)); nw = (nel + 1) // 2 if dt == BF16 else nel
            v = big.ap[0:rows, off:off + nw]
            if dt == BF16:
                v = v.bitcast(BF16)
            v = v[:, 0:nel]
            if len(free) == 2:
                v = v.rearrange("p (a b) -> p a b", a=free[0])
            elif len(free) == 3:
                v = v.rearrange("p (a b c) -> p a b c", a=free[0], b=free[1])
            return Tl(v)

        def alias_sync(src, dst):
            evs = []
            for t in src:
                if t.r.last_write is not None:
                    evs.append(t.r.last_write)
                evs.extend(t.r.reads)
            for t in dst:
                t.r.reads = list(t.r.reads) + evs
        ho4 = bview(0, 128, (4, 256), F32); sq = bview(1024, 128, (1024,), F32); hn4 = bview(2048, 128, (4, 256), BF16)
        t1 = bview(2560, 128, (512,), F32)
        Cin = [bview(i * 516, 128, (1, 2, 257), F32) for i in range(4)]
        Cgq2 = [bview(2064, 128, (1, 2, 257), BF16), bview(3560, 128, (1, 2, 257), BF16)]
        qx2 = [bview(2328, 128, (2, 1, 64), BF16), bview(3824, 128, (2, 1, 64), BF16)]
        kx2 = [bview(2392, 64, (1, 256), BF16), bview(3888, 64, (1, 256), BF16)]
        gnold = bview(4016, 128, (2, 16), F32)
        ntok = bview(2520, 16, (256,), F32); nout = bview(2776, 32, (128,), F32); t1s = bview(2904, 128, (128,), F32)
        ho_s = bview(3032, 128, (256,), F32); hn_s = bview(3288, 128, (256,), BF16); ncol = bview(3416, 128, (2, 16), F32)
        junk3 = bview(3448, 128, (208,), BF16)
        bufrow = bview(0, 48, (1024,), F32)
        prompt_views = [ho4, sq, hn4, t1]
        sample_views = Cin + Cgq2 + qx2 + kx2 + [gnold, ntok, nout, t1s, ho_s, hn_s, ncol, junk3, bufrow]
        wm = Tl(wbuf.ap.rearrange("p (k s c) -> p k s c", k=8, s=3))
        wm.r = wbuf.r
        for O in MO:
            POOL(lambda e, O=O: e.memset(O["vext4"].ap, 1.0), w=[O["vext4"]])
        DMA(lambda e: e.dma_start(out=bufrow.ap, in_=scv), w=[bufrow])
        for half in range(2):
            MM([lambda e, k=k: e.transpose(out=bS.ap[:, (k % 4) * 48:(k % 4) * 48 + 48], in_=bufrow.ap[:, k * 128:(k + 1) * 128], identity=ident32.ap[0:48, 0:48])
                for k in range(half * 4, half * 4 + 4)], r=[bufrow, ident32], w=[bS])
            DVE(lambda e, half=half: e.tensor_copy(out=bufT.ap[:, half * 4:half * 4 + 4, :], in_=bS.ap[:, 0:192].rearrange("p (a b) -> p a b", a=4)), r=[bS], w=[bufT])

        def m_stage1(hm, gi, tok0, T, O):
            kc0 = 2 * hm
            sample = (T == 64)
            rows = 64 if sample else 128
            nch = 1 if sample else 4
            smz, sxc, qT, kT, kp4, vext4, so4 = O["smz"], O["sxc"], O["qT"], O["kT"], O["kp4"], O["vext4"], O["so4"]
            if gi == 0:
                DMAC(lambda e: e.dma_start(out=wbuf.ap, in_=w_md[hm]), w=[wm])
                POOL(lambda e: e.memset(muX.ap[:, :, 0:3], 0.0), w=[muX]); yield
            for kc, bank in ((0, bA), (1, bB)):
                MM([lambda e, k=k, kc=kc, bank=bank: e.matmul(bank.ap[:, 0:T], lhsT=wm.ap[:, k, 0, kc * 128:(kc + 1) * 128], rhs=hT.ap[:, k, tok0:tok0 + T],
                                                              start=(k == 0), stop=(k == 7)) for k in range(8)], r=[wm, hT], w=[bank]); yield
            for kc, bank in ((0, bC), (1, bD)):
                MM([lambda e, k=k, kc=kc, bank=bank: e.matmul(bank.ap[:, 0:T], lhsT=wm.ap[:, k, 1, kc * 128:(kc + 1) * 128], rhs=hT.ap[:, k, tok0:tok0 + T],
                                                              start=(k == 0), stop=(k == 7)) for k in range(8)], r=[wm, hT], w=[bank]); yield
            for kc, bank in ((0, bA), (1, bB)):
                ACT(lambda e, kc=kc, bank=bank: e.activation(out=muX.ap[:, kc, 3:3 + T], in_=bank.ap[:, 0:T], func=AF.Identity, bias=P.ap[:, kc0 + kc, 4:5]),
                    r=[bank, P], w=[muX]); yield
                if sample:
                    ACT(lambda e, kc=kc, bank=bank: e.activation(out=extS.ap[:, kc, :, 3:7], in_=bank.ap[:, 0:64].rearrange("p (b t) -> p b t", t=4),
                                                                 func=AF.Identity, bias=P.ap[:, kc0 + kc, 4:5]), r=[bank, P], w=[extS])
                    DVE(lambda e, kc=kc: e.tensor_copy(out=extS.ap[:, kc, :, 0:3], in_=bufT.ap[:, kc0 + kc, :].rearrange("p (b i) -> p b i", i=3)),
                        r=[bufT, extS], w=[extS]); yield
                    for i in range(4):
                        DVE(lambda e, kc=kc, i=i: e.tensor_copy(out=shS.ap[:, kc, i, :].rearrange("p (b t) -> p b t", t=4), in_=extS.ap[:, kc, :, i:i + 4]),
                            r=[extS], w=[shS])
                    yield
                if sample or tok0 == 1536:
                    c_lo = 0 if sample else 384
                    ACT(lambda e, kc=kc, bank=bank, c_lo=c_lo: e.activation(out=mu32.ap[:, kc, 0:rows], in_=bank.ap[:, c_lo:c_lo + rows], func=AF.Identity,
                                                                          bias=P.ap[:, kc0 + kc, 4:5]), r=[bank, P], w=[mu32]); yield
            for kc, bank in ((0, bC), (1, bD)):
                yield from silu_gen(bank, Pn.ap[:, kc0 + kc, 5:6], P.ap[:, kc0 + kc, 5:6], T, m_ez, m_sg, smz.ap[:, kc, 0:T], smz)
            for kc, bank in ((0, bA), (1, bB)):
                if not sample:
                    MM([lambda e, i=i, kc=kc, bank=bank: e.matmul(bank.ap[:, 0:T], lhsT=convd.ap[:, kc0 + kc, i, :], rhs=muX.ap[:, kc, i:i + T],
                                                                  start=(i == 0), stop=(i == 3)) for i in range(4)], r=[convd, muX], w=[bank]); yield
                else:
                    MM([lambda e, i=i, kc=kc, bank=bank: e.matmul(bank.ap[:, 0:64], lhsT=convd.ap[:, kc0 + kc, i, :],
                                                                  rhs=shS.ap[:, kc, i, :], start=(i == 0), stop=(i == 3)) for i in range(4)],
                       r=[convd, shS], w=[bank]); yield
                yield from silu_gen(bank, Pn.ap[:, kc0 + kc, 15:16], P.ap[:, kc0 + kc, 15:16], T, m_ez, m_sg, xcT.ap[:, kc, 0:T], xcT)
                DVE(lambda e, kc=kc: e.tensor_scalar(out=sxc.ap[:, kc, 0:T], in0=xcT.ap[:, kc, 0:T], scalar1=P.ap[:, kc0 + kc, 17:18], scalar2=None, op0=ALU.mult),
                    r=[xcT, P], w=[sxc]); yield
            for kc, bank in ((0, bC), (1, bD)):
                MM([lambda e, kc=kc, bank=bank: e.matmul(bank.ap[:, 0:T], lhsT=wbd.ap[:, 0, kc0 + kc, :], rhs=xcT.ap[:, kc, 0:T], start=True, stop=True)],
                   r=[wbd, xcT], w=[bank]); yield
                DVE(lambda e, kc=kc, bank=bank: e.tensor_copy(out=qT.ap[:, kc, 0:T], in_=bank.ap[:, 0:T]), r=[bank], w=[qT]); yield
            for kc, bank in ((0, bA), (1, bB)):
                MM([lambda e, kc=kc, bank=bank: e.matmul(bank.ap[:, 0:T], lhsT=wbd.ap[:, 1, kc0 + kc, :], rhs=xcT.ap[:, kc, 0:T], start=True, stop=True)],
                   r=[wbd, xcT], w=[bank]); yield
                DVE(lambda e, kc=kc, bank=bank: e.tensor_copy(out=kT.ap[:, kc, 0:T], in_=bank.ap[:, 0:T]), r=[bank], w=[kT]); yield
            npair = (nch + 1) // 2
            for pr in range(npair):
                bank = bC if pr == 0 else bD
                nin = min(2, nch - pr * 2)
                fns = []
                for ii in range(nin):
                    i = pr * 2 + ii
                    for k in range(8):
                        fns.append(lambda e, i=i, ii=ii, k=k, bank=bank: e.matmul(bank.ap[0:rows, ii * 256:(ii + 1) * 256],
                                                                                 lhsT=hT.ap[:, k, tok0 + i * 128:tok0 + i * 128 + rows], rhs=wm.ap[:, k, 2, :],
                                                                                 start=(k == 0), stop=(k == 7)))
                MM(fns, r=[hT, wm], w=[bank]); yield
                DVE(lambda e, pr=pr, nin=nin, bank=bank: e.tensor_tensor(
                    out=scr.ap[0:rows, pr * 512:pr * 512 + nin * 256].rearrange("p (a b) -> p a b", a=nin),
                    in0=bank.ap[0:rows, 0:nin * 256].rearrange("p (a b) -> p a b", a=nin),
                    in1=bc_mo.ap[0:rows, hm * 256:(hm + 1) * 256].unsqueeze(1).to_broadcast([rows, nin, 256]), op=ALU.add), r=[bank, bc_mo], w=[scr]); yield
            W = nch * 256
            ACT(lambda e: e.activation(out=scr.ap[0:rows, 0:W], in_=scr.ap[0:rows, 0:W], func=AF.Exp, scale=-1.0), r=[scr], w=[scr]); yield
            ACT(lambda e: e.activation(out=scr.ap[0:rows, 0:W], in_=scr.ap[0:rows, 0:W], func=AF.Ln, bias=1.0), r=[scr], w=[scr]); yield
            ACT(lambda e: e.activation(out=so4.ap[0:rows, 0:nch, :].rearrange("p a b -> p (a b)"), in_=scr.ap[0:rows, 0:W], func=AF.Exp, scale=-1.0),
                r=[scr], w=[so4]); yield
            for i in range(nch):
                bank = (bA, bB, bC, bD)[i]
                ci = 16 if sample else gi * 4 + i
                c0 = i * 128
                fns = []
                for kc in range(2):
                    fns.append(lambda e, kc=kc, c0=c0, bank=bank: e.matmul(bank.ap[0:rows, kc * 128:(kc + 1) * 128], lhsT=xcT.ap[:, kc, c0:c0 + rows],
                                                                          rhs=wbd.ap[:, 1, kc0 + kc, :], start=True, stop=True))
                for kc in range(2):
                    fns.append(lambda e, kc=kc, c0=c0, bank=bank: e.matmul(bank.ap[0:rows, 256 + kc * 128:256 + (kc + 1) * 128],
                                                                          lhsT=muX.ap[:, kc, 3 + c0:3 + c0 + rows], rhs=wbd.ap[:, 2, kc0 + kc, :], start=True, stop=True))
                MM(fns, r=[xcT, muX, wbd], w=[bank]); yield
                ACT(lambda e, i=i, ci=ci, bank=bank: e.activation(out=kp4.ap[0:rows, i, :], in_=bank.ap[0:rows, 0:256], func=AF.Copy,
                                                                  scale=ectok.ap[0:rows, ci, hm:hm + 1]), r=[bank, ectok], w=[kp4]); yield
                ACT(lambda e, i=i, bank=bank: e.activation(out=vext4.ap[0:rows, i, 0:256], in_=bank.ap[0:rows, 256:512], func=AF.Copy), r=[bank], w=[vext4]); yield
            if sample or tok0 == 1536:
                MM([lambda e, kc=kc: e.transpose(out=bA.ap[0:rows, kc * 128:(kc + 1) * 128], in_=mu32.ap[:, kc, 0:rows], identity=ident32.ap) for kc in range(2)],
                   r=[mu32, ident32], w=[bA]); yield
                DVE(lambda e: e.tensor_copy(out=mutok.ap[0:rows, :], in_=bA.ap[0:rows, 0:256]), r=[bA], w=[mutok]); yield
                if sample:
                    for b in range(16):
                        DMA(lambda e, b=b: e.dma_start(out=ocs[b, :, hm * 256:(hm + 1) * 256], in_=mutok.ap[b * 4 + 1:b * 4 + 4, :]), r=[mutok], out=True)
                else:
                    DMA(lambda e: e.dma_start(out=ocp[:, hm * 256:(hm + 1) * 256], in_=mutok.ap[125:128, :]), r=[mutok], out=True)
                yield
            if not sample:
                DVE(lambda e: e.tensor_copy(out=muX.ap[:, :, 0:3], in_=muX.ap[:, :, 512:515]), r=[muX], w=[muX]); yield

        def m_epilogue_s(hm, O):
            rows, tok0, ci = 64, 2048, 16
            kc0 = 2 * hm
            so = O["so4"]; sxc = O["sxc"]; smz = O["smz"]
            DVE(lambda e: e.tensor_copy(out=sm8.ap[0:rows, 3:4], in_=bO.ap[0:rows, 256:257]), r=[bO], w=[sm8]); yield
            DVE(lambda e: e.scalar_tensor_tensor(out=sm8.ap[0:rows, 4:5], in0=sm8.ap[0:rows, 3:4], scalar=-1.0, in1=sm8.ap[0:rows, 3:4],
                                                 op0=ALU.mult, op1=ALU.max), r=[sm8], w=[sm8]); yield
            DVE(lambda e: e.tensor_tensor(out=sm8.ap[0:rows, 5:6], in0=sm8.ap[0:rows, 4:5], in1=thrtok.ap[0:rows, ci, hm:hm + 1], op=ALU.max),
                r=[sm8, thrtok], w=[sm8]); yield
            DVE(lambda e: e.reciprocal(out=sm8.ap[0:rows, 6:7], in_=sm8.ap[0:rows, 5:6]), r=[sm8], w=[sm8]); yield
            DVE(lambda e: e.scalar_tensor_tensor(out=ho_s.ap[0:rows, :], in0=bO.ap[0:rows, 0:256], scalar=sm8.ap[0:rows, 6:7], in1=so.ap[0:rows, 0, :],
                                                 op0=ALU.mult, op1=ALU.mult), r=[bO, sm8, so], w=[ho_s]); yield
            ACT(lambda e: e.activation(out=junk3.ap[0:rows, 0:256] if False else hn_s.ap[0:rows, :], in_=ho_s.ap[0:rows, :], func=AF.Square, scale=1.0 / 16.0,
                                       accum_out=sm8.ap[0:rows, 7:8]), r=[ho_s], w=[hn_s, sm8]); yield
            ACT(lambda e: e.activation(out=sm8.ap[0:rows, 8:9], in_=sm8.ap[0:rows, 7:8], func=AF.Ln, bias=EPS), r=[sm8], w=[sm8]); yield
            ACT(lambda e: e.activation(out=sm8.ap[0:rows, 9:10], in_=sm8.ap[0:rows, 8:9], func=AF.Exp, scale=-0.5), r=[sm8], w=[sm8]); yield
            ACT(lambda e: e.activation(out=hn_s.ap[0:rows, :], in_=ho_s.ap[0:rows, :], func=AF.Copy, scale=sm8.ap[0:rows, 9:10]), r=[ho_s, sm8], w=[hn_s]); yield
            MM([lambda e, kc=kc: e.transpose(out=bT.ap[:, kc * 128:kc * 128 + rows], in_=hn_s.ap[0:rows, kc * 128:(kc + 1) * 128], identity=identb.ap[0:rows, 0:rows])
                for kc in range(2)], r=[hn_s, identb], w=[bT]); yield
            for kc in range(2):
                DVE(lambda e, kc=kc: e.scalar_tensor_tensor(out=t1s.ap[:, 0:rows], in0=bT.ap[:, kc * 128:kc * 128 + rows], scalar=P.ap[:, kc0 + kc, 16:17],
                                                            in1=sxc.ap[:, kc, 0:rows], op0=ALU.mult, op1=ALU.add), r=[bT, P, sxc], w=[t1s]); yield
                DVE(lambda e, kc=kc: e.tensor_tensor(out=mixT.ap[:, 8 + kc0 + kc, tok0:tok0 + rows], in0=t1s.ap[:, 0:rows], in1=smz.ap[:, kc, 0:rows],
                                                     op=ALU.mult), r=[t1s, smz], w=[mixT]); yield

        def m_stage2(hm, gi, tok0, T, O):
            kc0 = 2 * hm
            sample = (T == 64)
            smz, sxc, qT, kT, kp4, vext4, so4 = O["smz"], O["sxc"], O["qT"], O["kT"], O["kp4"], O["vext4"], O["so4"]
            if gi == 0:
                S.alias(sample_views, prompt_views)
                POOL(lambda e: e.memset(Cst.ap, 0.0), w=[Cst]); yield
            if sample:
                S.alias(prompt_views, sample_views)
            if not sample:
                ci0 = gi * 4
                fns = []
                for i in range(4):
                    for kc in range(2):
                        fns.append(lambda e, i=i, kc=kc: e.matmul(bS.ap[:, i * 128:(i + 1) * 128], lhsT=kT.ap[:, kc, i * 128:(i + 1) * 128],
                                                                  rhs=qT.ap[:, kc, i * 128:(i + 1) * 128], start=(kc == 0), stop=(kc == 1)))
                MM(fns, r=[kT, qT], w=[bS]); yield
                for i in range(4):
                    DVE(lambda e, i=i: e.scalar_tensor_tensor(out=STm4.ap[:, i, :], in0=bS.ap[:, i * 128:(i + 1) * 128], scalar=ectok.ap[:, ci0 + i, hm:hm + 1],
                                                              in1=cmask.ap, op0=ALU.mult, op1=ALU.mult), r=[bS, ectok, cmask], w=[STm4]); yield
                for i in range(4):
                    ci = ci0 + i
                    c0 = i * 128
                    bN = bO if i % 2 == 0 else bS
                    ACT(lambda e, ci=ci: e.activation(out=Cgb.ap, in_=Cst.ap, func=AF.Copy, scale=gbc.ap[:, ci, hm:hm + 1]), r=[Cst, gbc], w=[Cgb]); yield
                    MM([lambda e, i=i, bN=bN: e.matmul(bN.ap[:, 0:257], lhsT=STm4.ap[:, i, :], rhs=vext4.ap[:, i, 0:257], start=True, stop=False)] +
                       [lambda e, kc=kc, c0=c0, bN=bN: e.matmul(bN.ap[:, 0:257], lhsT=qT.ap[:, kc, c0:c0 + 128], rhs=Cgb.ap[:, kc, :], start=False, stop=(kc == 1))
                        for kc in range(2)], r=[STm4, vext4, qT, Cgb], w=[bN]); yield
                    MM([lambda e, i=i, kc=kc: e.matmul(bU.ap[:, kc * 256:(kc + 1) * 256], lhsT=kp4.ap[:, i, kc * 128:(kc + 1) * 128], rhs=vext4.ap[:, i, 0:256],
                                                       start=True, stop=True) for kc in range(2)] +
                       [lambda e, i=i, kc=kc, bN=bN: e.matmul(bN.ap[:, 300 + kc:301 + kc], lhsT=kp4.ap[:, i, kc * 128:(kc + 1) * 128], rhs=vext4.ap[:, i, 256:257],
                                                              start=True, stop=True) for kc in range(2)], r=[kp4, vext4], w=[bU, bN]); yield
                    DVE(lambda e, i=i, bN=bN: e.tensor_tensor(out=ho4.ap[:, i, :], in0=bN.ap[:, 0:256], in1=so4.ap[:, i, :], op=ALU.mult), r=[bN, so4], w=[ho4]); yield
                    DVE(lambda e, i=i, bN=bN: e.tensor_copy(out=sm8.ap[:, 12 + i:13 + i], in_=bN.ap[:, 256:257]), r=[bN], w=[sm8]); yield
                    DVE(lambda e, ci=ci: e.scalar_tensor_tensor(out=Cst.ap[:, :, 0:256], in0=Cst.ap[:, :, 0:256], scalar=gbc.ap[:, ci, hm:hm + 1],
                                                                in1=bU.ap.rearrange("p (a b) -> p a b", a=2), op0=ALU.mult, op1=ALU.add), r=[Cst, gbc, bU], w=[Cst]); yield
                    DVE(lambda e, ci=ci, bN=bN: e.scalar_tensor_tensor(out=Cst.ap[:, :, 256], in0=Cst.ap[:, :, 256], scalar=gbc.ap[:, ci, hm:hm + 1],
                                                                       in1=bN.ap[:, 300:302], op0=ALU.mult, op1=ALU.add), r=[Cst, gbc, bN], w=[Cst]); yield
                DVE(lambda e: e.scalar_tensor_tensor(out=sm8.ap[:, 0:4], in0=sm8.ap[:, 12:16], scalar=-1.0, in1=sm8.ap[:, 12:16], op0=ALU.mult, op1=ALU.max),
                    r=[sm8], w=[sm8]); yield
                DVE(lambda e: e.tensor_tensor(out=sm8.ap[:, 4:8], in0=sm8.ap[:, 0:4], in1=thrtok.ap[:, ci0:ci0 + 4, hm], op=ALU.max), r=[sm8, thrtok], w=[sm8]); yield
                DVE(lambda e: e.reciprocal(out=sm8.ap[:, 8:12], in_=sm8.ap[:, 4:8]), r=[sm8], w=[sm8]); yield
                ACT(lambda e: e.activation(out=sq.ap, in_=ho4.ap.rearrange("p a b -> p (a b)"), func=AF.Square, scale=1.0 / 16.0), r=[ho4], w=[sq]); yield
                DVE(lambda e: e.reduce_sum(out=sm8.ap[:, 0:4], in_=sq.ap.rearrange("p (a b) -> p a b", a=4), axis=AX.X), r=[sq, sm8], w=[sm8]); yield
                DVE(lambda e: e.tensor_tensor(out=sm8.ap[:, 4:8], in0=sm8.ap[:, 8:12], in1=sm8.ap[:, 8:12], op=ALU.mult), r=[sm8], w=[sm8]); yield
                DVE(lambda e: e.tensor_tensor(out=sm8.ap[:, 4:8], in0=sm8.ap[:, 4:8], in1=sm8.ap[:, 0:4], op=ALU.mult), r=[sm8], w=[sm8]); yield
                ACT(lambda e: e.activation(out=sm8.ap[:, 4:8], in_=sm8.ap[:, 4:8], func=AF.Ln, bias=EPS), r=[sm8], w=[sm8]); yield
                ACT(lambda e: e.activation(out=sm8.ap[:, 4:8], in_=sm8.ap[:, 4:8], func=AF.Exp, scale=-0.5), r=[sm8], w=[sm8]); yield
                DVE(lambda e: e.tensor_tensor(out=sm8.ap[:, 0:4], in0=sm8.ap[:, 4:8], in1=sm8.ap[:, 8:12], op=ALU.mult), r=[sm8], w=[sm8]); yield
                DVE(lambda e: e.tensor_tensor(out=hn4.ap, in0=ho4.ap, in1=sm8.ap[:, 0:4].unsqueeze(2).to_broadcast([128, 4, 256]), op=ALU.mult),
                    r=[ho4, sm8], w=[hn4]); yield
                MM([lambda e, i=i, kc=kc: e.transpose(out=bT.ap[:, kc * 512 + i * 128:kc * 512 + (i + 1) * 128], in_=hn4.ap[:, i, kc * 128:(kc + 1) * 128],
                                                      identity=identb.ap) for i in range(4) for kc in range(2)], r=[hn4, identb], w=[bT]); yield
                for kc in range(2):
                    DVE(lambda e, kc=kc: e.scalar_tensor_tensor(out=t1.ap, in0=bT.ap[:, kc * 512:(kc + 1) * 512], scalar=P.ap[:, kc0 + kc, 16:17],
                                                                in1=sxc.ap[:, kc, :], op0=ALU.mult, op1=ALU.add), r=[bT, P, sxc], w=[t1]); yield
                    DVE(lambda e, kc=kc: e.tensor_tensor(out=mixT.ap[:, 8 + kc0 + kc, tok0:tok0 + 512], in0=t1.ap, in1=smz.ap[:, kc, :], op=ALU.mult),
                        r=[t1, smz], w=[mixT]); yield
                if gi == 3:
                    DMA(lambda e: e.dma_start(out=oCp[hm].rearrange("(kc p) v -> p kc v", p=128), in_=Cst.ap[:, :, 0:256]), r=[Cst], out=True)
                    DMA(lambda e: e.dma_start(out=onp[hm].rearrange("(kc p) -> p kc", p=128), in_=Cst.ap[:, :, 256], allow_slow_non_contiguous=True),
                        r=[Cst], out=True); yield
            else:
                rows = 64
                MM([lambda e, kc=kc: e.matmul(bS.ap[0:64, 0:64], lhsT=kT.ap[:, kc, 0:64], rhs=qT.ap[:, kc, 0:64], start=(kc == 0), stop=(kc == 1))
                    for kc in range(2)], r=[kT, qT], w=[bS]); yield
                DVE(lambda e: e.scalar_tensor_tensor(out=STm4.ap[0:64, 0, 0:64], in0=bS.ap[0:64, 0:64], scalar=ectok.ap[0:64, 16, hm:hm + 1],
                                                     in1=smask.ap, op0=ALU.mult, op1=ALU.mult), r=[bS, ectok, smask], w=[STm4]); yield
                DMA(lambda e: e.dma_start(out=ntok.ap, in_=sn[:, hm, :]), w=[ntok]); yield
                MM([lambda e, kc=kc: e.transpose(out=bS.ap[:, 256 + kc * 16:256 + (kc + 1) * 16], in_=ntok.ap[:, kc * 128:(kc + 1) * 128], identity=ident32.ap[0:16, 0:16])
                    for kc in range(2)], r=[ntok, ident32, STm4], w=[bS]); yield
                first = True
                NS = 4
                nold = bS.ap[:, 256:288].rearrange("p (kc b) -> p kc b", kc=2)
                DVE(lambda e: e.tensor_tensor(out=gnold.ap, in0=nold, in1=gbc.ap[:, 16:32, hm].unsqueeze(1).to_broadcast([128, 2, 16]), op=ALU.mult),
                    r=[bS, gbc], w=[gnold]); yield
                MM([lambda e, kc=kc: e.matmul(bS.ap[:, 320 + kc * 16:320 + (kc + 1) * 16], lhsT=kp4.ap[0:64, 0, kc * 128:(kc + 1) * 128], rhs=rowmask.ap,
                                              start=True, stop=True) for kc in range(2)], r=[kp4, rowmask, gnold], w=[bS]); yield
                DVE(lambda e: e.tensor_tensor(out=ncol.ap, in0=bS.ap[:, 320:352].rearrange("p (kc b) -> p kc b", kc=2), in1=gnold.ap, op=ALU.add),
                    r=[bS, gnold], w=[ncol]); yield

                def load_round(b):
                    C_l = Cin[b % NS]
                    DMA(lambda e, b=b, C_l=C_l: e.dma_start(out=C_l.ap[:, 0, :, 0:256], in_=sC[b, hm].rearrange("(kc p) v -> p kc v", p=128)), w=[C_l])
                for b in range(NS - 1):
                    load_round(b)
                yield
                for b in range(16):
                    C_t = Cin[b % NS]
                    Cg_t = Cgq2[b % 2]; qx_t = qx2[b % 2]; kx_t = kx2[b % 2]
                    Ub = bU if b % 2 == 0 else bTf
                    if b + NS - 1 < 16:
                        load_round(b + NS - 1)
                    ACT(lambda e, b=b, C_t=C_t, Cg_t=Cg_t: e.activation(out=Cg_t.ap[:, 0, :, 0:256], in_=C_t.ap[:, 0, :, 0:256], func=AF.Copy,
                                                                        scale=gbc.ap[:, 16 + b, hm:hm + 1]), r=[C_t, gbc], w=[Cg_t]); yield
                    DVE(lambda e, b=b, Cg_t=Cg_t: e.tensor_copy(out=Cg_t.ap[:, 0, :, 256], in_=gnold.ap[:, :, b]), r=[gnold, Cg_t], w=[Cg_t]); yield
                    DVE(lambda e, b=b, qx_t=qx_t: e.tensor_tensor(out=qx_t.ap[:, :, 0, :], in0=qT.ap[:, :, 0:64],
                                                                  in1=colmask.ap[:, b:b + 1, :].to_broadcast([128, 2, 64]), op=ALU.mult), r=[qT, colmask], w=[qx_t]); yield
                    DVE(lambda e, b=b, kx_t=kx_t: e.tensor_tensor(out=kx_t.ap[:, 0, :], in0=kp4.ap[0:64, 0, :],
                                                                  in1=rowmask.ap[:, b:b + 1].to_broadcast([64, 256]), op=ALU.mult), r=[kp4, rowmask], w=[kx_t]); yield
                    fns = []
                    if first:
                        fns.append(lambda e: e.matmul(bO.ap[0:64, 0:257], lhsT=STm4.ap[0:64, 0, 0:64], rhs=vext4.ap[0:64, 0, 0:257], start=True, stop=False))
                        first = False
                    for kc in range(2):
                        last = (b == 15 and kc == 1)
                        fns.append(lambda e, kc=kc, last=last, qx_t=qx_t, Cg_t=Cg_t: e.matmul(bO.ap[0:64, 0:257], lhsT=qx_t.ap[:, kc, 0, :], rhs=Cg_t.ap[:, 0, kc, :],
                                                                                             start=False, stop=last))
                    MM(fns, r=[STm4, vext4, qx_t, Cg_t], w=[bO]); yield
                    MM([lambda e, kc=kc, kx_t=kx_t, Ub=Ub: e.matmul(Ub.ap[:, kc * 256:(kc + 1) * 256], lhsT=kx_t.ap[:, 0, kc * 128:(kc + 1) * 128],
                                                                    rhs=vext4.ap[0:64, 0, 0:256], start=True, stop=True) for kc in range(2)], r=[kx_t, vext4], w=[Ub]); yield
                    DVE(lambda e, b=b, C_t=C_t, Ub=Ub: e.scalar_tensor_tensor(out=C_t.ap[:, 0, :, 0:256], in0=C_t.ap[:, 0, :, 0:256], scalar=gbc.ap[:, 16 + b, hm:hm + 1],
                                                                             in1=Ub.ap.rearrange("p (a b) -> p a b", a=2), op0=ALU.mult, op1=ALU.add),
                        r=[C_t, gbc, Ub], w=[C_t]); yield
                    DMA(lambda e, b=b, C_t=C_t: e.dma_start(out=oCs[b, hm].rearrange("(kc p) v -> p kc v", p=128), in_=C_t.ap[:, 0, :, 0:256]), r=[C_t], out=True); yield
                MM([lambda e: e.transpose(out=bU.ap[0:32, 0:128], in_=ncol.ap.rearrange("p kc b -> p (kc b)"), identity=ident32.ap)], r=[ncol, ident32], w=[bU]); yield
                DVE(lambda e: e.tensor_copy(out=nout.ap[0:32, :], in_=bU.ap[0:32, 0:128]), r=[bU], w=[nout]); yield
                for kc in range(2):
                    DMA(lambda e, kc=kc: e.dma_start(out=ons[:, hm, kc * 128:(kc + 1) * 128], in_=nout.ap[kc * 16:(kc + 1) * 16, :]), r=[nout], out=True)
                yield
                yield from m_epilogue_s(hm, O)

        m_units = [(hm, gi, tok0, T) for hm in range(4) for gi, (tok0, T) in enumerate(GROUPS)]
        S.begin_window()
        for n, u in enumerate(m_units):
            interleave(m_stage1(*u, MO[n % 2]))
            interleave(m_stage2(*u, MO[n % 2]))
        S.end_window()

        S.barrier()
        AR.reset()
        wo = Tl(hT.ap.rearrange("p a b -> p (a b)")[:, 0:16384].rearrange("p (k c) -> p k c", k=16))
        wo.r = hT.r
        for k4 in range(4):
            DMAC(lambda e, k4=k4: e.dma_start(out=wo.ap[:, k4 * 4:(k4 + 1) * 4, :], in_=w_out_v[:, k4 * 4:(k4 + 1) * 4, :]), w=[wo])
        gate_tok = AR.alloc(17, (1024,), F32)
        selg = AR.alloc(17, (192,), F32)
        gbp = AR.alloc(128, (1024,), F32); gbs = AR.alloc(64, (1024,), F32); fgb = AR.alloc(128, (1024,), F32)
        prm2 = AR.alloc(32, (1024,), F32); sel2 = AR.alloc(32, (3, 128), F32)
        x4 = [AR.alloc(128, (1024,), F32) for _ in range(2)]
        res4 = [AR.alloc(128, (1024,), F32) for _ in range(2)]
        y4 = [AR.alloc(128, (1024,), F32) for _ in range(2)]
        junk4 = AR.alloc(128, (1024,), BF16)
        st4 = AR.alloc(128, (8,), F32)
        DMA(lambda e: e.dma_start(out=selg.ap, in_=selg_d), w=[selg])
        DMA(lambda e: e.dma_start(out=prm2.ap, in_=prm), w=[prm2])
        DMA(lambda e: e.dma_start(out=sel2.ap, in_=sel_d), w=[sel2])
        for half, bank in ((0, bA), (1, bB)):
            MM([lambda e, k=k, bank=bank: e.transpose(out=bank.ap[0:17, (k % 4) * 128:(k % 4 + 1) * 128], in_=modT.ap[:, 16 + k, :], identity=ident32.ap)
                for k in range(half * 4, half * 4 + 4)], r=[modT, ident32], w=[bank])
            DVE(lambda e, half=half, bank=bank: e.tensor_copy(out=gate_tok.ap[:, half * 512:(half + 1) * 512], in_=bank.ap[0:17, :]), r=[bank], w=[gate_tok])
        for half, bank in ((0, bA), (1, bB)):
            MM([lambda e, half=half, bank=bank: e.matmul(bank.ap, lhsT=selg.ap[:, 0:128], rhs=gate_tok.ap[:, half * 512:(half + 1) * 512], start=True, stop=True)],
               r=[selg, gate_tok], w=[bank])
            DVE(lambda e, half=half, bank=bank: e.tensor_copy(out=gbp.ap[:, half * 512:(half + 1) * 512], in_=bank.ap), r=[bank], w=[gbp])
        for half, bank in ((0, bA), (1, bB)):
            MM([lambda e, half=half, bank=bank: e.matmul(bank.ap[0:64, :], lhsT=selg.ap[:, 128:192], rhs=gate_tok.ap[:, half * 512:(half + 1) * 512], start=True, stop=True)],
               r=[selg, gate_tok], w=[bank])
            DVE(lambda e, half=half, bank=bank: e.tensor_copy(out=gbs.ap[:, half * 512:(half + 1) * 512], in_=bank.ap[0:64, :]), r=[bank], w=[gbs])
        for half, bank in ((0, bA), (1, bB)):
            MM([lambda e, half=half, bank=bank: e.matmul(bank.ap, lhsT=sel2.ap[:, 2, :], rhs=prm2.ap[:, half * 512:(half + 1) * 512], start=True, stop=True)],
               r=[sel2, prm2], w=[bank])
            DVE(lambda e, half=half, bank=bank: e.tensor_copy(out=fgb.ap[:, half * 512:(half + 1) * 512], in_=bank.ap), r=[bank], w=[fgb])
        def tile4(i):
            rows = 128 if i < 16 else 64
            x_t = x4[i % 2]; r_t = res4[i % 2]; y_t = y4[i % 2]
            gb = gbp if i < 16 else gbs
            src = xp[i * 128:(i + 1) * 128, :] if i < 16 else xs
            dst = yp[i * 128:(i + 1) * 128, :] if i < 16 else ys
            tk = i * 128
            DMA(lambda e, x_t=x_t, src=src, rows=rows: e.dma_start(out=x_t.ap[0:rows, :], in_=src), w=[x_t])
            banks = (bC, bD) if i % 2 == 0 else (bA, bB)
            for half, bank in enumerate(banks):
                MM([lambda e, k=k, half=half, bank=bank: e.matmul(bank.ap[0:rows, :], lhsT=mixT.ap[:, k, tk:tk + rows], rhs=wo.ap[:, k, half * 512:(half + 1) * 512],
                                                                  start=(k == 0), stop=(k == 15)) for k in range(16)], r=[mixT, wo], w=[bank])
                DVE(lambda e, half=half, bank=bank, r_t=r_t, gb=gb: e.tensor_tensor(out=r_t.ap[0:rows, half * 512:(half + 1) * 512], in0=bank.ap[0:rows, :],
                                                                                   in1=gb.ap[0:rows, half * 512:(half + 1) * 512], op=ALU.mult), r=[bank, gb], w=[r_t])
            DVE(lambda e, r_t=r_t, x_t=x_t: e.tensor_tensor(out=r_t.ap[0:rows, :], in0=r_t.ap[0:rows, :], in1=x_t.ap[0:rows, :], op=ALU.add), r=[r_t, x_t], w=[r_t])
            ACT(lambda e, r_t=r_t: e.activation(out=junk4.ap[0:rows, :], in_=r_t.ap[0:rows, :], func=AF.Square, scale=1.0 / 32.0, accum_out=st4.ap[0:rows, 0:1]),
                r=[r_t], w=[junk4, st4])
            ACT(lambda e: e.activation(out=st4.ap[0:rows, 1:2], in_=st4.ap[0:rows, 0:1], func=AF.Ln, bias=EPS), r=[st4], w=[st4])
            ACT(lambda e: e.activation(out=st4.ap[0:rows, 2:3], in_=st4.ap[0:rows, 1:2], func=AF.Exp, scale=-0.5), r=[st4], w=[st4])
            DVE(lambda e, r_t=r_t, y_t=y_t: e.scalar_tensor_tensor(out=y_t.ap[0:rows, :], in0=r_t.ap[0:rows, :], scalar=st4.ap[0:rows, 2:3], in1=fgb.ap[0:rows, :],
                                                                  op0=ALU.mult, op1=ALU.mult), r=[r_t, st4, fgb], w=[y_t])
            DMA(lambda e, y_t=y_t, dst=dst, rows=rows: e.dma_start(out=dst, in_=y_t.ap[0:rows, :]), r=[y_t], out=True)

        for i0 in range(0, 17, 6):
            S.begin_window()
            for i in range(i0, min(17, i0 + 6)):
                tile4(i)
            S.end_window()

        S.finish()
        with nc.Block() as block:
            S.emit(block)
    return nc


_NC_CACHE = {}


def _consts():
    c = {}
    j = np.arange(128)
    c["cmask"] = (j[:, None] <= j[None, :]).astype(np.float32)
    j = np.arange(64)
    c["smask"] = ((j[:, None] <= j[None, :]) & (j[:, None] // 4 == j[None, :] // 4)).astype(np.float32)
    c["rowmask"] = (j[:, None] // 4 == np.arange(16)[None, :]).astype(np.float32)
    c["colmask"] = (np.arange(16)[:, None] == j[None, :] // 4).astype(np.float32).reshape(1, 1024)
    sel = np.zeros((32, 3, 128), np.float32)
    sel[2, 0, :] = 1.0
    sel[6, 1, :] = 1.0
    sel[18, 2, :] = 1.0
    c["sel"] = sel
    selg = np.zeros((17, 192), np.float32)
    selg[0, 0:128] = 1.0
    for t in range(64):
        selg[1 + t // 4, 128 + t] = 1.0
    c["selg"] = selg
    return c


def kernel(x_prompt, x_sample, c_prompt, c_sample, state_hgrn, state_mlstm_C, state_mlstm_n,
           state_mlstm_m, state_mlstm_conv, w_ada, b_ada, norm_g, w_in, b_in, hgrn_lb_logits,
           hgrn_norm_g, mlstm_conv_w, mlstm_conv_b, mlstm_wq, mlstm_wk, mlstm_wv, mlstm_norm_g,
           mlstm_skip, w_out, final_g):
    f = lambda a: np.ascontiguousarray(np.asarray(a, dtype=np.float32))
    x_prompt, x_sample, c_prompt, c_sample = f(x_prompt), f(x_sample), f(c_prompt), f(c_sample)
    state_hgrn, state_mlstm_C, state_mlstm_n = f(state_hgrn), f(state_mlstm_C), f(state_mlstm_n)
    state_mlstm_m, state_mlstm_conv = f(state_mlstm_m), f(state_mlstm_conv)
    w_ada, b_ada, norm_g, w_in, b_in = f(w_ada), f(b_ada), f(norm_g), f(w_in), f(b_in)
    prm = np.zeros((32, 1024), np.float32)
    prm[0:7] = b_in[0, 0:7168].reshape(7, 1024)
    prm[7] = norm_g[0]
    prm[8:10] = f(hgrn_lb_logits)
    prm[10] = f(hgrn_norm_g)[0]
    prm[11:15] = f(mlstm_conv_w)[0]
    prm[15] = f(mlstm_conv_b)[0]
    prm[16] = f(mlstm_norm_g)[0]
    prm[17] = f(mlstm_skip)[0]
    prm[18] = f(final_g)
    bg = np.ascontiguousarray(b_in[0, 7168:7176].reshape(2, 4).T)
    wbd = np.zeros((128, 3, 8, 128), np.float32)
    for wi, wsrc in enumerate((mlstm_wq, mlstm_wk, mlstm_wv)):
        wv = f(wsrc)[0].reshape(8, 32, 4, 4)
        for g in range(32):
            wbd[g * 4:(g + 1) * 4, wi, :, g * 4:(g + 1) * 4] = wv[:, g].transpose(1, 0, 2)
    consts = _consts()
    wv = w_in[0].reshape(8, 128, 7176)
    w_h = np.empty((8, 128, 8, 4, 128), np.float32)
    for s_i, off in enumerate((0, 1024, 2048, 3072)):
        w_h[:, :, :, s_i, :] = wv[:, :, off:off + 1024].reshape(8, 128, 8, 128).transpose(2, 1, 0, 3)
    w_h = w_h.reshape(8, 128, 4096)
    w_m = np.empty((4, 128, 8, 3, 256), np.float32)
    for s_i in range(3):
        off = 4096 + s_i * 1024
        w_m[:, :, :, s_i, :] = wv[:, :, off:off + 1024].reshape(8, 128, 4, 256).transpose(2, 1, 0, 3)
    w_m = w_m.reshape(4, 128, 6144)
    w_g = np.ascontiguousarray(wv[:, :, 7168:7176].transpose(1, 0, 2)).reshape(128, 64)
    if "nc" not in _NC_CACHE:
        _NC_CACHE["nc"] = build_nc()
    nc = _NC_CACHE["nc"]
    in_maps = []
    for c in range(NCORES):
        sl = slice(16 * c, 16 * c + 16)
        m = dict(
            xp=x_prompt[c], xs=x_sample[sl].reshape(64, 1024),
            cc=np.concatenate([c_prompt[c:c + 1], c_sample[sl]], axis=0),
            sh=state_hgrn[0, sl], sC=state_mlstm_C[0, sl], sn=state_mlstm_n[0, sl], sm=state_mlstm_m[0, sl],
            scv=state_mlstm_conv[0, sl].reshape(48, 1024),
            w_ada=w_ada[0], b_ada=b_ada[0].reshape(24, 128), w_out=f(w_out)[0],
            prm=prm, bg=bg, wbd=wbd, w_h=w_h, w_m=w_m, w_g=w_g, **consts)
        in_maps.append({k: np.ascontiguousarray(v) for k, v in m.items()})
    res = run_bass_kernel_spmd(nc, in_maps, core_ids=list(range(NCORES)))
    R = res.results
    cat = lambda k: np.stack([r[k] for r in R], axis=0)
    y_prompt = cat("yp")
    y_sample = np.concatenate([r["ys"].reshape(16, 4, 1024) for r in R], axis=0)
    hgrn_p = cat("ohp")[None]
    C_p = cat("oCp")[None]
    n_p = cat("onp")[None]
    m_p = cat("omp").reshape(8, 4)[None]
    conv_p = cat("ocp")[None]
    hgrn_s = np.concatenate([r["ohs"] for r in R], axis=0)[None]
    C_s = np.concatenate([r["oCs"] for r in R], axis=0)[None]
    n_s = np.concatenate([r["ons"] for r in R], axis=0)[None]
    m_s = np.concatenate([r["oms"] for r in R], axis=0)[None]
    conv_s = np.concatenate([r["ocs"] for r in R], axis=0)[None]
    outs = (y_prompt, y_sample, hgrn_p, C_p, n_p, m_p, conv_p, hgrn_s, C_s, n_s, m_s, conv_s)
    return tuple(np.ascontiguousarray(o, dtype=np.float32) for o in outs)
```
